# Optimizing a Trainium2 kernel written in Bass

```python
import jax, jax.numpy as jnp
from jax import lax
import numpy as np

D_MODEL = 1024
BATCH = 8
SEQ = 2048
DEPTH = 1

MEM_LEN = 256
EPS = 1e-6
ATTN_HEADS = 8
ATTN_KV_HEADS = 2
ATTN_HEAD_DIM = 64
ATTN_GROUP = ATTN_HEADS // ATTN_KV_HEADS
WINDOW = 128
ATTN_BLOCK = 128
ROPE_THETA = 10000.0
DN_HEADS = 4
DN_HEAD_K = 128
DN_HEAD_V = 128
DN_CONV = 5
DN_CHUNK = 64
N_DIRS = 2
CROSS_HEADS = 4
CROSS_HEAD_DIM = D_MODEL // CROSS_HEADS
D_FF = -(-8 * D_MODEL // (3 * 256)) * 256
ATTN_WIDTH = ATTN_HEADS * ATTN_HEAD_DIM
DN_WIDTH = DN_HEADS * DN_HEAD_V
MIX_WIDTH = ATTN_WIDTH + DN_WIDTH
DN_QKV = DN_HEADS * (2 * DN_HEAD_K + DN_HEAD_V)
IN_SIZES = [ATTN_WIDTH, ATTN_KV_HEADS * ATTN_HEAD_DIM, ATTN_KV_HEADS * ATTN_HEAD_DIM,
            DN_QKV, DN_WIDTH, N_DIRS * DN_HEADS, N_DIRS * DN_HEADS]
IN_COLS = int(np.sum(IN_SIZES))
IN_SPLITS = [int(v) for v in np.cumsum(IN_SIZES)[:-1]]

kernel_name = 'hybrid_swa_gdn_memory_block'


def _rms_norm(x, gain):
    xf = x.astype(jnp.float32)
    y = xf * lax.rsqrt(jnp.mean(xf * xf, axis=-1, keepdims=True) + EPS)
    return (y * gain.astype(jnp.float32)).astype(x.dtype)


def _l2norm(x):
    return x * lax.rsqrt(jnp.sum(x * x, axis=-1, keepdims=True) + EPS)


def _rope(x, positions):
    half = x.shape[-1] // 2
    inv_freq = ROPE_THETA ** (-jnp.arange(half, dtype=jnp.float32) / half)
    ang = positions.astype(jnp.float32)[..., None] * inv_freq
    cos = jnp.cos(ang)[:, :, None, :]
    sin = jnp.sin(ang)[:, :, None, :]
    xf = x.astype(jnp.float32)
    x1, x2 = xf[..., :half], xf[..., half:]
    return jnp.concatenate([x1 * cos - x2 * sin, x2 * cos + x1 * sin], axis=-1)


def _window_attention(q, k, v, sink):
    b, s = q.shape[0], q.shape[1]
    nb = s // ATTN_BLOCK
    qb = q.reshape(b, nb, ATTN_BLOCK, ATTN_KV_HEADS, ATTN_GROUP, ATTN_HEAD_DIM)

    def band(t):
        tp = t.reshape(b, nb, ATTN_BLOCK, ATTN_KV_HEADS, ATTN_HEAD_DIM)
        tp = jnp.pad(tp, ((0, 0), (1, 1), (0, 0), (0, 0), (0, 0)))
        return jnp.concatenate([tp[:, :-2], tp[:, 1:-1], tp[:, 2:]], axis=2)

    kw, vw = band(k), band(v)
    scores = jnp.einsum('bnqhgd,bnkhd->bnhgqk', qb, kw) * (ATTN_HEAD_DIM ** -0.5)
    qi = jnp.arange(ATTN_BLOCK)[:, None]
    ki = jnp.arange(3 * ATTN_BLOCK)[None, :]
    in_band = jnp.abs(ki - ATTN_BLOCK - qi) <= WINDOW
    kpos = (jnp.arange(nb)[:, None] - 1) * ATTN_BLOCK + jnp.arange(3 * ATTN_BLOCK)[None, :]
    in_seq = (kpos >= 0) & (kpos < s)
    mask = in_band[None, :, :] & in_seq[:, None, :]
    scores = jnp.where(mask[None, :, None, None], scores, -jnp.inf)
    sk = sink.astype(jnp.float32).reshape(ATTN_KV_HEADS, ATTN_GROUP)[None, None, :, :, None, None]
    m = jnp.maximum(jnp.max(scores, axis=-1, keepdims=True), sk)
    p = jnp.exp(scores - m)
    denom = jnp.sum(p, axis=-1, keepdims=True) + jnp.exp(sk - m)
    o = jnp.einsum('bnhgqk,bnkhd->bnqhgd', p / denom, vw)
    return o.reshape(b, s, ATTN_WIDTH)


def _gated_delta_chunked(q, k, v, g, beta):
    b, h, s, dk = q.shape
    dv = v.shape[-1]
    c = DN_CHUNK
    n = s // c
    q = q.reshape(b, h, n, c, dk)
    k = k.reshape(b, h, n, c, dk)
    v = v.reshape(b, h, n, c, dv)
    g = jnp.cumsum(g.reshape(b, h, n, c), axis=-1)
    beta = beta.reshape(b, h, n, c)
    incl = jnp.tril(jnp.ones((c, c), dtype=bool))
    strict = jnp.tril(jnp.ones((c, c), dtype=bool), k=-1)
    decay = jnp.exp(jnp.where(incl, g[..., :, None] - g[..., None, :], -jnp.inf))
    kb = k * beta[..., None]
    lower = jnp.where(strict, jnp.einsum('bhnid,bhnjd->bhnij', kb, k) * decay, 0.0)
    tmat = lower + jnp.eye(c, dtype=jnp.float32)
    rhs = jnp.concatenate([v * beta[..., None], kb * jnp.exp(g)[..., None]], axis=-1)
    sol = lax.linalg.triangular_solve(tmat, rhs, left_side=True, lower=True, unit_diagonal=True)
    u, w = sol[..., :dv], sol[..., dv:]
    qk = jnp.einsum('bhnid,bhnjd->bhnij', q, k) * decay
    qg = q * jnp.exp(g)[..., None]
    kg = k * jnp.exp(g[..., -1:] - g)[..., None]
    g_last = jnp.exp(g[..., -1])

    def step(state, inp):
        qg_i, kg_i, u_i, w_i, qk_i, gl_i = inp
        v_new = u_i - jnp.einsum('bhck,bhkv->bhcv', w_i, state)
        o_i = jnp.einsum('bhck,bhkv->bhcv', qg_i, state) + jnp.einsum('bhij,bhjv->bhiv', qk_i, v_new)
        state = state * gl_i[..., None, None] + jnp.einsum('bhck,bhcv->bhkv', kg_i, v_new)
        return state, o_i

    xs = tuple(jnp.moveaxis(t, 2, 0) for t in (qg, kg, u, w, qk, g_last))
    state0 = jnp.zeros((b, h, dk, dv), jnp.float32)
    _, o = lax.scan(step, state0, xs)
    return jnp.moveaxis(o, 0, 2).reshape(b, h, s, dv)


def _gated_deltanet(qkv, gate, a, bt, conv_w, a_log, dt_bias, g_out):
    b, s = qkv.shape[0], qkv.shape[1]
    ch = qkv.shape[-1]
    pad = DN_CONV // 2
    conv = lax.conv_general_dilated(qkv, conv_w.reshape(DN_CONV, 1, ch).astype(qkv.dtype),
                                    window_strides=(1,), padding=[(pad, pad)],
                                    dimension_numbers=('NWC', 'WIO', 'NWC'), feature_group_count=ch)
    conv = jax.nn.silu(conv.astype(jnp.float32))
    nq = DN_HEADS * DN_HEAD_K
    q = conv[..., :nq].reshape(b, s, DN_HEADS, DN_HEAD_K)
    k = conv[..., nq:2 * nq].reshape(b, s, DN_HEADS, DN_HEAD_K)
    v = conv[..., 2 * nq:].reshape(b, s, DN_HEADS, DN_HEAD_V)
    q = (_l2norm(q) * (DN_HEAD_K ** -0.5)).transpose(0, 2, 1, 3)
    k = _l2norm(k).transpose(0, 2, 1, 3)
    v = v.transpose(0, 2, 1, 3)
    a = a.astype(jnp.float32).reshape(b, s, N_DIRS, DN_HEADS)
    bt = bt.astype(jnp.float32).reshape(b, s, N_DIRS, DN_HEADS)
    g = -jnp.exp(a_log.astype(jnp.float32)) * jax.nn.softplus(a + dt_bias.astype(jnp.float32))
    g = g.transpose(2, 0, 3, 1)
    beta = jax.nn.sigmoid(bt).transpose(2, 0, 3, 1)
    o_fwd = _gated_delta_chunked(q, k, v, g[0], beta[0])
    flip = lambda t: jnp.flip(t, axis=2)
    o_bwd = flip(_gated_delta_chunked(flip(q), flip(k), flip(v), flip(g[1]), flip(beta[1])))
    o = (o_fwd + o_bwd).transpose(0, 2, 1, 3)
    o = o * lax.rsqrt(jnp.mean(o * o, axis=-1, keepdims=True) + EPS) * g_out.astype(jnp.float32)
    o = o * jax.nn.silu(gate.astype(jnp.float32).reshape(b, s, DN_HEADS, DN_HEAD_V))
    return o.reshape(b, s, DN_WIDTH)


def _memory_cross_attention(h, mem_n, w_q, w_kv, w_o):
    b, s = h.shape[0], h.shape[1]
    m = mem_n.shape[1]
    q = (h @ w_q).reshape(b, s, CROSS_HEADS, CROSS_HEAD_DIM).astype(jnp.float32)
    kv = mem_n @ w_kv
    k = kv[..., :D_MODEL].reshape(b, m, CROSS_HEADS, CROSS_HEAD_DIM).astype(jnp.float32)
    v = kv[..., D_MODEL:].reshape(b, m, CROSS_HEADS, CROSS_HEAD_DIM).astype(jnp.float32)
    p = jax.nn.softmax(jnp.einsum('bqhd,bkhd->bhqk', q, k) * (CROSS_HEAD_DIM ** -0.5), axis=-1)
    o = jnp.einsum('bhqk,bkhd->bqhd', p, v).reshape(b, s, D_MODEL).astype(h.dtype)
    return o @ w_o


def _swiglu(h, w_gate_up, w_down):
    gu = h @ w_gate_up
    return (jax.nn.silu(gu[..., :D_FF]) * gu[..., D_FF:]) @ w_down


def setup_inputs(seed: int = 0) -> dict:
    key = jax.random.key(seed)
    ks = jax.random.split(key, 24)
    f32 = jnp.float32

    def dense(k, fan_in, fan_out):
        return jax.random.normal(k, (DEPTH, fan_in, fan_out), f32) * fan_in ** -0.5

    def gain(k, n):
        return 1.0 + 0.02 * jax.random.normal(k, (DEPTH, n), f32)

    x = jax.random.normal(ks[0], (BATCH, SEQ, D_MODEL), f32)
    mem = jax.random.normal(ks[1], (BATCH, MEM_LEN, D_MODEL), f32)
    offsets = jax.random.randint(ks[2], (BATCH, 1), 0, 4096, dtype=jnp.int32)
    positions = jnp.arange(SEQ, dtype=jnp.int32)[None, :] + offsets
    dt = jnp.exp(jax.random.uniform(ks[3], (DEPTH, N_DIRS, DN_HEADS), f32)
                 * (np.log(0.1) - np.log(0.001)) + np.log(0.001))
    return {
        'x': x,
        'mem': mem,
        'positions': positions,
        'g_mix_pre': gain(ks[4], D_MODEL),
        'w_in': dense(ks[5], D_MODEL, IN_COLS),
        'conv_w': jax.random.normal(ks[6], (DEPTH, DN_CONV, DN_QKV), f32) * DN_CONV ** -0.5,
        'a_log': jnp.log(jax.random.uniform(ks[7], (DEPTH, N_DIRS, DN_HEADS), f32, 1.0, 16.0)),
        'dt_bias': dt + jnp.log(-jnp.expm1(-dt)),
        'g_dn_out': gain(ks[8], DN_HEAD_V),
        'attn_sink': 0.5 * jax.random.normal(ks[9], (DEPTH, ATTN_HEADS), f32),
        'w_out': dense(ks[10], MIX_WIDTH, D_MODEL),
        'g_mix_post': gain(ks[11], D_MODEL),
        'g_cross_pre': gain(ks[12], D_MODEL),
        'g_mem': gain(ks[13], D_MODEL),
        'w_cq': dense(ks[14], D_MODEL, D_MODEL),
        'w_ckv': dense(ks[15], D_MODEL, 2 * D_MODEL),
        'w_co': dense(ks[16], D_MODEL, D_MODEL),
        'g_cross_post': gain(ks[17], D_MODEL),
        'g_ffn_pre': gain(ks[18], D_MODEL),
        'w_gate_up': dense(ks[19], D_MODEL, 2 * D_FF),
        'w_down': dense(ks[20], D_FF, D_MODEL),
        'g_ffn_post': gain(ks[21], D_MODEL),
    }


def reference(x, mem, positions, g_mix_pre, w_in, conv_w, a_log, dt_bias, g_dn_out, attn_sink,
              w_out, g_mix_post, g_cross_pre, g_mem, w_cq, w_ckv, w_co, g_cross_post,
              g_ffn_pre, w_gate_up, w_down, g_ffn_post):
    b, s = x.shape[0], x.shape[1]
    for l in range(DEPTH):
        h = _rms_norm(x, g_mix_pre[l])
        proj = h @ w_in[l]
        aq, ak, av, dqkv, dgate, dalpha, dbeta = jnp.split(proj, IN_SPLITS, axis=-1)
        aq = _rope(aq.reshape(b, s, ATTN_HEADS, ATTN_HEAD_DIM), positions)
        ak = _rope(ak.reshape(b, s, ATTN_KV_HEADS, ATTN_HEAD_DIM), positions)
        av = av.reshape(b, s, ATTN_KV_HEADS, ATTN_HEAD_DIM).astype(jnp.float32)
        attn_o = _window_attention(aq, ak, av, attn_sink[l]).astype(x.dtype)
        dn_o = _gated_deltanet(dqkv, dgate, dalpha, dbeta, conv_w[l], a_log[l], dt_bias[l],
                               g_dn_out[l]).astype(x.dtype)
        mix = jnp.concatenate([attn_o, dn_o], axis=-1) @ w_out[l]
        x = x + _rms_norm(mix, g_mix_post[l])
        c = _memory_cross_attention(_rms_norm(x, g_cross_pre[l]), _rms_norm(mem, g_mem[l]),
                                    w_cq[l], w_ckv[l], w_co[l])
        x = x + _rms_norm(c, g_cross_post[l])
        f = _swiglu(_rms_norm(x, g_ffn_pre[l]), w_gate_up[l], w_down[l])
        x = x + _rms_norm(f, g_ffn_post[l])
    return x
```

```python
import numpy as np
import ml_dtypes
import concourse.bass as bass
import concourse.mybir as mybir
from concourse.bass_utils import run_bass_kernel_spmd
from contextlib import ExitStack

F32 = mybir.dt.float32
BF16 = mybir.dt.bfloat16
I32 = mybir.dt.int32
AF = mybir.ActivationFunctionType
ALU = mybir.AluOpType
AX = mybir.AxisListType

S_LEN = 2048
NT = 16
D = 1024
EPS = 1e-6
DFF = 2816
NFF = 22
PI = float(np.pi)
TWO_PI = float(2 * np.pi)
DEBUG = {}


class Ins:
    __slots__ = ("eng", "fn", "deps", "needed", "sem", "val", "is_dma")


class _Rec:
    def __init__(self):
        self.call = None

    def __getattr__(self, name):
        def f(*a, **k):
            self.call = (name, a, k)
            return self
        return f


class Sched:
    ENG = ("pe", "act", "dve", "pool", "sp")

    def __init__(self, nc, es):
        self.nc = nc
        self.es = es
        self.streams = {e: [] for e in self.ENG}
        self.last_w = {}
        self.readers = {}
        self.esem = {e: es.enter_context(nc.semaphore("s_" + e)) for e in self.ENG}
        self.dsem = {}
        self.dcount = {}
        self.last = {}

    def op(self, eng, fn, reads=(), writes=(), dma=None):
        ins = Ins()
        ins.eng = eng
        rec = _Rec()
        fn(rec)
        assert rec.call is not None
        ins.fn = rec.call
        ins.needed = False
        ins.is_dma = dma is not None
        px = [k for k in reads if k.startswith("pb")]
        if px:
            reads = [k for k in reads if not k.startswith("pb")]
            writes = list(writes) + [k for k in px if k not in writes]
        deps = []
        for k in reads:
            w = self.last_w.get(k)
            if w is not None:
                deps.append(w)
        strict = eng == "pool"
        for k in writes:
            w = self.last_w.get(k)
            if w is not None and (w.eng != eng or w.is_dma or ins.is_dma or strict):
                deps.append(w)
            for e, r in self.readers.get(k, {}).items():
                if e != eng or r.is_dma or ins.is_dma or strict:
                    deps.append(r)
        for d in deps:
            d.needed = True
        ins.deps = deps
        if ins.is_dma:
            if dma not in self.dsem:
                self.dsem[dma] = self.es.enter_context(self.nc.semaphore("d_" + dma))
                self.dcount[dma] = 0
            self.dcount[dma] += 16
            ins.sem = self.dsem[dma]
            ins.val = self.dcount[dma]
        else:
            ins.sem = self.esem[eng]
            ins.val = None
            self.last[eng] = ins
        for k in writes:
            self.last_w[k] = ins
            self.readers[k] = {}
        for k in reads:
            rk = self.readers.setdefault(k, {})
            rk[eng if not ins.is_dma else (eng, dma)] = ins
        self.streams[eng].append(ins)
        return ins

    def barrier(self):
        lasts = [i for i in self.last.values()]
        dm = []
        for name in self.dsem:
            d = Ins()
            d.eng = "sp"
            d.is_dma = True
            d.sem = self.dsem[name]
            d.val = self.dcount[name]
            d.needed = True
            dm.append(d)
        for i in lasts:
            i.needed = True
        for e in self.ENG:
            ins = Ins()
            ins.eng = e
            ins.fn = None
            ins.needed = False
            ins.is_dma = False
            ins.deps = [i for i in lasts if i.eng != e] + dm
            ins.sem = self.esem[e]
            ins.val = None
            self.streams[e].append(ins)
        self.last_w = {}
        self.readers = {}

    def finalize(self, final_waits=()):
        for e in self.ENG:
            c = 0
            for ins in self.streams[e]:
                if not ins.is_dma and ins.needed and ins.fn is not None:
                    c += 1
                    ins.val = c
                elif not ins.is_dma and ins.fn is None:
                    ins.val = c
        streams = self.streams
        stats = {}

        def emit(engobj, ename):
            seen = {}
            nw = 0
            for ins in streams[ename]:
                need = {}
                for d in ins.deps:
                    if d.val is None or d.val == 0:
                        continue
                    sid = id(d.sem)
                    if sid not in need or need[sid][1] < d.val:
                        need[sid] = (d.sem, d.val)
                for sid, (sem, val) in need.items():
                    if seen.get(sid, 0) < val:
                        engobj.wait_ge(sem, val)
                        seen[sid] = val
                        nw += 1
                if ins.fn is None:
                    continue
                nm_, a_, k_ = ins.fn
                bi = getattr(engobj, nm_)(*a_, **k_)
                if ins.is_dma:
                    bi.then_inc(ins.sem, 16)
                elif ins.needed:
                    bi.then_inc(ins.sem, 1)
            if ename == "sp":
                for name in final_waits:
                    engobj.wait_ge(self.dsem[name], self.dcount[name])
            stats[ename] = (len(streams[ename]), nw)

        with self.nc.Block() as block:
            @block.tensor
            def _(e):
                emit(e, "pe")

            @block.scalar
            def _(e):
                emit(e, "act")

            @block.vector
            def _(e):
                emit(e, "dve")

            @block.gpsimd
            def _(e):
                emit(e, "pool")

            @block.sync
            def _(e):
                emit(e, "sp")
        return stats


class Arena:
    def __init__(self, ap, nwords):
        self.ap = ap
        self.free = [(0, nwords * 4)]
        self.live = {}

    def alloc(self, name, shape, dt, top=False):
        esz = 2 if dt == BF16 else 4
        n = 1
        for s in shape[1:]:
            n *= s
        nbytes = ((n * esz + 63) // 64) * 64
        order = range(len(self.free) - 1, -1, -1) if top else range(len(self.free))
        for idx in order:
            off, sz = self.free[idx]
            if sz >= nbytes:
                if top:
                    self.free[idx] = (off, sz - nbytes)
                    off = off + sz - nbytes
                else:
                    self.free[idx] = (off + nbytes, sz - nbytes)
                if sz == nbytes:
                    del self.free[idx]
                break
        else:
            raise RuntimeError("arena OOM for %s (%d bytes); free=%s live=%s" % (name, nbytes, self.free, sorted((v[0], v[1], k) for k, v in self.live.items())))
        self.live[name] = (off, nbytes)
        v = self.ap[:, off // 4:(off + nbytes) // 4]
        if dt != F32:
            v = v.bitcast(dt)
        v = v[:, 0:n]
        if len(shape) == 3:
            v = v.rearrange("p (a b) -> p a b", a=shape[1])
        elif len(shape) == 4:
            v = v.rearrange("p (a b c) -> p a b c", a=shape[1], b=shape[2])
        elif len(shape) == 5:
            v = v.rearrange("p (a b c d) -> p a b c d", a=shape[1], b=shape[2], c=shape[3])
        if shape[0] < 128:
            v = v[0:shape[0]]
        return v

    def view(self, name, shape, dt):
        off, nb = self.live[name]
        esz = 2 if dt == BF16 else 4
        n = 1
        for s in shape[1:]:
            n *= s
        assert n * esz <= nb
        v = self.ap[:, off // 4:(off + nb) // 4]
        if dt != F32:
            v = v.bitcast(dt)
        v = v[:, 0:n]
        if len(shape) == 3:
            v = v.rearrange("p (a b) -> p a b", a=shape[1])
        return v

    def release(self, *names):
        for name in names:
            off, nb = self.live.pop(name)
            self.free.append((off, nb))
        self.free.sort()
        m = []
        for off, sz in self.free:
            if m and m[-1][0] + m[-1][1] == off:
                m[-1] = (m[-1][0], m[-1][1] + sz)
            else:
                m.append((off, sz))
        self.free = m


def bc(ap, shape):
    return ap.to_broadcast(list(shape))


def build_program():
    nc = bass.Bass("TRN2", target_bir_lowering=False)

    def din(name, shape, dt=F32):
        return nc.dram_tensor(name, list(shape), dt, kind="ExternalInput").ap()

    x_d = din("x", [S_LEN, D])
    mem_d = din("mem", [256, D])
    pos_d = din("pos", [1, S_LEN], I32)
    w_inr = din("w_inr", [D, 3472])
    cw_d = din("cw", [128, 12, 5])
    alog_d = din("alog", [1, 8])
    dtb_d = din("dtb", [1, 8])
    gdn_d = din("gdn", [1, 128])
    sink_d = din("sink", [1, 8])
    w_out = din("w_out", [D, D])
    w_cq = din("w_cq", [D, D])
    w_ckv = din("w_ckv", [D, 2 * D])
    w_co = din("w_co", [D, D])
    w_gu = din("w_gu", [NFF, 128, 2048])
    w_dn = din("w_dn", [DFF, D])
    gains = {k: din(k, [1, D]) for k in ("g_mix_pre", "g_mix_post", "g_cross_pre", "g_mem", "g_cross_post", "g_ffn_pre", "g_ffn_post")}
    cst_bf = din("cst_bf", [128, 4, 128], BF16)
    cst_f = din("cst_f", [128, 10, 128])
    cst_c = din("cst_c", [128, 2])
    out_d = nc.dram_tensor("out", [S_LEN, D], F32, kind="ExternalOutput").ap()
    dbg = {}
    for k, shp in DEBUG.items():
        dbg[k] = nc.dram_tensor("dbg_" + k, list(shp), F32, kind="ExternalOutput").ap()

    es = ExitStack()
    with es:
        S = Sched(nc, es)
        NW = 52900
        arena_t = es.enter_context(nc.sbuf_tensor("arena", [128, NW], F32))
        A = Arena(arena_t[:], NW)
        pbig = [es.enter_context(nc.psum_tensor("pb%d" % i, [128, 1024], F32)) for i in range(4)]
        bank_ctr = [0]

        def bank():
            i = bank_ctr[0] % 8
            bank_ctr[0] += 1
            return pbig[i // 2][:, (i % 2) * 512:(i % 2) * 512 + 512], "pb%d" % i

        def bank_at(i):
            return pbig[i // 2][:, (i % 2) * 512:(i % 2) * 512 + 512], "pb%d" % i

        rot4 = [0]

        def bank_hi():
            i = 4 + rot4[0] % 4
            rot4[0] += 1
            return bank_at(i)

        def bank2():
            if bank_ctr[0] % 2:
                bank_ctr[0] += 1
            i = bank_ctr[0] % 8
            bank_ctr[0] += 2
            return pbig[i // 2][:], ["pb%d" % i, "pb%d" % (i + 1)]

        def pipeline(gens, depth):
            gens = list(gens)
            active = []
            while gens or active:
                if gens and len(active) < depth:
                    active.append(gens.pop(0))
                for g_ in list(active):
                    try:
                        next(g_)
                    except StopIteration:
                        active.remove(g_)

        def PE(fn, r, w):
            S.op("pe", fn, reads=r, writes=w)

        def ACT(fn, r, w):
            S.op("act", fn, reads=r, writes=w)

        def DVE(fn, r, w):
            S.op("dve", fn, reads=r, writes=w)

        def POOL(fn, r, w):
            S.op("pool", fn, reads=r, writes=w)

        def DMA(q, out, in_, r, w, grp):
            S.op(q, lambda e, o=out, i=in_: e.dma_start(out=o, in_=i), reads=r, writes=w, dma=grp)

        def mm(out, lhsT, rhs, start, stop, r, w, tp=None):
            if tp is None:
                PE(lambda e, o=out, l=lhsT, rr=rhs, s=start, t=stop: e.matmul(out=o, lhsT=l, rhs=rr, start=s, stop=t), r, w)
            else:
                PE(lambda e, o=out, l=lhsT, rr=rhs, s=start, t=stop, tp=tp: e.matmul(out=o, lhsT=l, rhs=rr, start=s, stop=t, tile_position=tp), r, w)

        def tr(out, in_, ident, r, w):
            PE(lambda e, o=out, i=in_, d=ident: e.transpose(out=o, in_=i, identity=d), r, w)

        def dump(name, ap, keys, rows=128):
            if name in dbg:
                tmp = A.alloc("dbgtmp_" + name, list(ap.shape), F32)
                DVE(lambda e, o=tmp, i=ap: e.tensor_copy(out=o, in_=i), keys, ["dbgtmp_" + name])
                DMA("sp", dbg[name], tmp, ["dbgtmp_" + name], [], "dbg_" + name)
                S.barrier()
                A.release("dbgtmp_" + name)

        cbf = A.alloc("cbf", [128, 4, 128], BF16, top=True)
        cf = A.alloc("cf", [128, 10, 128], F32, top=True)
        cc = A.alloc("cc", [128, 2], F32, top=True)
        DMA("sp", cbf, cst_bf, [], ["cbf"], "c0")
        DMA("sp", cf, cst_f, [], ["cf"], "c1")
        DMA("sp", cc, cst_c, [], ["cc"], "c2")
        identb, onesb, mprev, mnext = cbf[:, 0, :], cbf[:, 1, :], cbf[:, 2, :], cbf[:, 3, :]
        identf = cf[:, 0, :]
        small = A.alloc("small", [128, 64], F32, top=True)
        gbc = A.alloc("gbc", [128, D], F32, top=True)

        wq_rr = [0]

        def load_w(dst, src, key):
            DMA("pool", dst, src, [], [key], "w_" + key)

        def kview(w, c0, c1):
            return w.rearrange("(c p) n -> p c n", p=128)[:, :, c0:c1]

        stat_ctr = [0]

        def norm_begin(gain_d, tag, gtile=None, gkey="gbc"):
            gtile = gbc if gtile is None else gtile
            DMA("sp", gtile, gain_d.partition_broadcast(128), [], [gkey], gkey)
            junk = A.alloc("junk_" + tag, [128, D], BF16)
            hb = [A.alloc("hb%d_%s" % (i, tag), [128, D], BF16) for i in range(4)]
            return dict(tag=tag, junk=junk, hb=hb, g=gtile, gk=gkey)

        def norm_item(cx, t, xt, xk, loader, hT, hkey):
            tag, junk, hb = cx["tag"], cx["junk"], cx["hb"]
            if loader is not None:
                loader()
            sc = stat_ctr[0] % 32
            stat_ctr[0] += 1
            ssq = small[:, 2 * sc:2 * sc + 1]
            rs = small[:, 2 * sc + 1:2 * sc + 2]
            sk = "st%d" % sc
            ACT(lambda e: e.activation(out=junk, in_=xt, func=AF.Square, accum_out=ssq), xk, ["junk" + tag, sk])
            ACT(lambda e: e.activation(out=rs, in_=ssq, func=AF.Sqrt, scale=1.0 / D, bias=EPS), [sk], [sk + "r"])
            yield
            h = hb[t % 4]
            hk = "hb%d%s" % (t % 4, tag)
            DVE(lambda e: e.reciprocal(out=rs, in_=rs), [sk + "r"], [sk + "r"])
            DVE(lambda e: e.scalar_tensor_tensor(out=h, in0=xt, scalar=rs, in1=cx["g"], op0=ALU.mult, op1=ALU.mult), xk + [sk + "r", cx["gk"]], [hk])
            yield
            pb, pk = bank()
            pbv = pb.bitcast(BF16).rearrange("p (c n) -> p c n", c=8)
            for c in range(8):
                tr(pbv[:, c, :], h[:, c * 128:(c + 1) * 128], identb, [hk, "cbf"], [pk])
            yield
            if t % 2 == 0:
                ACT(lambda e: e.copy(out=hT[:, :, t * 128:(t + 1) * 128], in_=pbv), [pk], ["%s%d" % (hkey, t)])
            else:
                DVE(lambda e: e.tensor_copy(out=hT[:, :, t * 128:(t + 1) * 128], in_=pbv), [pk], ["%s%d" % (hkey, t)])
            yield

        def norm_end(cx):
            tag = cx["tag"]
            A.release("junk_" + tag, "hb0_" + tag, "hb1_" + tag, "hb2_" + tag, "hb3_" + tag)

        def norm_T(tiles, gain_d, hT, hkey, tag):
            cx = norm_begin(gain_d, tag)
            pipeline([norm_item(cx, t, xt, xk, ld, hT, hkey) for t, (xt, xk, ld) in enumerate(tiles)], 4)
            S.barrier()
            norm_end(cx)

        def hkeys(hkey, n4):
            return ["%s%d" % (hkey, 4 * n4 + i) for i in range(4)]

        xbuf = [A.alloc("xb%d" % i, [128, D], F32) for i in range(4)]

        def x_tiles():
            out = []
            for t in range(NT):
                def loader(t=t):
                    DMA("sp", xbuf[t % 4], x_d[t * 128:(t + 1) * 128, :], [], ["xb%d" % (t % 4)], "xb%d" % (t % 4))
                out.append((xbuf[t % 4], ["xb%d" % (t % 4)], loader))
            return out

        def resid_epilogue(pb2, pk2, gkey_loaded, xin, xin_keys, xout, xout_keys, tmp, tmpk):
            sc = stat_ctr[0] % 32
            stat_ctr[0] += 1
            ssq = small[:, 2 * sc:2 * sc + 1]
            rs = small[:, 2 * sc + 1:2 * sc + 2]
            sk = "st%d" % sc
            ACT(lambda e: e.activation(out=tmp, in_=pb2, func=AF.Square, accum_out=ssq), pk2, [tmpk, sk])
            ACT(lambda e: e.activation(out=rs, in_=ssq, func=AF.Sqrt, scale=1.0 / D, bias=EPS), [sk], [sk + "r"])
            yield
            DVE(lambda e: e.reciprocal(out=rs, in_=rs), [sk + "r"], [sk + "r"])
            DVE(lambda e: e.scalar_tensor_tensor(out=tmp, in0=pb2, scalar=rs, in1=gbc, op0=ALU.mult, op1=ALU.mult), pk2 + [sk + "r", "gbc", tmpk], [tmpk])
            yield
            if sc % 2 == 0:
                POOL(lambda e: e.tensor_tensor(out=xout, in0=tmp, in1=xin, op=ALU.add), [tmpk] + xin_keys, xout_keys)
            else:
                DVE(lambda e: e.tensor_tensor(out=xout, in0=tmp, in1=xin, op=ALU.add), [tmpk] + xin_keys, xout_keys)
            yield

        hT = A.alloc("hT", [128, 8, S_LEN], BF16)
        wsl = [A.alloc("wsl%d" % i, [128, 8, 512], BF16) for i in range(2)]
        load_w(wsl[0], kview(w_inr, 0, 512), "wsl0")
        load_w(wsl[1], kview(w_inr, 512, 1024), "wsl1")
        norm_T(x_tiles(), gains["g_mix_pre"], hT, "hT", "a")
        A.release("xb0", "xb1", "xb2", "xb3")
        wctr = [0]

        def wslot():
            i = wctr[0] % 2
            wctr[0] += 1
            return wsl[i], "wsl%d" % i

        cosT = A.alloc("cosT", [128, S_LEN], F32)
        sinT = A.alloc("sinT", [128, S_LEN], F32)
        posi = A.alloc("posi", [128, S_LEN], I32)
        ang = A.alloc("ang", [128, S_LEN], F32)
        rr = A.alloc("rr", [128, S_LEN], F32)
        kf = A.alloc("kf", [128, S_LEN], F32)
        DMA("sp", posi, pos_d.partition_broadcast(128), [], ["posi"], "posi")
        DVE(lambda e: e.tensor_copy(out=kf, in_=posi), ["posi"], ["kf"])
        DVE(lambda e: e.tensor_scalar(out=ang, in0=kf, scalar1=cc[:, 0:1], scalar2=None, op0=ALU.mult), ["kf", "cc"], ["ang"])
        for which, dst in (("sin", sinT), ("cos", cosT)):
            if which == "cos":
                DVE(lambda e: e.tensor_scalar(out=ang, in0=ang, scalar1=PI / 2, scalar2=None, op0=ALU.add), ["ang"], ["ang"])
            DVE(lambda e: e.tensor_scalar(out=posi, in0=ang, scalar1=1.0 / TWO_PI, scalar2=None, op0=ALU.mult), ["ang"], ["posi"])
            DVE(lambda e: e.tensor_copy(out=kf, in_=posi), ["posi"], ["kf"])
            DVE(lambda e: e.scalar_tensor_tensor(out=rr, in0=kf, scalar=-TWO_PI, in1=ang, op0=ALU.mult, op1=ALU.add), ["kf", "ang"], ["rr"])
            DVE(lambda e: e.tensor_scalar(out=kf, in0=rr, scalar1=PI, scalar2=TWO_PI, op0=ALU.is_gt, op1=ALU.mult), ["rr"], ["kf"])
            DVE(lambda e: e.tensor_tensor(out=rr, in0=rr, in1=kf, op=ALU.subtract), ["rr", "kf"], ["rr"])
            DVE(lambda e: e.tensor_scalar(out=rr, in0=rr, scalar1=-PI, scalar2=PI, op0=ALU.max, op1=ALU.min), ["rr"], ["rr"])
            if which == "sin":
                ACT(lambda e, d=dst: e.activation(out=d, in_=rr, func=AF.Sin, scale=cc[:, 1:2]), ["rr", "cc"], ["sinT"])
            else:
                ACT(lambda e, d=dst: e.activation(out=d, in_=rr, func=AF.Sin), ["rr"], ["cosT"])
        S.barrier()
        A.release("posi", "ang", "rr", "kf")

        qT = A.alloc("qT", [128, 4, S_LEN], BF16)
        kT = A.alloc("kT", [128, S_LEN], BF16)
        ropeA = [A.alloc("ropeA%d" % i, [128, 512], F32) for i in range(2)]
        ropeB = [A.alloc("ropeB%d" % i, [128, 512], F32) for i in range(2)]
        rctr = [0]
        wq0, wq0k = wslot()
        wq1, wq1k = wslot()
        def rope_gen(wa, wak, ca, wb, wbk, cb, n4p, dst, dkey):
            n4s = (2 * n4p, 2 * n4p + 1)
            pas = [bank() for _ in n4s]
            pbs = [bank() for _ in n4s]
            for c in range(8):
                for i_, n4 in enumerate(n4s):
                    mm(pas[i_][0], wa[:, c, ca], hT[:, c, n4 * 512:(n4 + 1) * 512], c == 0, c == 7, [wak] + hkeys("hT", n4), [pas[i_][1]])
                for i_, n4 in enumerate(n4s):
                    mm(pbs[i_][0], wb[:, c, cb], hT[:, c, n4 * 512:(n4 + 1) * 512], c == 0, c == 7, [wbk] + hkeys("hT", n4), [pbs[i_][1]])
            yield
            for i_, n4 in enumerate(n4s):
                cs = slice(n4 * 512, (n4 + 1) * 512)
                DVE(lambda e: e.tensor_tensor(out=ropeA[i_], in0=pas[i_][0], in1=cosT[:, cs], op=ALU.mult), [pas[i_][1], "cosT"], ["ropeA%d" % i_])
                DVE(lambda e: e.tensor_tensor(out=ropeB[i_], in0=pbs[i_][0], in1=sinT[:, cs], op=ALU.mult), [pbs[i_][1], "sinT"], ["ropeB%d" % i_])
                POOL(lambda e: e.tensor_tensor(out=dst[:, cs], in0=ropeA[i_], in1=ropeB[i_], op=ALU.add), ["ropeA%d" % i_, "ropeB%d" % i_], [dkey])
            yield

        wk, wkk = wslot()
        gl_ = [rope_gen(wq0, wq0k, slice(j * 128, (j + 1) * 128), wq1, wq1k, slice(j * 128, (j + 1) * 128), n4p, qT[:, j, :], "qT") for j in range(4) for n4p in range(2)]
        pipeline(gl_, 2)
        load_w(wk[:, :, 0:256], kview(w_inr, 1024, 1280), wkk)
        gl_ = [rope_gen(wk, wkk, slice(0, 128), wk, wkk, slice(128, 256), n4p, kT, "kT") for n4p in range(2)]
        pipeline(gl_, 2)
        gate_s = A.alloc("gate_s", [128, NT, 512], BF16, top=True)
        vtokA = A.alloc("vtokA", [128, NT, 128], BF16)
        ab = A.alloc("ab", [128, NT, 16], F32, top=True)
        wt0, wt0k = wslot()
        load_w(wt0, kview(w_inr, 2816, 3328), wt0k)
        wt1, wt1k = wslot()
        load_w(wt1[:, :, 0:144], kview(w_inr, 3328, 3472), wt1k)
        def tokm_gen(t):
            pa, pak = bank()
            pb_, pbk = bank()
            ts_ = slice(t * 128, (t + 1) * 128)
            for c in range(8):
                mm(pa, hT[:, c, ts_], wt0[:, c, :], c == 0, c == 7, [wt0k, "hT%d" % t], [pak])
                mm(pb_[:, 0:144], hT[:, c, ts_], wt1[:, c, 0:144], c == 0, c == 7, [wt1k, "hT%d" % t], [pbk])
            yield
            ACT(lambda e: e.activation(out=gate_s[:, t, :], in_=pa, func=AF.Silu), [pak], ["gate_s"])
            DVE(lambda e: e.tensor_copy(out=vtokA[:, t, :], in_=pb_[:, 0:128]), [pbk], ["vtokA"])
            DVE(lambda e: e.tensor_copy(out=ab[:, t, :], in_=pb_[:, 128:144]), [pbk], ["ab"])
            yield

        pipeline([tokm_gen(t) for t in range(NT)], 2)
        S.barrier()
        A.release("cosT", "sinT", "ropeA0", "ropeA1", "ropeB0", "ropeB1")
        dump("qT", qT[:, 0, :], [])
        dump("kT", kT, [])

        attn_oT = A.alloc("attn_oT", [128, 4, S_LEN], BF16, top=True)
        sk_f = A.alloc("sk_f", [1, 8], F32)
        sinkrow = A.alloc("sinkrow", [1, 2, 512], BF16)
        sinkrow_lo = A.alloc("sinkrow_lo", [1, 2, 512], BF16)
        sk_t = A.alloc("sk_t", [1, 2, 512], F32)
        DMA("sp", sk_f, sink_d, [], ["sk_f"], "sk")
        ACT(lambda e: e.activation(out=sk_f, in_=sk_f, func=AF.Exp), ["sk_f"], ["sk_f"])
        for g in range(2):
            DVE(lambda e, g=g: e.tensor_copy(out=sk_t[:, g, :].rearrange("p (j q) -> p j q", j=4), in_=bc(sk_f[:, 4 * g:4 * g + 4].unsqueeze(2), [1, 4, 128])), ["sk_f"], ["sk_t"])
        DVE(lambda e: e.tensor_copy(out=sinkrow, in_=sk_t), ["sk_t"], ["sinkrow"])
        DVE(lambda e: e.tensor_tensor(out=sk_t, in0=sk_t, in1=sinkrow, op=ALU.subtract), ["sk_t", "sinkrow"], ["sk_t"])
        DVE(lambda e: e.tensor_copy(out=sinkrow_lo, in_=sk_t), ["sk_t"], ["sinkrow_lo"])
        Pt = [A.alloc("Pt%d" % i, [128, 512], BF16) for i in range(8)]
        rec = [A.alloc("rec%d" % i, [128, 512], F32) for i in range(2)]
        pctr = [0]
        def attn_gen(qb):
            pO, pOk = bank_at((qb % 2) * 2)
            pD, pDk = bank_at((qb % 2) * 2 + 1)
            qs_ = slice(qb * 128, (qb + 1) * 128)
            kbs = [kb for kb in (qb - 1, qb, qb + 1) if 0 <= kb < NT]
            items = [(ki, kb, len(kbs)) for ki, kb in enumerate(kbs)]
            prep = {}

            def stage1(it):
                ki, kb, nk = it
                Ps = []
                pss = []
                for g in range(2):
                    gs = slice(g * 64, (g + 1) * 64)
                    pS, pSk = bank_hi()
                    mm(pS.rearrange("p (j q) -> p j q", j=4), kT[gs, kb * 128:(kb + 1) * 128], qT[gs, :, qs_], True, True, ["kT", "qT"], [pSk])
                    pss.append((pS, pSk))
                for g in range(2):
                    pS, pSk = pss[g]
                    pi = pctr[0] % 8
                    pctr[0] += 1
                    P = Pt[pi]
                    Pk = "Pt%d" % pi
                    ACT(lambda e: e.activation(out=P, in_=pS, func=AF.Exp, scale=0.125), [pSk], [Pk])
                    if kb != qb:
                        m = mprev if kb < qb else mnext
                        POOL(lambda e: e.tensor_tensor(out=P.rearrange("p (j q) -> p j q", j=4), in0=P.rearrange("p (j q) -> p j q", j=4), in1=bc(m.unsqueeze(1), [128, 4, 128]), op=ALU.mult), [Pk, "cbf"], [Pk])
                    Ps.append((P, Pk))
                prep[it] = Ps

            def stage2(it):
                ki, kb, nk = it
                Ps = prep[it]
                for g in range(2):
                    gs = slice(g * 64, (g + 1) * 64)
                    mm(pO[gs, :], vtokA[:, kb, gs], Ps[g][0], ki == 0, ki == nk - 1, ["vtokA", Ps[g][1]], [pOk], tp=(0, g * 64))
                for g in range(2):
                    gs = slice(g * 64, (g + 1) * 64)
                    mm(pD[gs, :], onesb[:, 0:64], Ps[g][0], ki == 0, False, ["cbf", Ps[g][1]], [pDk], tp=(0, g * 64))
                if ki == nk - 1:
                    for g in range(2):
                        gs = slice(g * 64, (g + 1) * 64)
                        mm(pD[gs, :], onesb[0:1, 0:64], sinkrow[0:1, g, :], False, False, ["cbf", "sinkrow"], [pDk], tp=(0, g * 64))
                    for g in range(2):
                        gs = slice(g * 64, (g + 1) * 64)
                        mm(pD[gs, :], onesb[0:1, 0:64], sinkrow_lo[0:1, g, :], False, True, ["cbf", "sinkrow_lo"], [pDk], tp=(0, g * 64))

            stage1(items[0])
            for i, it in enumerate(items):
                if i + 1 < len(items):
                    stage1(items[i + 1])
                yield
                stage2(it)
            yield
            ri = qb % 2
            ACT(lambda e: e.activation(out=rec[ri], in_=pD, func=AF.Ln), [pDk], ["rec%d" % ri])
            ACT(lambda e: e.activation(out=rec[ri], in_=rec[ri], func=AF.Exp, scale=-1.0), ["rec%d" % ri], ["rec%d" % ri])
            DVE(lambda e: e.tensor_tensor(out=attn_oT[:, :, qs_], in0=pO.rearrange("p (j q) -> p j q", j=4), in1=rec[ri].rearrange("p (j q) -> p j q", j=4), op=ALU.mult), [pOk, "rec%d" % ri], ["attn_oT"])
            yield

        pipeline([attn_gen(qb) for qb in range(NT)], 2)
        S.barrier()
        A.release("qT", "kT", "vtokA", "Pt0", "Pt1", "Pt2", "Pt3", "Pt4", "Pt5", "Pt6", "Pt7", "rec0", "rec1", "sk_f", "sinkrow", "sinkrow_lo", "sk_t")
        dump("attn_oT", attn_oT[:, 0, :], [])

        cwt = A.alloc("cwt", [128, 12, 5], F32)
        DMA("sp", cwt, cw_d, [], ["cwt"], "cwt")
        kqT = A.alloc("kqT", [128, 4, NT, 2, 128], BF16, top=True)
        ktok = A.alloc("ktok", [128, NT, 4, 128], BF16, top=True)
        vtok = A.alloc("vtok", [128, NT, 4, 128], BF16, top=True)
        xbp = [A.alloc("xbp%d" % i, [128, S_LEN + 4], BF16) for i in range(2)]
        prb = [A.alloc("prb%d" % i, [128, S_LEN], BF16) for i in range(2)]
        sqb = [A.alloc("sqb%d" % i, [128, S_LEN], BF16) for i in range(2)]
        dwb = [A.alloc("dwb%d" % i, [128, 5, 128], BF16) for i in range(2)]
        rtmp = [A.alloc("rtmp%d" % i, [128, 512], F32) for i in range(2)]
        for i in range(2):
            POOL(lambda e: e.memset(xbp[i], 0.0), [], ["xbp%d" % i])
        wdn = {}
        for m0 in (0, 4, 8):
            wd_, wdk = wslot() if m0 < 8 else (None, None)
            if m0 < 8:
                load_w(wd_, kview(w_inr, 1280 + m0 * 128, 1280 + (m0 + 4) * 128), wdk)
                wdn[m0] = (wd_, wdk)

        def conv_gen(m):
            kind, h = m // 4, m % 4
            bi = m % 2
            if m == 8:
                wd_, wdk = wslot()
                load_w(wd_, kview(w_inr, 1280 + 8 * 128, 1280 + 12 * 128), wdk)
                wdn[8] = (wd_, wdk)
            wd_, wdk = wdn[(m // 4) * 4]
            xb_, xbk = xbp[bi], "xbp%d" % bi
            pr, prk = prb[bi], "prb%d" % bi
            sq, sqk = sqb[bi], "sqb%d" % bi
            dw, dwk = dwb[bi], "dwb%d" % bi
            rt, rtk = rtmp[bi], "rtmp%d" % bi
            POOL(lambda e: e.tensor_tensor(out=dw, in0=bc(identb.unsqueeze(1), [128, 5, 128]), in1=bc(cwt[:, m, :].unsqueeze(2), [128, 5, 128]), op=ALU.mult), ["cbf", "cwt"], [dwk])
            for n4 in range(4):
                pa, pak = bank()
                cs = slice(n4 * 512, (n4 + 1) * 512)
                for c in range(8):
                    mm(pa, wd_[:, c, (m % 4) * 128:(m % 4 + 1) * 128], hT[:, c, cs], c == 0, c == 7, [wdk] + hkeys("hT", n4), [pak])
                ACT(lambda e: e.copy(out=xb_[:, 2 + n4 * 512:2 + (n4 + 1) * 512], in_=pa), [pak], [xbk])
                if n4 % 2 == 1:
                    yield
            for n4 in range(4):
                pa, pak = bank()
                cs = slice(n4 * 512, (n4 + 1) * 512)
                for j in range(5):
                    mm(pa, dw[:, j, :], xb_[:, n4 * 512 + j:n4 * 512 + j + 512], j == 0, j == 4, [dwk, xbk], [pak])
                dsto = sq if kind == 2 else pr
                ACT(lambda e: e.activation(out=dsto[:, cs], in_=pa, func=AF.Silu), [pak], [sqk if kind == 2 else prk])
                if n4 % 2 == 1:
                    yield
            if kind < 2:
                POOL(lambda e: e.tensor_tensor(out=sq, in0=pr, in1=pr, op=ALU.mult), [prk], [sqk])
                yield
                kqi = 1 if kind == 0 else 0
                dk_ = ("qnT%d" if kind == 0 else "knT%d") % h
                sc_ = float(128 ** -0.5) if kind == 0 else 1.0
                for n4 in range(4):
                    cs = slice(n4 * 512, (n4 + 1) * 512)
                    pa, pak = bank()
                    mm(pa, onesb, sq[:, cs], True, True, ["cbf", sqk], [pak])
                    ACT(lambda e: e.activation(out=rt, in_=pa, func=AF.Ln, bias=EPS), [pak], [rtk])
                    ACT(lambda e: e.activation(out=rt, in_=rt, func=AF.Exp, scale=-0.5), [rtk], [rtk])
                    DVE(lambda e: e.scalar_tensor_tensor(out=kqT[:, h, 4 * n4:4 * n4 + 4, kqi, :], in0=pr[:, cs].rearrange("p (t n) -> p t n", t=4), scalar=sc_, in1=rt.rearrange("p (t n) -> p t n", t=4), op0=ALU.mult, op1=ALU.mult), [prk, rtk], [dk_])
                    yield
                srcf = (lambda t: kqT[:, h, t, 0, :])
                srck = dk_
            else:
                srcf = (lambda t: sq[:, t * 128:(t + 1) * 128])
                srck = sqk
            if kind >= 1:
                dtok = ktok if kind == 1 else vtok
                dtk = "ktok" if kind == 1 else "vtok"
                for half in range(2):
                    pa, pak = bank()
                    pv = pa.bitcast(BF16).rearrange("p (c n) -> p c n", c=8)
                    for c in range(8):
                        t = half * 8 + c
                        tr(pv[:, c, :], srcf(t), identb, [srck, "cbf"], [pak])
                    ACT(lambda e: e.copy(out=dtok[:, half * 8:(half + 1) * 8, h, :], in_=pv), [pak], [dtk])
                    yield

        pipeline([conv_gen(m) for m in range(12)], 2)
        S.barrier()
        A.release("hT", "xbp0", "xbp1", "prb0", "prb1", "sqb0", "sqb1", "dwb0", "dwb1", "rtmp0", "rtmp1", "wsl0", "wsl1")

        alb = A.alloc("alb", [128, 8], F32)
        dtb = A.alloc("dtb", [128, 8], F32)
        gdnb = A.alloc("gdnb", [128, 128], F32)
        DMA("sp", alb, alog_d.partition_broadcast(128), [], ["alb"], "alb")
        DMA("sp", dtb, dtb_d.partition_broadcast(128), [], ["dtb"], "dtb")
        DMA("sp", gdnb, gdn_d.partition_broadcast(128), [], ["gdnb"], "gdnb")
        g_ = A.alloc("g_", [128, NT, 8], F32)
        lnb = A.alloc("lnb", [128, NT, 8], F32)
        gc = A.alloc("gc", [128, NT, 8], F32)
        gtot = A.alloc("gtot", [128, NT, 8], F32)
        gl = A.alloc("gl", [128, 2, NT, 8], F32)
        tmp8 = A.alloc("tmp8", [128, NT, 8], F32)
        ghl = A.alloc("ghl", [128, 2, NT, 8], BF16)
        DVE(lambda e: e.tensor_tensor(out=g_, in0=ab[:, :, 0:8], in1=bc(dtb.unsqueeze(1), [128, NT, 8]), op=ALU.add), ["ab", "dtb"], ["g_"])
        ACT(lambda e: e.activation(out=g_, in_=g_, func=AF.Exp), ["g_"], ["g_"])
        ACT(lambda e: e.activation(out=g_, in_=g_, func=AF.Ln, bias=1.0), ["g_"], ["g_"])
        ACT(lambda e: e.activation(out=alb, in_=alb, func=AF.Exp), ["alb"], ["alb"])
        DVE(lambda e: e.scalar_tensor_tensor(out=g_, in0=g_, scalar=-1.0, in1=bc(alb.unsqueeze(1), [128, NT, 8]), op0=ALU.mult, op1=ALU.mult), ["g_", "alb"], ["g_"])
        ACT(lambda e: e.activation(out=lnb, in_=ab[:, :, 8:16], func=AF.Exp, scale=-1.0), ["ab"], ["lnb"])
        ACT(lambda e: e.activation(out=lnb, in_=lnb, func=AF.Ln, bias=1.0), ["lnb"], ["lnb"])
        DVE(lambda e: e.tensor_scalar(out=lnb, in0=lnb, scalar1=-1.0, scalar2=None, op0=ALU.mult), ["lnb"], ["lnb"])
        DVE(lambda e: e.tensor_copy(out=ghl[:, 0], in_=g_), ["g_"], ["ghl"])
        DVE(lambda e: e.tensor_tensor(out=tmp8, in0=g_, in1=ghl[:, 0], op=ALU.subtract), ["g_", "ghl"], ["tmp8"])
        DVE(lambda e: e.tensor_copy(out=ghl[:, 1], in_=tmp8), ["tmp8"], ["ghl"])
        cfb = A.alloc("cfb", [128, 5, 128], BF16)
        DVE(lambda e: e.tensor_copy(out=cfb, in_=cf[:, 5:10, :]), ["cf"], ["cfb"])
        pa, pak = bank()
        for s_ in range(2):
            mm(pa[:, 0:64].rearrange("p (t h) -> p t h", t=NT), cfb[:, 0, :], ghl[:, s_, :, 0:4], s_ == 0, s_ == 1, ["cfb", "ghl"], [pak])
        for s_ in range(2):
            mm(pa[:, 64:128].rearrange("p (t h) -> p t h", t=NT), cfb[:, 1, :], ghl[:, s_, :, 4:8], s_ == 0, s_ == 1, ["cfb", "ghl"], [pak])
        for k_, cidx in ((0, 2), (1, 3), (2, 4)):
            for s_ in range(2):
                mm(pa[:, 128 + 128 * k_:256 + 128 * k_].rearrange("p (t h) -> p t h", t=NT), cfb[:, cidx, :], ghl[:, s_, :, :], s_ == 0, s_ == 1, ["cfb", "ghl"], [pak])
        DVE(lambda e, p=pa: e.tensor_copy(out=gc[:, :, 0:4], in_=p[:, 0:64].rearrange("p (t h) -> p t h", t=NT)), [pak], ["gc"])
        DVE(lambda e, p=pa: e.tensor_copy(out=gc[:, :, 4:8], in_=p[:, 64:128].rearrange("p (t h) -> p t h", t=NT)), [pak], ["gc"])
        DVE(lambda e, p=pa: e.tensor_copy(out=gtot, in_=p[:, 128:256].rearrange("p (t h) -> p t h", t=NT)), [pak], ["gtot"])
        ACT(lambda e, p=pa: e.activation(out=gl, in_=p[:, 256:512].rearrange("p (a t h) -> p a t h", a=2, t=NT), func=AF.Exp), [pak], ["gl"])
        kgs = A.alloc("kgs", [128, NT, 8], F32)
        bgs = A.alloc("bgs", [128, NT, 8], F32)
        beta = A.alloc("beta", [128, NT, 8], F32)
        gcb = A.alloc("gcb", [128, NT, 8], F32)
        DVE(lambda e: e.tensor_tensor(out=kgs, in0=gtot, in1=gc, op=ALU.subtract), ["gtot", "gc"], ["kgs"])
        ACT(lambda e: e.activation(out=kgs, in_=kgs, func=AF.Exp), ["kgs"], ["kgs"])
        DVE(lambda e: e.tensor_tensor(out=gcb, in0=gc, in1=lnb, op=ALU.add), ["gc", "lnb"], ["gcb"])
        ACT(lambda e: e.activation(out=bgs, in_=gcb, func=AF.Exp), ["gcb"], ["bgs"])
        ACT(lambda e: e.activation(out=beta, in_=lnb, func=AF.Exp), ["lnb"], ["beta"])
        GGf = A.alloc("GGf", [128, 4, 2, 2 * NT], F32)
        GGh = A.alloc("GGh", [128, 4, 2, 2 * NT], BF16)
        GGl = A.alloc("GGl", [128, 4, 2, 2 * NT], BF16)
        for h in range(4):
            for d_ in range(2):
                gv = GGf[:, h, d_, :].rearrange("p (t k) -> p t k", k=2)
                DVE(lambda e: e.tensor_copy(out=gv[:, :, 0], in_=gc[:, :, 4 * d_ + h]), ["gc"], ["GGf"])
                DVE(lambda e: e.tensor_copy(out=gv[:, :, 1], in_=gcb[:, :, 4 * d_ + h]), ["gcb"], ["GGf"])
        DVE(lambda e: e.tensor_copy(out=GGh, in_=GGf), ["GGf"], ["GGh"])
        DVE(lambda e: e.tensor_tensor(out=GGf, in0=GGf, in1=GGh, op=ALU.subtract), ["GGf", "GGh"], ["GGf"])
        DVE(lambda e: e.tensor_copy(out=GGl, in_=GGf), ["GGf"], ["GGl"])
        mb2 = A.alloc("mb2", [128, 2, 4, 128], F32)
        for pr in range(2):
            for a_ in range(4):
                POOL(lambda e: e.tensor_copy(out=mb2[:, pr, a_, :], in_=cf[:, 1 + 2 * pr + (a_ % 2), :]), ["cf"], ["mb2"])
        S.barrier()
        A.release("g_", "lnb", "gtot", "tmp8", "ghl", "alb", "dtb", "GGf", "ab", "cfb", "cf")
        dump("gc", gc.rearrange("p t h -> p (t h)"), [])

        dn_oT = A.alloc("dn_oT", [128, 4, S_LEN], BF16, top=True)
        uuG = [A.alloc("uuG%d" % i, [128, 2, 2, 128], BF16) for i in range(3)]
        kgG = [A.alloc("kgG%d" % i, [128, 2, 2, 128], BF16) for i in range(3)]
        qkG = [A.alloc("qkG%d" % i, [128, 2, 2, 128], BF16) for i in range(3)]
        ctG = [A.alloc("ctG%d" % i, [128, 2, 2, 128], BF16) for i in range(3)]
        atG = [A.alloc("atG%d" % i, [128, 2, 4, 128], BF16) for i in range(3)]
        qgG = [A.alloc("qgG%d" % i, [128, 2, 2, 128], BF16) for i in range(2)]
        wtok = A.alloc("wtok", [128, 2, 2, 128], BF16)
        osum = [A.alloc("osum%d" % i, [128, 2, NT, 128], BF16) for i in range(2)]
        osf = A.alloc("osf", [128, NT, 128], F32)
        dno = A.alloc("dno", [128, NT, 128], BF16)
        dgh = A.alloc("dgh", [128, 2, 4, 128], BF16)
        dgl = A.alloc("dgl", [128, 2, 4, 128], BF16)
        Dm = A.alloc("Dm", [128, 2, 4, 128], F32)
        Em = A.alloc("Em", [128, 2, 4, 128], F32)
        egr = A.alloc("egr", [128, 2, 2, 128], F32)
        P0b = [A.alloc("P0b%d" % i, [128, 2, 2, 128], BF16) for i in range(2)]
        vkb = [A.alloc("vkb%d" % i, [128, 2, 2, 2, 128], BF16) for i in range(2)]
        PR = [A.alloc("PR%d" % i, [128, 4, 2, 128], BF16) for i in range(2)]
        PT_ = [A.alloc("PT_%d" % i, [128, 4, 128], BF16) for i in range(2)]
        Sb = [A.alloc("Sb%d" % d_, [128, 128], BF16) for d_ in range(2)]
        vn = [A.alloc("vn%d" % d_, [128, 128], BF16) for d_ in range(2)]
        ident3 = bc(identb.unsqueeze(1), [128, 4, 128])

        def pair_t0(gi, pr):
            return 2 * gi if pr == 0 else 14 - 2 * gi

        for _once in range(1):
            def Egen(h, gi):
                par = gi % 2
                g3 = (8 * h + gi) % 3
                for pr in range(2):
                    t0 = pair_t0(gi, pr)
                    dh = 4 * pr + h
                    tsl = slice(t0 * 128, (t0 + 2) * 128)
                    gk = "_%d_%d" % (pr, gi)
                    POOL(lambda e: e.tensor_tensor(out=dgh[:, pr], in0=ident3, in1=bc(GGh[:, h, pr, 2 * t0:2 * t0 + 4].unsqueeze(2), [128, 4, 128]), op=ALU.mult), ["cbf", "GGh"], ["dgh%d" % pr])
                    POOL(lambda e: e.tensor_tensor(out=dgl[:, pr], in0=ident3, in1=bc(GGl[:, h, pr, 2 * t0:2 * t0 + 4].unsqueeze(2), [128, 4, 128]), op=ALU.mult), ["cbf", "GGl"], ["dgl%d" % pr])
                    pg, pgk = bank()
                    mm(pg, onesb, dgh[:, pr].rearrange("p a b -> p (a b)"), True, False, ["cbf", "dgh%d" % pr], [pgk])
                    mm(pg, onesb, dgl[:, pr].rearrange("p a b -> p (a b)"), False, True, ["cbf", "dgl%d" % pr], [pgk])
                    pg4 = pg.rearrange("p (a b) -> p a b", a=4)
                    for tt in range(2):
                        DVE(lambda e: e.scalar_tensor_tensor(out=Dm[:, pr, 2 * tt:2 * tt + 2, :], in0=pg4[:, 2 * tt:2 * tt + 2, :], scalar=gc[:, t0 + tt, dh:dh + 1], in1=mb2[:, pr, 2 * tt:2 * tt + 2, :], op0=ALU.subtract, op1=ALU.add), [pgk, "gc", "mb2"], ["Dm%d" % pr])
                    ACT(lambda e: e.activation(out=egr[:, pr], in_=pg4.rearrange("p (t k) n -> p t k n", t=2)[:, :, 0, :], func=AF.Exp), [pgk], ["egr%d" % pr])
                    ACT(lambda e: e.activation(out=Em[:, pr], in_=Dm[:, pr], func=AF.Exp), ["Dm%d" % pr], ["Em%d" % pr])
                    yield
                    POOL(lambda e: e.tensor_tensor(out=qgG[par][:, pr], in0=kqT[:, h, t0:t0 + 2, 1, :], in1=egr[:, pr], op=ALU.mult), ["kqT", "egr%d" % pr], ["qgG%d_%d" % (par, pr)])
                    pG, pGk = bank()
                    for tt in range(2):
                        ts_ = slice((t0 + tt) * 128, (t0 + tt + 1) * 128)
                        mm(pG[:, tt * 256:(tt + 1) * 256], kqT[:, h, t0 + tt, 0, :], kqT[:, h, t0 + tt, :, :].rearrange("p a n -> p (a n)"), True, True, ["kqT"], [pGk])
                    pG4 = pG.rearrange("p (t k n) -> p t k n", t=2, k=2)
                    Em4 = Em[:, pr].rearrange("p (t k) n -> p t k n", t=2)
                    DVE(lambda e: e.scalar_tensor_tensor(out=P0b[par][:, pr], in0=pG4[:, :, 0, :], scalar=-1.0, in1=Em4[:, :, 1, :], op0=ALU.mult, op1=ALU.mult), [pGk, "Em%d" % pr], ["P0b%d_%d" % (par, pr)])
                    DVE(lambda e: e.tensor_tensor(out=qkG[g3][:, pr], in0=pG4[:, :, 1, :], in1=Em4[:, :, 0, :], op=ALU.mult), [pGk, "Em%d" % pr], ["qkG%d_%d" % (g3, pr)])
                    yield
                    POOL(lambda e: e.tensor_tensor(out=vkb[par][:, pr, :, 0, :], in0=vtok[:, t0:t0 + 2, h, :], in1=bc(beta[:, t0:t0 + 2, dh:dh + 1], [128, 2, 128]), op=ALU.mult), ["vtok", "beta"], ["vbb%d_%d" % (par, pr)])
                    POOL(lambda e: e.tensor_tensor(out=vkb[par][:, pr, :, 1, :], in0=ktok[:, t0:t0 + 2, h, :], in1=bc(bgs[:, t0:t0 + 2, dh:dh + 1], [128, 2, 128]), op=ALU.mult), ["ktok", "bgs"], ["kbb%d_%d" % (par, pr)])
                    POOL(lambda e: e.tensor_tensor(out=kgG[g3][:, pr], in0=ktok[:, t0:t0 + 2, h, :], in1=bc(kgs[:, t0:t0 + 2, dh:dh + 1], [128, 2, 128]), op=ALU.mult), ["ktok", "kgs"], ["kgG%d_%d" % (g3, pr)])
                    yield

            def Ngen(h, gi):
                par = gi % 2
                P0 = P0b[par].rearrange("p a b n -> p (a b) n")
                p0k = ["P0b%d_%d" % (par, pr) for pr in range(2)]
                for pr in range(2):
                    ms = slice(2 * pr, 2 * pr + 2)
                    pa, pak = bank()
                    pv = pa.bitcast(BF16).rearrange("p (c n) -> p c n", c=8)
                    for tt in range(2):
                        tr(pv[:, tt, :], P0[:, 2 * pr + tt, :], identb, [p0k[pr], "cbf"], [pak])
                    ACT(lambda e: e.copy(out=PT_[0][:, ms, :], in_=pv[:, 0:2, :]), [pak], ["PT0_%d" % pr])
                    POOL(lambda e: e.tensor_tensor(out=PR[1][:, ms, 1, :], in0=P0[:, ms, :], in1=ident3[:, 0:2, :], op=ALU.add), [p0k[pr], "cbf"], ["PR1r_%d" % pr])
                yield
                for pr in range(2):
                    ms = slice(2 * pr, 2 * pr + 2)
                    pa, pak = bank()
                    for tt in range(2):
                        m_ = 2 * pr + tt
                        mm(pa[:, tt * 128:(tt + 1) * 128], PT_[0][:, m_, :], P0[:, m_, :], True, True, [p0k[pr], "PT0_%d" % pr], [pak])
                    for tt in range(2):
                        m_ = 2 * pr + tt
                        mm(pa[:, 256 + tt * 128:256 + (tt + 1) * 128], P0[:, m_, :], PT_[0][:, m_, :], True, True, [p0k[pr], "PT0_%d" % pr], [pak])
                    pav = pa.rearrange("p (a t n) -> p a t n", a=2, t=2)
                    ACT(lambda e: e.copy(out=PR[1][:, ms, 0, :], in_=pav[:, 0]), [pak], ["PR1p_%d" % pr])
                    ACT(lambda e: e.copy(out=PT_[1][:, ms, :], in_=pav[:, 1]), [pak], ["PT1_%d" % pr])
                yield
                for k in range(1, 6):
                    ci, ni = k % 2, (k + 1) % 2
                    cur, nxt = PR[ci], PR[ni]
                    ptc = PT_[ci]
                    for pr in range(2):
                        ms = slice(2 * pr, 2 * pr + 2)
                        ptk = "PT%d_%d" % (ci, pr)
                        kp, kr = "PR%dp_%d" % (ci, pr), "PR%dr_%d" % (ci, pr)
                        np_, nr = "PR%dp_%d" % (ni, pr), "PR%dr_%d" % (ni, pr)
                        if k <= 3:
                            p2, p2k = bank()
                            for tt in range(2):
                                m_ = 2 * pr + tt
                                mm(p2[:, tt * 256:(tt + 1) * 256], ptc[:, m_, :], cur[:, m_, :, :].rearrange("p a n -> p (a n)"), True, True, [ptk, kp, kr], [p2k])
                            p2v = p2.rearrange("p (t a n) -> p t a n", t=2, a=2)
                            ACT(lambda e: e.copy(out=nxt[:, ms, 0, :], in_=p2v[:, :, 0, :]), [p2k], [np_])
                            DVE(lambda e: e.tensor_tensor(out=nxt[:, ms, 1, :], in0=p2v[:, :, 1, :], in1=cur[:, ms, 1, :], op=ALU.add), [p2k, kr], [nr])
                        else:
                            pc, pck = bank()
                            for tt in range(2):
                                m_ = 2 * pr + tt
                                mm(pc[:, tt * 128:(tt + 1) * 128], ptc[:, m_, :], cur[:, m_, 1, :], True, True, [ptk, kr], [pck])
                            DVE(lambda e: e.tensor_tensor(out=nxt[:, ms, 1, :], in0=pc[:, 0:256].rearrange("p (t n) -> p t n", t=2), in1=cur[:, ms, 1, :], op=ALU.add), [pck, kr], [nr])
                        if k <= 4:
                            pb_, pbk = bank()
                            for tt in range(2):
                                m_ = 2 * pr + tt
                                mm(pb_[:, tt * 128:(tt + 1) * 128], cur[:, m_, 0, :], ptc[:, m_, :], True, True, [ptk, kp], [pbk])
                            ACT(lambda e: e.copy(out=PT_[ni][:, ms, :], in_=pb_[:, 0:256].rearrange("p (t n) -> p t n", t=2)), [pbk], ["PT%d_%d" % (ni, pr)])
                    yield
                XR = PR[0]
                g3 = (8 * h + gi) % 3
                for pr in range(2):
                    t0 = pair_t0(gi, pr)
                    dh = 4 * pr + h
                    pu, puk = bank()
                    for tt in range(2):
                        mm(pu[:, tt * 256:(tt + 1) * 256], XR[:, pr * 2 + tt, 1, :], vkb[par][:, pr, tt, :, :].rearrange("p a n -> p (a n)"), True, True, ["PR0r_%d" % pr, "vbb%d_%d" % (par, pr), "kbb%d_%d" % (par, pr)], [puk])
                    puv = pu.rearrange("p (t a n) -> p t a n", t=2, a=2)
                    ACT(lambda e: e.copy(out=uuG[g3][:, pr], in_=puv[:, :, 0, :]), [puk], ["uuG%d_%d" % (g3, pr)])
                    DVE(lambda e: e.tensor_copy(out=wtok[:, pr], in_=puv[:, :, 1, :]), [puk], ["wtok%d" % pr])
                    yield
                    pAs = [bank(), bank()]
                    pC, pCk = bank()
                    for tt in range(2):
                        for cp in range(2):
                            cs_ = slice(cp * 64, cp * 64 + 64)
                            mm(pAs[cp][0][:, tt * 128:(tt + 1) * 128], wtok[cs_, pr, tt, :], kgG[g3][cs_, pr, tt, :], True, True, ["wtok%d" % pr, "kgG%d_%d" % (g3, pr)], [pAs[cp][1]], tp=(cp * 64, 0))
                        mm(pC[:, tt * 128:(tt + 1) * 128], wtok[:, pr, tt, :], qkG[g3][:, pr, tt, :], True, True, ["wtok%d" % pr, "qkG%d_%d" % (g3, pr)], [pCk])
                    for tt in range(2):
                        for cp in range(2):
                            DVE(lambda e: e.scalar_tensor_tensor(out=atG[g3][:, pr, tt * 2 + cp, :], in0=identb, scalar=gl[:, cp, t0 + tt, dh:dh + 1], in1=pAs[cp][0][:, tt * 128:(tt + 1) * 128], op0=ALU.mult, op1=ALU.subtract), [pAs[cp][1], "gl", "cbf"], ["atG%d_%d" % (g3, pr)])
                    DVE(lambda e: e.tensor_tensor(out=ctG[g3][:, pr], in0=qgG[par][:, pr], in1=pC[:, 0:256].rearrange("p (t n) -> p t n", t=2), op=ALU.subtract), [pCk, "qgG%d_%d" % (par, pr)], ["ctG%d_%d" % (g3, pr)])
                    yield

            def Sgen(h, gi):
                g3 = (8 * h + gi) % 3
                hp = h % 2
                if gi == 0:
                    for d_ in range(2):
                        POOL(lambda e: e.memset(Sb[d_], 0.0), [], ["Sb%d" % d_])
                for s_ in range(4):
                    step = 4 * gi + s_
                    info = []
                    for d_ in range(2):
                        n = step if d_ == 0 else 31 - step
                        t, cp = n // 2, n % 2
                        tt = t - pair_t0(gi, d_)
                        info.append(dict(d=d_, t=t, cp=cp, tt=tt, ps=slice(cp * 64, cp * 64 + 64), sbk="Sb%d" % d_, kq="%d_%d" % (g3, d_), pS=bank(), po=bank()))
                    for x_ in info:
                        mm(x_["pS"][0][:, 0:128], atG[g3][:, x_["d"], x_["tt"] * 2 + x_["cp"], :], Sb[x_["d"]], True, False, ["atG" + x_["kq"], x_["sbk"]], [x_["pS"][1]])
                    for x_ in info:
                        mm(x_["pS"][0][:, 0:128], kgG[g3][x_["ps"], x_["d"], x_["tt"], :], uuG[g3][x_["ps"], x_["d"], x_["tt"], :], False, True, ["kgG" + x_["kq"], "uuG" + x_["kq"]], [x_["pS"][1]], tp=(x_["cp"] * 64, 0))
                    for x_ in info:
                        mm(x_["po"][0][x_["ps"], 0:128], ctG[g3][:, x_["d"], x_["tt"], x_["ps"]], Sb[x_["d"]], True, False, ["ctG" + x_["kq"], x_["sbk"]], [x_["po"][1]], tp=(0, x_["cp"] * 64))
                    for x_ in info:
                        mm(x_["po"][0][x_["ps"], 0:128], qkG[g3][x_["ps"], x_["d"], x_["tt"], x_["ps"]], uuG[g3][x_["ps"], x_["d"], x_["tt"], :], False, True, ["qkG" + x_["kq"], "uuG" + x_["kq"]], [x_["po"][1]], tp=(x_["cp"] * 64, x_["cp"] * 64))
                    for x_ in info:
                        DVE(lambda e: e.tensor_copy(out=Sb[x_["d"]], in_=x_["pS"][0][:, 0:128]), [x_["pS"][1]], [x_["sbk"]])
                    for x_ in info:
                        ACT(lambda e: e.copy(out=osum[hp][x_["ps"], x_["d"], x_["t"], :], in_=x_["po"][0][x_["ps"], 0:128]), [x_["po"][1]], ["osum%d_%d_%d" % (hp, x_["d"], x_["t"])])
                    yield

            def drive(gens):
                gens = [g_ for g_ in gens if g_ is not None]
                while gens:
                    for g_ in list(gens):
                        try:
                            next(g_)
                        except StopIteration:
                            gens.remove(g_)

            def gate_gen(h):
                hp = h % 2
                okeys = ["osum%d_%d_%d" % (hp, d_, t) for d_ in range(2) for t in range(NT)]
                POOL(lambda e: e.tensor_tensor(out=osf, in0=osum[hp][:, 0], in1=osum[hp][:, 1], op=ALU.add), okeys, ["osf"])
                yield
                POOL(lambda e: e.tensor_tensor(out=dno, in0=osf, in1=osf, op=ALU.mult), ["osf"], ["dno"])
                yield
                orn = small[:, 0:NT]
                DVE(lambda e: e.tensor_reduce(out=orn, in_=dno, axis=AX.X, op=ALU.add), ["dno"], ["orn"])
                ACT(lambda e: e.activation(out=orn, in_=orn, func=AF.Sqrt, scale=1.0 / 128, bias=EPS), ["orn"], ["orn"])
                DVE(lambda e: e.reciprocal(out=orn, in_=orn), ["orn"], ["orn"])
                yield
                DVE(lambda e: e.tensor_tensor(out=osf, in0=osf, in1=bc(orn.unsqueeze(2), [128, NT, 128]), op=ALU.mult), ["orn", "osf"], ["osf"])
                yield
                POOL(lambda e: e.tensor_tensor(out=osf, in0=osf, in1=bc(gdnb.unsqueeze(1), [128, NT, 128]), op=ALU.mult), ["osf", "gdnb"], ["osf"])
                yield
                DVE(lambda e: e.tensor_tensor(out=dno, in0=osf, in1=gate_s[:, :, h * 128:(h + 1) * 128], op=ALU.mult), ["osf", "gate_s", "dno"], ["dno"])
                yield
                for half in range(2):
                    pa, pak = bank()
                    pv = pa.bitcast(BF16).rearrange("p (c n) -> p c n", c=8)
                    for c in range(8):
                        t = half * 8 + c
                        tr(pv[:, c, :], dno[:, t, :], identb, ["dno", "cbf"], [pak])
                    ACT(lambda e: e.copy(out=dn_oT[:, h, half * 1024:(half + 1) * 1024], in_=pv.rearrange("p c n -> p (c n)")), [pak], ["dn_oT"])
                    yield

            wo = A.view("kqT", [128, 8, D], BF16)
            gates = []
            for k_ in range(32 + 2):
                gens = []
                if k_ < 32:
                    gens.append(Egen(*divmod(k_, 8)))
                if k_ == 32:
                    for g in range(2):
                        load_w(wo[g * 64:(g + 1) * 64, 0:4, :], w_out[g * 256:(g + 1) * 256, :].rearrange("(j d) n -> d j n", d=64), "kqT")
                    load_w(wo[:, 4:8, :], w_out[512:1024, :].rearrange("(h p) n -> p h n", p=128), "kqT")
                if 0 <= k_ - 1 < 32:
                    gens.append(Ngen(*divmod(k_ - 1, 8)))
                if 0 <= k_ - 2 < 32:
                    gens.append(Sgen(*divmod(k_ - 2, 8)))
                gens += gates
                drive(gens)
                gates = []
                if k_ - 2 >= 0 and (k_ - 2) % 8 == 7:
                    gates = [gate_gen((k_ - 2) // 8)]
            drive(gates)
            S.barrier()
        A.release("ktok", "vtok", "uuG0", "uuG1", "uuG2", "kgG0", "kgG1", "kgG2", "qkG0", "qkG1", "qkG2", "ctG0", "ctG1", "ctG2", "atG0", "atG1", "atG2", "qgG0", "qgG1", "wtok", "osf", "dno", "osum0", "osum1", "dgh", "dgl", "Dm", "Em", "egr", "PR0", "PR1", "PT_0", "PT_1",
                  "P0b0", "P0b1", "vkb0", "vkb1", "mb2", "Sb0", "Sb1", "vn0", "vn1", "gate_s", "gc", "gl", "kgs", "bgs", "beta", "gcb", "GGh", "GGl",
                  "alb" if "alb" in A.live else "gdnb", "cwt")
        if "gdnb" in A.live:
            A.release("gdnb")

        dump("dn_oT", dn_oT[:, 0, :], [])
        x1 = A.alloc("x1", [128, NT, D], F32)
        DMA("sp", gbc, gains["g_mix_post"].partition_broadcast(128), [], ["gbc"], "gbc")
        NE = 3
        etmps = [A.alloc("etmp" if i == 0 else "etmp%d" % i, [128, D], F32, top=True) for i in range(NE)]
        etk = ["etmp%d" % i for i in range(NE)]
        xbuf = [A.alloc("xb%d" % i, [128, D], F32) for i in range(NE)]

        h2T = A.alloc("h2T", [128, 8, S_LEN], BF16)
        gbc2 = A.alloc("gbc2", [128, D], F32)
        cxb = norm_begin(gains["g_cross_pre"], "b", gbc2, "gbc2")

        def oproj_gen(t):
            ts_ = slice(t * 128, (t + 1) * 128)
            DMA("sp", xbuf[t % NE], x_d[ts_, :], [], ["xb%d" % (t % NE)], "xb%d" % (t % NE))
            p2, p2k = bank2()
            for c in range(8):
                for hf in range(2):
                    lhs = attn_oT[:, c, ts_] if c < 4 else dn_oT[:, c - 4, ts_]
                    mm(p2[:, hf * 512:(hf + 1) * 512], lhs, wo[:, c, hf * 512:(hf + 1) * 512], c == 0, c == 7, ["attn_oT", "dn_oT"], [p2k[hf]])
            yield
            yield from resid_epilogue(p2, p2k, None, xbuf[t % NE], ["xb%d" % (t % NE)], x1[:, t, :], ["x1_%d" % t], etmps[t % NE], etk[t % NE])

        gl_ = []
        for t in range(NT):
            gl_.append(oproj_gen(t))
            if t >= NE - 1:
                tn = t - (NE - 1)
                gl_.append(norm_item(cxb, tn, x1[:, tn, :], ["x1_%d" % tn], None, h2T, "h2T"))
        for tn in range(NT - (NE - 1), NT):
            gl_.append(norm_item(cxb, tn, x1[:, tn, :], ["x1_%d" % tn], None, h2T, "h2T"))
        pipeline(gl_, NE)
        S.barrier()
        norm_end(cxb)
        A.release("attn_oT", "dn_oT", "kqT", "gbc2")
        dump("x1", x1[:, 0, :], [])

        wsl = [A.alloc("wsm%d" % i, [128, 8, 512], BF16) for i in range(2)]
        pre_ckv = []
        for q4 in range(2):
            w_, wk_ = wslot()
            load_w(w_, kview(w_ckv, q4 * 512, (q4 + 1) * 512), wk_)
            pre_ckv.append((w_, wk_))
        memT = A.alloc("memT", [128, 8, 256], BF16)

        def mem_tiles():
            out = []
            for t in range(2):
                def loader(t=t):
                    DMA("sp", xbuf[t % 2], mem_d[t * 128:(t + 1) * 128, :], [], ["xb%d" % (t % 2)], "xb%d" % (t % 2))
                out.append((xbuf[t % 2], ["xb%d" % (t % 2)], loader))
            return out
        norm_T(mem_tiles(), gains["g_mem"], memT, "memT", "c")
        kcT = A.alloc("kcT", [128, 8, 256], BF16)
        vc = A.alloc("vc", [128, 2, D], BF16)
        for q4 in range(2):
            w_, wk_ = pre_ckv[q4]
            for nn in range(4):
                n = q4 * 4 + nn
                pa, pak = bank()
                for c in range(8):
                    mm(pa[:, 0:256], w_[:, c, nn * 128:(nn + 1) * 128], memT[:, c, :], c == 0, c == 7, [wk_, "memT0", "memT1"], [pak])
                ACT(lambda e, n=n, pa=pa: e.copy(out=kcT[:, n, :], in_=pa[:, 0:256]), [pak], ["kcT"])
        for q4 in range(2):
            w_, wk_ = wslot()
            load_w(w_, kview(w_ckv, D + q4 * 512, D + (q4 + 1) * 512), wk_)
            for mt in range(2):
                pa, pak = bank()
                for c in range(8):
                    mm(pa, memT[:, c, mt * 128:(mt + 1) * 128], w_[:, c, :], c == 0, c == 7, [wk_, "memT%d" % mt], [pak])
                DVE(lambda e, mt=mt, q4=q4, pa=pa: e.tensor_copy(out=vc[:, mt, q4 * 512:(q4 + 1) * 512], in_=pa), [pak], ["vc"])
        qcT = A.alloc("qcT", [128, 8, S_LEN], BF16)
        for q4 in range(2):
            w_, wk_ = wslot()
            load_w(w_, kview(w_cq, q4 * 512, (q4 + 1) * 512), wk_)
            for nn in range(4):
                n = q4 * 4 + nn
                bks = [bank() for _ in range(4)]
                for c in range(8):
                    for n4 in range(4):
                        mm(bks[n4][0], w_[:, c, nn * 128:(nn + 1) * 128], h2T[:, c, n4 * 512:(n4 + 1) * 512], c == 0, c == 7, [wk_] + hkeys("h2T", n4), [bks[n4][1]])
                for n4 in range(4):
                    cs = slice(n4 * 512, (n4 + 1) * 512)
                    if (n + n4) % 2 == 0:
                        ACT(lambda e: e.copy(out=qcT[:, n, cs], in_=bks[n4][0]), [bks[n4][1]], ["qcT%d" % n])
                    else:
                        DVE(lambda e: e.tensor_copy(out=qcT[:, n, cs], in_=bks[n4][0]), [bks[n4][1]], ["qcT%d" % n])
        S.barrier()
        A.release("h2T", "memT", "wsm0", "wsm1")
        ocT = A.alloc("ocT", [128, 8, S_LEN], BF16)
        wco = A.alloc("wco", [128, 8, D], BF16)
        load_w(wco[:, :, 0:512], kview(w_co, 0, 512), "wco")
        load_w(wco[:, :, 512:1024], kview(w_co, 512, 1024), "wco")
        Pc = [A.alloc("Pc%d" % i, [128, 512], BF16) for i in range(4)]
        rec = [A.alloc("rcc%d" % i, [128, 512], F32) for i in range(2)]
        pcc = [0]
        def cross_gen(hh, n4):
            cs = slice(n4 * 512, (n4 + 1) * 512)
            Ps = []
            for mt in range(2):
                pS, pSk = bank()
                for dc in range(2):
                    mm(pS, kcT[:, 2 * hh + dc, mt * 128:(mt + 1) * 128], qcT[:, 2 * hh + dc, cs], dc == 0, dc == 1, ["kcT", "qcT%d" % (2 * hh + dc)], [pSk])
                pi = pcc[0] % 4
                pcc[0] += 1
                ACT(lambda e: e.activation(out=Pc[pi], in_=pS, func=AF.Exp, scale=1.0 / 16), [pSk], ["Pc%d" % pi])
                Ps.append((Pc[pi], "Pc%d" % pi))
            yield
            pD, pDk = bank()
            for mt in range(2):
                mm(pD, onesb, Ps[mt][0], mt == 0, mt == 1, ["cbf", Ps[mt][1]], [pDk])
            pOs = []
            for dc in range(2):
                pO, pOk = bank()
                for mt in range(2):
                    mm(pO, vc[:, mt, (2 * hh + dc) * 128:(2 * hh + dc + 1) * 128], Ps[mt][0], mt == 0, mt == 1, ["vc", Ps[mt][1]], [pOk])
                pOs.append((pO, pOk))
            yield
            ri = (hh * 4 + n4) % 2
            ACT(lambda e: e.activation(out=rec[ri], in_=pD, func=AF.Ln), [pDk], ["rcc%d" % ri])
            ACT(lambda e: e.activation(out=rec[ri], in_=rec[ri], func=AF.Exp, scale=-1.0), ["rcc%d" % ri], ["rcc%d" % ri])
            yield
            for dc in range(2):
                pO, pOk = pOs[dc]
                DVE(lambda e: e.tensor_tensor(out=ocT[:, 2 * hh + dc, cs], in0=pO, in1=rec[ri], op=ALU.mult), [pOk, "rcc%d" % ri], ["ocT"])
            yield

        pipeline([cross_gen(hh, n4) for hh in range(4) for n4 in range(4)], 2)
        S.barrier()
        A.release("kcT", "vc", "qcT", "Pc0", "Pc1", "Pc2", "Pc3", "rcc0", "rcc1")
        wd = A.alloc("wd", [128, NFF, D], BF16, top=True)
        wdv = w_dn.rearrange("(c p) n -> p c n", p=128)
        for i in range(0, NFF, 6):
            load_w(wd[:, i:min(i + 6, NFF), :], wdv[:, i:min(i + 6, NFF), :], "wd")
        DMA("sp", gbc, gains["g_cross_post"].partition_broadcast(128), [], ["gbc"], "gbc")
        def coproj_gen(t):
            ts_ = slice(t * 128, (t + 1) * 128)
            p2, p2k = bank2()
            for c in range(8):
                for hf in range(2):
                    mm(p2[:, hf * 512:(hf + 1) * 512], ocT[:, c, ts_], wco[:, c, hf * 512:(hf + 1) * 512], c == 0, c == 7, ["ocT", "wco"], [p2k[hf]])
            yield
            yield from resid_epilogue(p2, p2k, None, x1[:, t, :], ["x1_%d" % t], x1[:, t, :], ["x1_%d" % t], etmps[t % NE], etk[t % NE])

        pipeline([coproj_gen(t) for t in range(NT)], NE)
        S.barrier()
        A.release("ocT", "wco")
        dump("x2", x1[:, 0, :], [])

        A.release("xb0", "xb1", "xb2")
        A.release("etmp1", "etmp2")
        etmps = [etmps[0], etmps[0]]
        etk = ["etmp0", "etmp0"]
        h3T = A.alloc("h3T", [128, 8, S_LEN], BF16)
        wg = [A.alloc("wg%d" % i, [128, 8, 2, 128], BF16, top=True) for i in range(2)]
        for i in range(2):
            load_w(wg[i].rearrange("p c a n -> p (c a n)"), w_gu[i], "wg%d" % i)
        norm_T([(x1[:, t, :], ["x1_%d" % t], None) for t in range(NT)], gains["g_ffn_pre"], h3T, "h3T", "d")
        DMA("sp", gbc, gains["g_ffn_post"].partition_broadcast(128), [], ["gbc"], "gbc")
        aT = A.alloc("aT", [128, NFF, 1024], BF16)
        sg = [A.alloc("sg%d" % i, [128, 512], BF16) for i in range(3)]
        wgc = [0]
        sgc = [0]
        for tg in range(2):
            def gu_gen(i, wi):
                banks = {}
                for nn in range(2):
                    banks[("g", nn)] = bank()
                    banks[("u", nn)] = bank()
                for c in range(8):
                    for gu, a_ in (("g", 0), ("u", 1)):
                        for nn in range(2):
                            n4 = tg * 2 + nn
                            cs = slice(n4 * 512, (n4 + 1) * 512)
                            pb_, pbk = banks[(gu, nn)]
                            mm(pb_, wg[wi][:, c, a_, :], h3T[:, c, cs], c == 0, c == 7, ["wg%d" % wi] + hkeys("h3T", n4), [pbk])
                yield
                for nn in range(2):
                    si = (2 * i + nn) % 3
                    ACT(lambda e: e.activation(out=sg[si], in_=banks[("g", nn)][0], func=AF.Silu), [banks[("g", nn)][1]], ["sg%d" % si])
                yield
                for nn in range(2):
                    si = (2 * i + nn) % 3
                    DVE(lambda e: e.tensor_tensor(out=aT[:, i, nn * 512:(nn + 1) * 512], in0=banks[("u", nn)][0], in1=sg[si], op=ALU.mult), [banks[("u", nn)][1], "sg%d" % si], ["aT%d" % nn])
                yield

            gl_ = []
            for i in range(NFF):
                wi = i % 2
                def ldgen(i=i, wi=wi):
                    load_w(wg[wi].rearrange("p c a n -> p (c a n)"), w_gu[i], "wg%d" % wi)
                    yield
                if not (tg == 0 and i < 2):
                    gl_.append(ldgen())
                gl_.append(gu_gen(i, wi))
            pipeline(gl_, 2)
            def down_gen(tt):
                t = tg * 8 + tt
                p2, p2k = bank2()
                for i in range(NFF):
                    for hf in range(2):
                        mm(p2[:, hf * 512:(hf + 1) * 512], aT[:, i, tt * 128:(tt + 1) * 128], wd[:, i, hf * 512:(hf + 1) * 512], i == 0, i == NFF - 1, ["aT%d" % (tt // 4), "wd"], [p2k[hf]])
                yield
                yield from resid_epilogue(p2, p2k, None, x1[:, t, :], ["x1_%d" % t], x1[:, t, :], ["x1_%d" % t], etmps[t % 2], etk[t % 2])
                DMA("sp", out_d[t * 128:(t + 1) * 128, :], x1[:, t, :], ["x1_%d" % t], [], "out")
                yield

            pipeline([down_gen(tt) for tt in range(8)], 1)
        stats = S.finalize(final_waits=["out"] + ["dbg_" + k for k in dbg])
    return nc, stats


def _consts():
    bf = ml_dtypes.bfloat16
    j = np.arange(128)[:, None]
    i = np.arange(128)[None, :]
    cst_bf = np.zeros((128, 4, 128), np.float32)
    cst_bf[:, 0] = np.eye(128)
    cst_bf[:, 1] = 1.0
    cst_bf[:, 2] = (i <= j)
    cst_bf[:, 3] = (j <= i)
    same = (j // 64) == (i // 64)
    NEG = -30000.0
    cst_f = np.zeros((128, 10, 128), np.float32)
    cst_f[:, 0] = np.eye(128)
    cst_f[:, 1] = np.where(same & (i >= j), 0.0, NEG)
    cst_f[:, 2] = np.where(same & (i > j), 0.0, NEG)
    cst_f[:, 3] = np.where(same & (i <= j), 0.0, NEG)
    cst_f[:, 4] = np.where(same & (i < j), 0.0, NEG)
    cst_f[:, 5] = same & (j <= i)
    cst_f[:, 6] = same & (j >= i)
    cst_f[:, 7] = same
    cst_f[:, 8] = (j < 64) & (i >= 0)
    cst_f[:, 9] = (j >= 64) & (i >= 0)
    d = np.arange(128) % 64
    inv = (10000.0 ** (-(d % 32).astype(np.float32) / np.float32(32))).astype(np.float32)
    sign = np.where(d < 32, -1.0, 1.0).astype(np.float32)
    cst_c = np.stack([inv, sign], 1).astype(np.float32)
    return cst_bf.astype(bf), cst_f, cst_c


_CACHE = {}


def _prep_weights(w_in, conv_w):
    w = np.asarray(w_in)[0]
    cols = []
    q0, k0, v0, dq0, dg0, da0, db0 = 0, 512, 640, 768, 2304, 2816, 2824

    def head(base, hd):
        return list(range(base + hd * 64, base + hd * 64 + 64))

    def swp(c):
        return c[32:] + c[:32]
    for j in range(4):
        cols += head(q0, j) + head(q0, 4 + j)
    for j in range(4):
        cols += swp(head(q0, j)) + swp(head(q0, 4 + j))
    cols += head(k0, 0) + head(k0, 1)
    cols += swp(head(k0, 0)) + swp(head(k0, 1))
    cols += list(range(dq0, dq0 + 1536))
    cols += list(range(dg0, dg0 + 512))
    cols += list(range(v0, v0 + 128))
    cols += list(range(da0, da0 + 8)) + list(range(db0, db0 + 8))
    assert len(cols) == 3472
    w_inr = np.ascontiguousarray(w[:, np.array(cols)])
    cw = np.ascontiguousarray(np.asarray(conv_w)[0].T.reshape(12, 128, 5).transpose(1, 0, 2))
    return w_inr, cw


def kernel(x, mem, positions, g_mix_pre, w_in, conv_w, a_log, dt_bias, g_dn_out, attn_sink,
           w_out, g_mix_post, g_cross_pre, g_mem, w_cq, w_ckv, w_co, g_cross_post,
           g_ffn_pre, w_gate_up, w_down, g_ffn_post):
    f = lambda a: np.ascontiguousarray(np.asarray(a, dtype=np.float32))
    if "nc" not in _CACHE:
        _CACHE["nc"] = build_program()
    nc, stats = _CACHE["nc"]
    cst_bf, cst_f, cst_c = _consts()
    w_inr, cw = _prep_weights(w_in, conv_w)
    shared = dict(
        w_inr=w_inr, cw=cw, alog=f(a_log).reshape(1, 8), dtb=f(dt_bias).reshape(1, 8), gdn=f(g_dn_out).reshape(1, 128),
        sink=f(attn_sink).reshape(1, 8), w_out=f(w_out)[0], w_cq=f(w_cq)[0], w_ckv=f(w_ckv)[0], w_co=f(w_co)[0],
        w_gu=np.ascontiguousarray(f(w_gate_up)[0].reshape(8, 128, 2, NFF, 128).transpose(3, 1, 0, 2, 4)).reshape(NFF, 128, 2048), w_dn=f(w_down)[0],
        g_mix_pre=f(g_mix_pre).reshape(1, D), g_mix_post=f(g_mix_post).reshape(1, D), g_cross_pre=f(g_cross_pre).reshape(1, D),
        g_mem=f(g_mem).reshape(1, D), g_cross_post=f(g_cross_post).reshape(1, D), g_ffn_pre=f(g_ffn_pre).reshape(1, D),
        g_ffn_post=f(g_ffn_post).reshape(1, D), cst_bf=cst_bf, cst_f=cst_f, cst_c=cst_c,
    )
    xs = f(x)
    ms = f(mem)
    ps = np.ascontiguousarray(np.asarray(positions).astype(np.int32))
    in_maps = []
    for b in range(8):
        m = dict(shared)
        m["x"] = xs[b]
        m["mem"] = ms[b]
        m["pos"] = ps[b:b + 1]
        in_maps.append(m)
    res = run_bass_kernel_spmd(nc, in_maps, core_ids=list(range(8)))
    _CACHE["res"] = res
    return np.stack([np.asarray(r["out"]) for r in res.results], 0).astype(np.float32)
```

```python
import numpy as np
import ml_dtypes
import concourse.bass as bass
import concourse.mybir as mybir
from concourse.bass_utils import run_bass_kernel_spmd
from contextlib import ExitStack

F32 = mybir.dt.float32
BF16 = mybir.dt.bfloat16
I32 = mybir.dt.int32
AF = mybir.ActivationFunctionType
ALU = mybir.AluOpType
AX = mybir.AxisListType

S_LEN = 2048
NT = 16
D = 1024
EPS = 1e-6
DFF = 2816
NFF = 22
PI = float(np.pi)
TWO_PI = float(2 * np.pi)
DEBUG = {}


class Ins:
    __slots__ = ("eng", "fn", "deps", "needed", "sem", "val", "is_dma")


class _Rec:
    def __init__(self):
        self.call = None

    def __getattr__(self, name):
        def f(*a, **k):
            self.call = (name, a, k)
            return self
        return f


class Sched:
    ENG = ("pe", "act", "dve", "pool", "sp")

    def __init__(self, nc, es):
        self.nc = nc
        self.es = es
        self.streams = {e: [] for e in self.ENG}
        self.last_w = {}
        self.readers = {}
        self.esem = {e: es.enter_context(nc.semaphore("s_" + e)) for e in self.ENG}
        self.dsem = {}
        self.dcount = {}
        self.last = {}

    def op(self, eng, fn, reads=(), writes=(), dma=None):
        ins = Ins()
        ins.eng = eng
        rec = _Rec()
        fn(rec)
        assert rec.call is not None
        ins.fn = rec.call
        ins.needed = False
        ins.is_dma = dma is not None
        px = [k for k in reads if k.startswith("pb")]
        if px:
            reads = [k for k in reads if not k.startswith("pb")]
            writes = list(writes) + [k for k in px if k not in writes]
        deps = []
        for k in reads:
            w = self.last_w.get(k)
            if w is not None:
                deps.append(w)
        strict = eng == "pool"
        for k in writes:
            w = self.last_w.get(k)
            if w is not None and (w.eng != eng or w.is_dma or ins.is_dma or strict):
                deps.append(w)
            for e, r in self.readers.get(k, {}).items():
                if e != eng or r.is_dma or ins.is_dma or strict:
                    deps.append(r)
        for d in deps:
            d.needed = True
        ins.deps = deps
        if ins.is_dma:
            if dma not in self.dsem:
                self.dsem[dma] = self.es.enter_context(self.nc.semaphore("d_" + dma))
                self.dcount[dma] = 0
            self.dcount[dma] += 16
            ins.sem = self.dsem[dma]
            ins.val = self.dcount[dma]
        else:
            ins.sem = self.esem[eng]
            ins.val = None
            self.last[eng] = ins
        for k in writes:
            self.last_w[k] = ins
            self.readers[k] = {}
        for k in reads:
            rk = self.readers.setdefault(k, {})
            rk[eng if not ins.is_dma else (eng, dma)] = ins
        self.streams[eng].append(ins)
        return ins

    def barrier(self):
        lasts = [i for i in self.last.values()]
        dm = []
        for name in self.dsem:
            d = Ins()
            d.eng = "sp"
            d.is_dma = True
            d.sem = self.dsem[name]
            d.val = self.dcount[name]
            d.needed = True
            dm.append(d)
        for i in lasts:
            i.needed = True
        for e in self.ENG:
            ins = Ins()
            ins.eng = e
            ins.fn = None
            ins.needed = False
            ins.is_dma = False
            ins.deps = [i for i in lasts if i.eng != e] + dm
            ins.sem = self.esem[e]
            ins.val = None
            self.streams[e].append(ins)
        self.last_w = {}
        self.readers = {}

    def finalize(self, final_waits=()):
        for e in self.ENG:
            c = 0
            for ins in self.streams[e]:
                if not ins.is_dma and ins.needed and ins.fn is not None:
                    c += 1
                    ins.val = c
                elif not ins.is_dma and ins.fn is None:
                    ins.val = c
        streams = self.streams
        stats = {}

        def emit(engobj, ename):
            seen = {}
            nw = 0
            for ins in streams[ename]:
                need = {}
                for d in ins.deps:
                    if d.val is None or d.val == 0:
                        continue
                    sid = id(d.sem)
                    if sid not in need or need[sid][1] < d.val:
                        need[sid] = (d.sem, d.val)
                for sid, (sem, val) in need.items():
                    if seen.get(sid, 0) < val:
                        engobj.wait_ge(sem, val)
                        seen[sid] = val
                        nw += 1
                if ins.fn is None:
                    continue
                nm_, a_, k_ = ins.fn
                bi = getattr(engobj, nm_)(*a_, **k_)
                if ins.is_dma:
                    bi.then_inc(ins.sem, 16)
                elif ins.needed:
                    bi.then_inc(ins.sem, 1)
            if ename == "sp":
                for name in final_waits:
                    engobj.wait_ge(self.dsem[name], self.dcount[name])
            stats[ename] = (len(streams[ename]), nw)

        with self.nc.Block() as block:
            @block.tensor
            def _(e):
                emit(e, "pe")

            @block.scalar
            def _(e):
                emit(e, "act")

            @block.vector
            def _(e):
                emit(e, "dve")

            @block.gpsimd
            def _(e):
                emit(e, "pool")

            @block.sync
            def _(e):
                emit(e, "sp")
        return stats


class Arena:
    def __init__(self, ap, nwords):
        self.ap = ap
        self.free = [(0, nwords * 4)]
        self.live = {}

    def alloc(self, name, shape, dt, top=False):
        esz = 2 if dt == BF16 else 4
        n = 1
        for s in shape[1:]:
            n *= s
        nbytes = ((n * esz + 63) // 64) * 64
        order = range(len(self.free) - 1, -1, -1) if top else range(len(self.free))
        for idx in order:
            off, sz = self.free[idx]
            if sz >= nbytes:
                if top:
                    self.free[idx] = (off, sz - nbytes)
                    off = off + sz - nbytes
                else:
                    self.free[idx] = (off + nbytes, sz - nbytes)
                if sz == nbytes:
                    del self.free[idx]
                break
        else:
            raise RuntimeError("arena OOM for %s (%d bytes); free=%s live=%s" % (name, nbytes, self.free, sorted((v[0], v[1], k) for k, v in self.live.items())))
        self.live[name] = (off, nbytes)
        v = self.ap[:, off // 4:(off + nbytes) // 4]
        if dt != F32:
            v = v.bitcast(dt)
        v = v[:, 0:n]
        if len(shape) == 3:
            v = v.rearrange("p (a b) -> p a b", a=shape[1])
        elif len(shape) == 4:
            v = v.rearrange("p (a b c) -> p a b c", a=shape[1], b=shape[2])
        elif len(shape) == 5:
            v = v.rearrange("p (a b c d) -> p a b c d", a=shape[1], b=shape[2], c=shape[3])
        if shape[0] < 128:
            v = v[0:shape[0]]
        return v

    def view(self, name, shape, dt):
        off, nb = self.live[name]
        esz = 2 if dt == BF16 else 4
        n = 1
        for s in shape[1:]:
            n *= s
        assert n * esz <= nb
        v = self.ap[:, off // 4:(off + nb) // 4]
        if dt != F32:
            v = v.bitcast(dt)
        v = v[:, 0:n]
        if len(shape) == 3:
            v = v.rearrange("p (a b) -> p a b", a=shape[1])
        return v

    def release(self, *names):
        for name in names:
            off, nb = self.live.pop(name)
            self.free.append((off, nb))
        self.free.sort()
        m = []
        for off, sz in self.free:
            if m and m[-1][0] + m[-1][1] == off:
                m[-1] = (m[-1][0], m[-1][1] + sz)
            else:
                m.append((off, sz))
        self.free = m


def bc(ap, shape):
    return ap.to_broadcast(list(shape))


def build_program():
    nc = bass.Bass("TRN2", target_bir_lowering=False)

    def din(name, shape, dt=F32):
        return nc.dram_tensor(name, list(shape), dt, kind="ExternalInput").ap()

    x_d = din("x", [S_LEN, D])
    mem_d = din("mem", [256, D])
    pos_d = din("pos", [1, S_LEN], I32)
    w_inr = din("w_inr", [D, 3472])
    cw_d = din("cw", [128, 12, 5])
    alog_d = din("alog", [1, 8])
    dtb_d = din("dtb", [1, 8])
    gdn_d = din("gdn", [1, 128])
    sink_d = din("sink", [1, 8])
    w_out = din("w_out", [D, D])
    w_cq = din("w_cq", [D, D])
    w_ckv = din("w_ckv", [D, 2 * D])
    w_co = din("w_co", [D, D])
    w_gu = din("w_gu", [NFF, 128, 2048])
    w_dn = din("w_dn", [DFF, D])
    gains = {k: din(k, [1, D]) for k in ("g_mix_pre", "g_mix_post", "g_cross_pre", "g_mem", "g_cross_post", "g_ffn_pre", "g_ffn_post")}
    cst_bf = din("cst_bf", [128, 4, 128], BF16)
    cst_f = din("cst_f", [128, 10, 128])
    cst_c = din("cst_c", [128, 2])
    out_d = nc.dram_tensor("out", [S_LEN, D], F32, kind="ExternalOutput").ap()
    dbg = {}
    for k, shp in DEBUG.items():
        dbg[k] = nc.dram_tensor("dbg_" + k, list(shp), F32, kind="ExternalOutput").ap()

    es = ExitStack()
    with es:
        S = Sched(nc, es)
        NW = 52900
        arena_t = es.enter_context(nc.sbuf_tensor("arena", [128, NW], F32))
        A = Arena(arena_t[:], NW)
        pbig = [es.enter_context(nc.psum_tensor("pb%d" % i, [128, 1024], F32)) for i in range(4)]
        bank_ctr = [0]

        def bank():
            i = bank_ctr[0] % 8
            bank_ctr[0] += 1
            return pbig[i // 2][:, (i % 2) * 512:(i % 2) * 512 + 512], "pb%d" % i

        def bank_at(i):
            return pbig[i // 2][:, (i % 2) * 512:(i % 2) * 512 + 512], "pb%d" % i

        rot4 = [0]

        def bank_hi():
            i = 4 + rot4[0] % 4
            rot4[0] += 1
            return bank_at(i)

        def bank2():
            if bank_ctr[0] % 2:
                bank_ctr[0] += 1
            i = bank_ctr[0] % 8
            bank_ctr[0] += 2
            return pbig[i // 2][:], ["pb%d" % i, "pb%d" % (i + 1)]

        def pipeline(gens, depth):
            gens = list(gens)
            active = []
            while gens or active:
                if gens and len(active) < depth:
                    active.append(gens.pop(0))
                for g_ in list(active):
                    try:
                        next(g_)
                    except StopIteration:
                        active.remove(g_)

        def PE(fn, r, w):
            S.op("pe", fn, reads=r, writes=w)

        def ACT(fn, r, w):
            S.op("act", fn, reads=r, writes=w)

        def DVE(fn, r, w):
            S.op("dve", fn, reads=r, writes=w)

        def POOL(fn, r, w):
            S.op("pool", fn, reads=r, writes=w)

        def DMA(q, out, in_, r, w, grp):
            S.op(q, lambda e, o=out, i=in_: e.dma_start(out=o, in_=i), reads=r, writes=w, dma=grp)

        def mm(out, lhsT, rhs, start, stop, r, w, tp=None):
            if tp is None:
                PE(lambda e, o=out, l=lhsT, rr=rhs, s=start, t=stop: e.matmul(out=o, lhsT=l, rhs=rr, start=s, stop=t), r, w)
            else:
                PE(lambda e, o=out, l=lhsT, rr=rhs, s=start, t=stop, tp=tp: e.matmul(out=o, lhsT=l, rhs=rr, start=s, stop=t, tile_position=tp), r, w)

        def tr(out, in_, ident, r, w):
            PE(lambda e, o=out, i=in_, d=ident: e.transpose(out=o, in_=i, identity=d), r, w)

        def dump(name, ap, keys, rows=128):
            if name in dbg:
                tmp = A.alloc("dbgtmp_" + name, list(ap.shape), F32)
                DVE(lambda e, o=tmp, i=ap: e.tensor_copy(out=o, in_=i), keys, ["dbgtmp_" + name])
                DMA("sp", dbg[name], tmp, ["dbgtmp_" + name], [], "dbg_" + name)
                S.barrier()
                A.release("dbgtmp_" + name)

        cbf = A.alloc("cbf", [128, 4, 128], BF16, top=True)
        cf = A.alloc("cf", [128, 10, 128], F32, top=True)
        cc = A.alloc("cc", [128, 2], F32, top=True)
        DMA("sp", cbf, cst_bf, [], ["cbf"], "c0")
        DMA("sp", cf, cst_f, [], ["cf"], "c1")
        DMA("sp", cc, cst_c, [], ["cc"], "c2")
        identb, onesb, mprev, mnext = cbf[:, 0, :], cbf[:, 1, :], cbf[:, 2, :], cbf[:, 3, :]
        identf = cf[:, 0, :]
        small = A.alloc("small", [128, 64], F32, top=True)
        gbc = A.alloc("gbc", [128, D], F32, top=True)

        wq_rr = [0]

        def load_w(dst, src, key):
            DMA("pool", dst, src, [], [key], "w_" + key)

        def kview(w, c0, c1):
            return w.rearrange("(c p) n -> p c n", p=128)[:, :, c0:c1]

        stat_ctr = [0]

        def norm_T(tiles, gain_d, hT, hkey, tag):
            DMA("sp", gbc, gain_d.partition_broadcast(128), [], ["gbc"], "gbc")
            junk = A.alloc("junk_" + tag, [128, D], BF16)
            hb = [A.alloc("hb%d_%s" % (i, tag), [128, D], BF16) for i in range(4)]

            def ngen(t, xt, xk, loader):
                if loader is not None:
                    loader()
                sc = stat_ctr[0] % 32
                stat_ctr[0] += 1
                ssq = small[:, 2 * sc:2 * sc + 1]
                rs = small[:, 2 * sc + 1:2 * sc + 2]
                sk = "st%d" % sc
                ACT(lambda e: e.activation(out=junk, in_=xt, func=AF.Square, accum_out=ssq), xk, ["junk" + tag, sk])
                ACT(lambda e: e.activation(out=rs, in_=ssq, func=AF.Sqrt, scale=1.0 / D, bias=EPS), [sk], [sk + "r"])
                yield
                h = hb[t % 4]
                hk = "hb%d%s" % (t % 4, tag)
                DVE(lambda e: e.reciprocal(out=rs, in_=rs), [sk + "r"], [sk + "r"])
                DVE(lambda e: e.scalar_tensor_tensor(out=h, in0=xt, scalar=rs, in1=gbc, op0=ALU.mult, op1=ALU.mult), xk + [sk + "r", "gbc"], [hk])
                yield
                pb, pk = bank()
                pbv = pb.bitcast(BF16).rearrange("p (c n) -> p c n", c=8)
                for c in range(8):
                    tr(pbv[:, c, :], h[:, c * 128:(c + 1) * 128], identb, [hk, "cbf"], [pk])
                yield
                if t % 2 == 0:
                    ACT(lambda e: e.copy(out=hT[:, :, t * 128:(t + 1) * 128], in_=pbv), [pk], ["%s%d" % (hkey, t)])
                else:
                    DVE(lambda e: e.tensor_copy(out=hT[:, :, t * 128:(t + 1) * 128], in_=pbv), [pk], ["%s%d" % (hkey, t)])
                yield

            pipeline([ngen(t, xt, xk, ld) for t, (xt, xk, ld) in enumerate(tiles)], 4)
            S.barrier()
            A.release("junk_" + tag, "hb0_" + tag, "hb1_" + tag, "hb2_" + tag, "hb3_" + tag)

        def hkeys(hkey, n4):
            return ["%s%d" % (hkey, 4 * n4 + i) for i in range(4)]

        xbuf = [A.alloc("xb%d" % i, [128, D], F32) for i in range(4)]

        def x_tiles():
            out = []
            for t in range(NT):
                def loader(t=t):
                    DMA("sp", xbuf[t % 4], x_d[t * 128:(t + 1) * 128, :], [], ["xb%d" % (t % 4)], "xb%d" % (t % 4))
                out.append((xbuf[t % 4], ["xb%d" % (t % 4)], loader))
            return out

        def resid_epilogue(pb2, pk2, gkey_loaded, xin, xin_keys, xout, xout_keys, tmp, tmpk):
            sc = stat_ctr[0] % 32
            stat_ctr[0] += 1
            ssq = small[:, 2 * sc:2 * sc + 1]
            rs = small[:, 2 * sc + 1:2 * sc + 2]
            sk = "st%d" % sc
            ACT(lambda e: e.activation(out=tmp, in_=pb2, func=AF.Square, accum_out=ssq), pk2, [tmpk, sk])
            ACT(lambda e: e.activation(out=rs, in_=ssq, func=AF.Sqrt, scale=1.0 / D, bias=EPS), [sk], [sk + "r"])
            yield
            DVE(lambda e: e.reciprocal(out=rs, in_=rs), [sk + "r"], [sk + "r"])
            DVE(lambda e: e.scalar_tensor_tensor(out=tmp, in0=pb2, scalar=rs, in1=gbc, op0=ALU.mult, op1=ALU.mult), pk2 + [sk + "r", "gbc", tmpk], [tmpk])
            yield
            if sc % 2 == 0:
                POOL(lambda e: e.tensor_tensor(out=xout, in0=tmp, in1=xin, op=ALU.add), [tmpk] + xin_keys, xout_keys)
            else:
                DVE(lambda e: e.tensor_tensor(out=xout, in0=tmp, in1=xin, op=ALU.add), [tmpk] + xin_keys, xout_keys)
            yield

        hT = A.alloc("hT", [128, 8, S_LEN], BF16)
        wsl = [A.alloc("wsl%d" % i, [128, 8, 512], BF16) for i in range(2)]
        load_w(wsl[0], kview(w_inr, 0, 512), "wsl0")
        load_w(wsl[1], kview(w_inr, 512, 1024), "wsl1")
        cosT = A.alloc("cosT", [128, S_LEN], F32)
        sinT = A.alloc("sinT", [128, S_LEN], F32)
        posi = A.alloc("posi", [128, S_LEN], I32)
        ang = A.alloc("ang", [128, S_LEN], F32)
        rr = A.alloc("rr", [128, S_LEN], F32)
        kf = A.alloc("kf", [128, S_LEN], F32)
        DMA("sp", posi, pos_d.partition_broadcast(128), [], ["posi"], "posi")
        DVE(lambda e: e.tensor_copy(out=kf, in_=posi), ["posi"], ["kf"])
        DVE(lambda e: e.tensor_scalar(out=ang, in0=kf, scalar1=cc[:, 0:1], scalar2=None, op0=ALU.mult), ["kf", "cc"], ["ang"])
        for which, dst in (("sin", sinT), ("cos", cosT)):
            if which == "cos":
                DVE(lambda e: e.tensor_scalar(out=ang, in0=ang, scalar1=PI / 2, scalar2=None, op0=ALU.add), ["ang"], ["ang"])
            DVE(lambda e: e.tensor_scalar(out=posi, in0=ang, scalar1=1.0 / TWO_PI, scalar2=None, op0=ALU.mult), ["ang"], ["posi"])
            DVE(lambda e: e.tensor_copy(out=kf, in_=posi), ["posi"], ["kf"])
            DVE(lambda e: e.scalar_tensor_tensor(out=rr, in0=kf, scalar=-TWO_PI, in1=ang, op0=ALU.mult, op1=ALU.add), ["kf", "ang"], ["rr"])
            DVE(lambda e: e.tensor_scalar(out=kf, in0=rr, scalar1=PI, scalar2=TWO_PI, op0=ALU.is_gt, op1=ALU.mult), ["rr"], ["kf"])
            DVE(lambda e: e.tensor_tensor(out=rr, in0=rr, in1=kf, op=ALU.subtract), ["rr", "kf"], ["rr"])
            DVE(lambda e: e.tensor_scalar(out=rr, in0=rr, scalar1=-PI, scalar2=PI, op0=ALU.max, op1=ALU.min), ["rr"], ["rr"])
            if which == "sin":
                ACT(lambda e, d=dst: e.activation(out=d, in_=rr, func=AF.Sin, scale=cc[:, 1:2]), ["rr", "cc"], ["sinT"])
            else:
                ACT(lambda e, d=dst: e.activation(out=d, in_=rr, func=AF.Sin), ["rr"], ["cosT"])

        norm_T(x_tiles(), gains["g_mix_pre"], hT, "hT", "a")
        A.release("posi", "ang", "rr", "kf")
        A.release("xb0", "xb1", "xb2", "xb3")
        wctr = [0]

        def wslot():
            i = wctr[0] % 2
            wctr[0] += 1
            return wsl[i], "wsl%d" % i

        qT = A.alloc("qT", [128, 4, S_LEN], BF16)
        kT = A.alloc("kT", [128, S_LEN], BF16)
        ropeA = [A.alloc("ropeA%d" % i, [128, 512], F32) for i in range(2)]
        ropeB = [A.alloc("ropeB%d" % i, [128, 512], F32) for i in range(2)]
        rctr = [0]
        wq0, wq0k = wslot()
        wq1, wq1k = wslot()
        def rope_gen(wa, wak, ca, wb, wbk, cb, n4p, dst, dkey):
            n4s = (2 * n4p, 2 * n4p + 1)
            pas = [bank() for _ in n4s]
            pbs = [bank() for _ in n4s]
            for c in range(8):
                for i_, n4 in enumerate(n4s):
                    mm(pas[i_][0], wa[:, c, ca], hT[:, c, n4 * 512:(n4 + 1) * 512], c == 0, c == 7, [wak] + hkeys("hT", n4), [pas[i_][1]])
                for i_, n4 in enumerate(n4s):
                    mm(pbs[i_][0], wb[:, c, cb], hT[:, c, n4 * 512:(n4 + 1) * 512], c == 0, c == 7, [wbk] + hkeys("hT", n4), [pbs[i_][1]])
            yield
            for i_, n4 in enumerate(n4s):
                cs = slice(n4 * 512, (n4 + 1) * 512)
                DVE(lambda e: e.tensor_tensor(out=ropeA[i_], in0=pas[i_][0], in1=cosT[:, cs], op=ALU.mult), [pas[i_][1], "cosT"], ["ropeA%d" % i_])
                DVE(lambda e: e.tensor_tensor(out=ropeB[i_], in0=pbs[i_][0], in1=sinT[:, cs], op=ALU.mult), [pbs[i_][1], "sinT"], ["ropeB%d" % i_])
                POOL(lambda e: e.tensor_tensor(out=dst[:, cs], in0=ropeA[i_], in1=ropeB[i_], op=ALU.add), ["ropeA%d" % i_, "ropeB%d" % i_], [dkey])
            yield

        wk, wkk = wslot()
        gl_ = [rope_gen(wq0, wq0k, slice(j * 128, (j + 1) * 128), wq1, wq1k, slice(j * 128, (j + 1) * 128), n4p, qT[:, j, :], "qT") for j in range(4) for n4p in range(2)]
        pipeline(gl_, 2)
        load_w(wk[:, :, 0:256], kview(w_inr, 1024, 1280), wkk)
        gl_ = [rope_gen(wk, wkk, slice(0, 128), wk, wkk, slice(128, 256), n4p, kT, "kT") for n4p in range(2)]
        pipeline(gl_, 2)
        gate_s = A.alloc("gate_s", [128, NT, 512], BF16, top=True)
        vtokA = A.alloc("vtokA", [128, NT, 128], BF16)
        ab = A.alloc("ab", [128, NT, 16], F32, top=True)
        wt0, wt0k = wslot()
        load_w(wt0, kview(w_inr, 2816, 3328), wt0k)
        wt1, wt1k = wslot()
        load_w(wt1[:, :, 0:144], kview(w_inr, 3328, 3472), wt1k)
        def tokm_gen(t):
            pa, pak = bank()
            pb_, pbk = bank()
            ts_ = slice(t * 128, (t + 1) * 128)
            for c in range(8):
                mm(pa, hT[:, c, ts_], wt0[:, c, :], c == 0, c == 7, [wt0k, "hT%d" % t], [pak])
                mm(pb_[:, 0:144], hT[:, c, ts_], wt1[:, c, 0:144], c == 0, c == 7, [wt1k, "hT%d" % t], [pbk])
            yield
            ACT(lambda e: e.activation(out=gate_s[:, t, :], in_=pa, func=AF.Silu), [pak], ["gate_s"])
            DVE(lambda e: e.tensor_copy(out=vtokA[:, t, :], in_=pb_[:, 0:128]), [pbk], ["vtokA"])
            DVE(lambda e: e.tensor_copy(out=ab[:, t, :], in_=pb_[:, 128:144]), [pbk], ["ab"])
            yield

        pipeline([tokm_gen(t) for t in range(NT)], 2)
        S.barrier()
        A.release("cosT", "sinT", "ropeA0", "ropeA1", "ropeB0", "ropeB1")
        dump("qT", qT[:, 0, :], [])
        dump("kT", kT, [])

        attn_oT = A.alloc("attn_oT", [128, 4, S_LEN], BF16, top=True)
        sk_f = A.alloc("sk_f", [1, 8], F32)
        sinkrow = A.alloc("sinkrow", [1, 2, 512], BF16)
        sinkrow_lo = A.alloc("sinkrow_lo", [1, 2, 512], BF16)
        sk_t = A.alloc("sk_t", [1, 2, 512], F32)
        DMA("sp", sk_f, sink_d, [], ["sk_f"], "sk")
        ACT(lambda e: e.activation(out=sk_f, in_=sk_f, func=AF.Exp), ["sk_f"], ["sk_f"])
        for g in range(2):
            DVE(lambda e, g=g: e.tensor_copy(out=sk_t[:, g, :].rearrange("p (j q) -> p j q", j=4), in_=bc(sk_f[:, 4 * g:4 * g + 4].unsqueeze(2), [1, 4, 128])), ["sk_f"], ["sk_t"])
        DVE(lambda e: e.tensor_copy(out=sinkrow, in_=sk_t), ["sk_t"], ["sinkrow"])
        DVE(lambda e: e.tensor_tensor(out=sk_t, in0=sk_t, in1=sinkrow, op=ALU.subtract), ["sk_t", "sinkrow"], ["sk_t"])
        DVE(lambda e: e.tensor_copy(out=sinkrow_lo, in_=sk_t), ["sk_t"], ["sinkrow_lo"])
        Pt = [A.alloc("Pt%d" % i, [128, 512], BF16) for i in range(8)]
        rec = [A.alloc("rec%d" % i, [128, 512], F32) for i in range(2)]
        pctr = [0]
        def attn_gen(qb):
            pO, pOk = bank_at((qb % 2) * 2)
            pD, pDk = bank_at((qb % 2) * 2 + 1)
            qs_ = slice(qb * 128, (qb + 1) * 128)
            kbs = [kb for kb in (qb - 1, qb, qb + 1) if 0 <= kb < NT]
            items = [(ki, kb, len(kbs)) for ki, kb in enumerate(kbs)]
            prep = {}

            def stage1(it):
                ki, kb, nk = it
                Ps = []
                pss = []
                for g in range(2):
                    gs = slice(g * 64, (g + 1) * 64)
                    pS, pSk = bank_hi()
                    mm(pS.rearrange("p (j q) -> p j q", j=4), kT[gs, kb * 128:(kb + 1) * 128], qT[gs, :, qs_], True, True, ["kT", "qT"], [pSk])
                    pss.append((pS, pSk))
                for g in range(2):
                    pS, pSk = pss[g]
                    pi = pctr[0] % 8
                    pctr[0] += 1
                    P = Pt[pi]
                    Pk = "Pt%d" % pi
                    ACT(lambda e: e.activation(out=P, in_=pS, func=AF.Exp, scale=0.125), [pSk], [Pk])
                    if kb != qb:
                        m = mprev if kb < qb else mnext
                        POOL(lambda e: e.tensor_tensor(out=P.rearrange("p (j q) -> p j q", j=4), in0=P.rearrange("p (j q) -> p j q", j=4), in1=bc(m.unsqueeze(1), [128, 4, 128]), op=ALU.mult), [Pk, "cbf"], [Pk])
                    Ps.append((P, Pk))
                prep[it] = Ps

            def stage2(it):
                ki, kb, nk = it
                Ps = prep[it]
                for g in range(2):
                    gs = slice(g * 64, (g + 1) * 64)
                    mm(pO[gs, :], vtokA[:, kb, gs], Ps[g][0], ki == 0, ki == nk - 1, ["vtokA", Ps[g][1]], [pOk], tp=(0, g * 64))
                for g in range(2):
                    gs = slice(g * 64, (g + 1) * 64)
                    mm(pD[gs, :], onesb[:, 0:64], Ps[g][0], ki == 0, False, ["cbf", Ps[g][1]], [pDk], tp=(0, g * 64))
                if ki == nk - 1:
                    for g in range(2):
                        gs = slice(g * 64, (g + 1) * 64)
                        mm(pD[gs, :], onesb[0:1, 0:64], sinkrow[0:1, g, :], False, False, ["cbf", "sinkrow"], [pDk], tp=(0, g * 64))
                    for g in range(2):
                        gs = slice(g * 64, (g + 1) * 64)
                        mm(pD[gs, :], onesb[0:1, 0:64], sinkrow_lo[0:1, g, :], False, True, ["cbf", "sinkrow_lo"], [pDk], tp=(0, g * 64))

            stage1(items[0])
            for i, it in enumerate(items):
                if i + 1 < len(items):
                    stage1(items[i + 1])
                yield
                stage2(it)
            yield
            ri = qb % 2
            ACT(lambda e: e.activation(out=rec[ri], in_=pD, func=AF.Ln), [pDk], ["rec%d" % ri])
            ACT(lambda e: e.activation(out=rec[ri], in_=rec[ri], func=AF.Exp, scale=-1.0), ["rec%d" % ri], ["rec%d" % ri])
            DVE(lambda e: e.tensor_tensor(out=attn_oT[:, :, qs_], in0=pO.rearrange("p (j q) -> p j q", j=4), in1=rec[ri].rearrange("p (j q) -> p j q", j=4), op=ALU.mult), [pOk, "rec%d" % ri], ["attn_oT"])
            yield

        pipeline([attn_gen(qb) for qb in range(NT)], 2)
        S.barrier()
        A.release("qT", "kT", "vtokA", "Pt0", "Pt1", "Pt2", "Pt3", "Pt4", "Pt5", "Pt6", "Pt7", "rec0", "rec1", "sk_f", "sinkrow", "sinkrow_lo", "sk_t")
        dump("attn_oT", attn_oT[:, 0, :], [])

        cwt = A.alloc("cwt", [128, 12, 5], F32)
        DMA("sp", cwt, cw_d, [], ["cwt"], "cwt")
        kqT = A.alloc("kqT", [128, 4, NT, 2, 128], BF16, top=True)
        ktok = A.alloc("ktok", [128, NT, 4, 128], BF16, top=True)
        vtok = A.alloc("vtok", [128, NT, 4, 128], BF16, top=True)
        xbp = [A.alloc("xbp%d" % i, [128, S_LEN + 4], BF16) for i in range(2)]
        prb = [A.alloc("prb%d" % i, [128, S_LEN], BF16) for i in range(2)]
        sqb = [A.alloc("sqb%d" % i, [128, S_LEN], BF16) for i in range(2)]
        dwb = [A.alloc("dwb%d" % i, [128, 5, 128], BF16) for i in range(2)]
        rtmp = [A.alloc("rtmp%d" % i, [128, 512], F32) for i in range(2)]
        for i in range(2):
            POOL(lambda e: e.memset(xbp[i], 0.0), [], ["xbp%d" % i])
        wdn = {}
        for m0 in (0, 4, 8):
            wd_, wdk = wslot() if m0 < 8 else (None, None)
            if m0 < 8:
                load_w(wd_, kview(w_inr, 1280 + m0 * 128, 1280 + (m0 + 4) * 128), wdk)
                wdn[m0] = (wd_, wdk)

        def conv_gen(m):
            kind, h = m // 4, m % 4
            bi = m % 2
            if m == 8:
                wd_, wdk = wslot()
                load_w(wd_, kview(w_inr, 1280 + 8 * 128, 1280 + 12 * 128), wdk)
                wdn[8] = (wd_, wdk)
            wd_, wdk = wdn[(m // 4) * 4]
            xb_, xbk = xbp[bi], "xbp%d" % bi
            pr, prk = prb[bi], "prb%d" % bi
            sq, sqk = sqb[bi], "sqb%d" % bi
            dw, dwk = dwb[bi], "dwb%d" % bi
            rt, rtk = rtmp[bi], "rtmp%d" % bi
            POOL(lambda e: e.tensor_tensor(out=dw, in0=bc(identb.unsqueeze(1), [128, 5, 128]), in1=bc(cwt[:, m, :].unsqueeze(2), [128, 5, 128]), op=ALU.mult), ["cbf", "cwt"], [dwk])
            for n4 in range(4):
                pa, pak = bank()
                cs = slice(n4 * 512, (n4 + 1) * 512)
                for c in range(8):
                    mm(pa, wd_[:, c, (m % 4) * 128:(m % 4 + 1) * 128], hT[:, c, cs], c == 0, c == 7, [wdk] + hkeys("hT", n4), [pak])
                ACT(lambda e: e.copy(out=xb_[:, 2 + n4 * 512:2 + (n4 + 1) * 512], in_=pa), [pak], [xbk])
                if n4 % 2 == 1:
                    yield
            for n4 in range(4):
                pa, pak = bank()
                cs = slice(n4 * 512, (n4 + 1) * 512)
                for j in range(5):
                    mm(pa, dw[:, j, :], xb_[:, n4 * 512 + j:n4 * 512 + j + 512], j == 0, j == 4, [dwk, xbk], [pak])
                dsto = sq if kind == 2 else pr
                ACT(lambda e: e.activation(out=dsto[:, cs], in_=pa, func=AF.Silu), [pak], [sqk if kind == 2 else prk])
                if n4 % 2 == 1:
                    yield
            if kind < 2:
                POOL(lambda e: e.tensor_tensor(out=sq, in0=pr, in1=pr, op=ALU.mult), [prk], [sqk])
                yield
                kqi = 1 if kind == 0 else 0
                dk_ = ("qnT%d" if kind == 0 else "knT%d") % h
                sc_ = float(128 ** -0.5) if kind == 0 else 1.0
                for n4 in range(4):
                    cs = slice(n4 * 512, (n4 + 1) * 512)
                    pa, pak = bank()
                    mm(pa, onesb, sq[:, cs], True, True, ["cbf", sqk], [pak])
                    ACT(lambda e: e.activation(out=rt, in_=pa, func=AF.Ln, bias=EPS), [pak], [rtk])
                    ACT(lambda e: e.activation(out=rt, in_=rt, func=AF.Exp, scale=-0.5), [rtk], [rtk])
                    DVE(lambda e: e.scalar_tensor_tensor(out=kqT[:, h, 4 * n4:4 * n4 + 4, kqi, :], in0=pr[:, cs].rearrange("p (t n) -> p t n", t=4), scalar=sc_, in1=rt.rearrange("p (t n) -> p t n", t=4), op0=ALU.mult, op1=ALU.mult), [prk, rtk], [dk_])
                    yield
                srcf = (lambda t: kqT[:, h, t, 0, :])
                srck = dk_
            else:
                srcf = (lambda t: sq[:, t * 128:(t + 1) * 128])
                srck = sqk
            if kind >= 1:
                dtok = ktok if kind == 1 else vtok
                dtk = "ktok" if kind == 1 else "vtok"
                for half in range(2):
                    pa, pak = bank()
                    pv = pa.bitcast(BF16).rearrange("p (c n) -> p c n", c=8)
                    for c in range(8):
                        t = half * 8 + c
                        tr(pv[:, c, :], srcf(t), identb, [srck, "cbf"], [pak])
                    ACT(lambda e: e.copy(out=dtok[:, half * 8:(half + 1) * 8, h, :], in_=pv), [pak], [dtk])
                    yield

        pipeline([conv_gen(m) for m in range(12)], 2)
        S.barrier()
        A.release("hT", "xbp0", "xbp1", "prb0", "prb1", "sqb0", "sqb1", "dwb0", "dwb1", "rtmp0", "rtmp1", "wsl0", "wsl1")

        alb = A.alloc("alb", [128, 8], F32)
        dtb = A.alloc("dtb", [128, 8], F32)
        gdnb = A.alloc("gdnb", [128, 128], F32)
        DMA("sp", alb, alog_d.partition_broadcast(128), [], ["alb"], "alb")
        DMA("sp", dtb, dtb_d.partition_broadcast(128), [], ["dtb"], "dtb")
        DMA("sp", gdnb, gdn_d.partition_broadcast(128), [], ["gdnb"], "gdnb")
        g_ = A.alloc("g_", [128, NT, 8], F32)
        lnb = A.alloc("lnb", [128, NT, 8], F32)
        gc = A.alloc("gc", [128, NT, 8], F32)
        gtot = A.alloc("gtot", [128, NT, 8], F32)
        gl = A.alloc("gl", [128, 2, NT, 8], F32)
        tmp8 = A.alloc("tmp8", [128, NT, 8], F32)
        ghl = A.alloc("ghl", [128, 2, NT, 8], BF16)
        DVE(lambda e: e.tensor_tensor(out=g_, in0=ab[:, :, 0:8], in1=bc(dtb.unsqueeze(1), [128, NT, 8]), op=ALU.add), ["ab", "dtb"], ["g_"])
        ACT(lambda e: e.activation(out=g_, in_=g_, func=AF.Exp), ["g_"], ["g_"])
        ACT(lambda e: e.activation(out=g_, in_=g_, func=AF.Ln, bias=1.0), ["g_"], ["g_"])
        ACT(lambda e: e.activation(out=alb, in_=alb, func=AF.Exp), ["alb"], ["alb"])
        DVE(lambda e: e.scalar_tensor_tensor(out=g_, in0=g_, scalar=-1.0, in1=bc(alb.unsqueeze(1), [128, NT, 8]), op0=ALU.mult, op1=ALU.mult), ["g_", "alb"], ["g_"])
        ACT(lambda e: e.activation(out=lnb, in_=ab[:, :, 8:16], func=AF.Exp, scale=-1.0), ["ab"], ["lnb"])
        ACT(lambda e: e.activation(out=lnb, in_=lnb, func=AF.Ln, bias=1.0), ["lnb"], ["lnb"])
        DVE(lambda e: e.tensor_scalar(out=lnb, in0=lnb, scalar1=-1.0, scalar2=None, op0=ALU.mult), ["lnb"], ["lnb"])
        DVE(lambda e: e.tensor_copy(out=ghl[:, 0], in_=g_), ["g_"], ["ghl"])
        DVE(lambda e: e.tensor_tensor(out=tmp8, in0=g_, in1=ghl[:, 0], op=ALU.subtract), ["g_", "ghl"], ["tmp8"])
        DVE(lambda e: e.tensor_copy(out=ghl[:, 1], in_=tmp8), ["tmp8"], ["ghl"])
        cfb = A.alloc("cfb", [128, 5, 128], BF16)
        DVE(lambda e: e.tensor_copy(out=cfb, in_=cf[:, 5:10, :]), ["cf"], ["cfb"])
        pa, pak = bank()
        for s_ in range(2):
            mm(pa[:, 0:64].rearrange("p (t h) -> p t h", t=NT), cfb[:, 0, :], ghl[:, s_, :, 0:4], s_ == 0, s_ == 1, ["cfb", "ghl"], [pak])
        for s_ in range(2):
            mm(pa[:, 64:128].rearrange("p (t h) -> p t h", t=NT), cfb[:, 1, :], ghl[:, s_, :, 4:8], s_ == 0, s_ == 1, ["cfb", "ghl"], [pak])
        for k_, cidx in ((0, 2), (1, 3), (2, 4)):
            for s_ in range(2):
                mm(pa[:, 128 + 128 * k_:256 + 128 * k_].rearrange("p (t h) -> p t h", t=NT), cfb[:, cidx, :], ghl[:, s_, :, :], s_ == 0, s_ == 1, ["cfb", "ghl"], [pak])
        DVE(lambda e, p=pa: e.tensor_copy(out=gc[:, :, 0:4], in_=p[:, 0:64].rearrange("p (t h) -> p t h", t=NT)), [pak], ["gc"])
        DVE(lambda e, p=pa: e.tensor_copy(out=gc[:, :, 4:8], in_=p[:, 64:128].rearrange("p (t h) -> p t h", t=NT)), [pak], ["gc"])
        DVE(lambda e, p=pa: e.tensor_copy(out=gtot, in_=p[:, 128:256].rearrange("p (t h) -> p t h", t=NT)), [pak], ["gtot"])
        ACT(lambda e, p=pa: e.activation(out=gl, in_=p[:, 256:512].rearrange("p (a t h) -> p a t h", a=2, t=NT), func=AF.Exp), [pak], ["gl"])
        kgs = A.alloc("kgs", [128, NT, 8], F32)
        bgs = A.alloc("bgs", [128, NT, 8], F32)
        beta = A.alloc("beta", [128, NT, 8], F32)
        gcb = A.alloc("gcb", [128, NT, 8], F32)
        DVE(lambda e: e.tensor_tensor(out=kgs, in0=gtot, in1=gc, op=ALU.subtract), ["gtot", "gc"], ["kgs"])
        ACT(lambda e: e.activation(out=kgs, in_=kgs, func=AF.Exp), ["kgs"], ["kgs"])
        DVE(lambda e: e.tensor_tensor(out=gcb, in0=gc, in1=lnb, op=ALU.add), ["gc", "lnb"], ["gcb"])
        ACT(lambda e: e.activation(out=bgs, in_=gcb, func=AF.Exp), ["gcb"], ["bgs"])
        ACT(lambda e: e.activation(out=beta, in_=lnb, func=AF.Exp), ["lnb"], ["beta"])
        GGf = A.alloc("GGf", [128, 4, 2, 2 * NT], F32)
        GGh = A.alloc("GGh", [128, 4, 2, 2 * NT], BF16)
        GGl = A.alloc("GGl", [128, 4, 2, 2 * NT], BF16)
        for h in range(4):
            for d_ in range(2):
                gv = GGf[:, h, d_, :].rearrange("p (t k) -> p t k", k=2)
                DVE(lambda e: e.tensor_copy(out=gv[:, :, 0], in_=gc[:, :, 4 * d_ + h]), ["gc"], ["GGf"])
                DVE(lambda e: e.tensor_copy(out=gv[:, :, 1], in_=gcb[:, :, 4 * d_ + h]), ["gcb"], ["GGf"])
        DVE(lambda e: e.tensor_copy(out=GGh, in_=GGf), ["GGf"], ["GGh"])
        DVE(lambda e: e.tensor_tensor(out=GGf, in0=GGf, in1=GGh, op=ALU.subtract), ["GGf", "GGh"], ["GGf"])
        DVE(lambda e: e.tensor_copy(out=GGl, in_=GGf), ["GGf"], ["GGl"])
        mb2 = A.alloc("mb2", [128, 2, 4, 128], F32)
        for pr in range(2):
            for a_ in range(4):
                POOL(lambda e: e.tensor_copy(out=mb2[:, pr, a_, :], in_=cf[:, 1 + 2 * pr + (a_ % 2), :]), ["cf"], ["mb2"])
        S.barrier()
        A.release("g_", "lnb", "gtot", "tmp8", "ghl", "alb", "dtb", "GGf", "ab", "cfb", "cf")
        dump("gc", gc.rearrange("p t h -> p (t h)"), [])

        dn_oT = A.alloc("dn_oT", [128, 4, S_LEN], BF16, top=True)
        uuG = [A.alloc("uuG%d" % i, [128, 2, 2, 128], BF16) for i in range(3)]
        kgG = [A.alloc("kgG%d" % i, [128, 2, 2, 128], BF16) for i in range(3)]
        qkG = [A.alloc("qkG%d" % i, [128, 2, 2, 128], BF16) for i in range(3)]
        ctG = [A.alloc("ctG%d" % i, [128, 2, 2, 128], BF16) for i in range(3)]
        atG = [A.alloc("atG%d" % i, [128, 2, 4, 128], BF16) for i in range(3)]
        qgG = [A.alloc("qgG%d" % i, [128, 2, 2, 128], BF16) for i in range(2)]
        wtok = A.alloc("wtok", [128, 2, 2, 128], BF16)
        osum = [A.alloc("osum%d" % i, [128, 2, NT, 128], BF16) for i in range(2)]
        osf = A.alloc("osf", [128, NT, 128], F32)
        dno = A.alloc("dno", [128, NT, 128], BF16)
        dgh = A.alloc("dgh", [128, 2, 4, 128], BF16)
        dgl = A.alloc("dgl", [128, 2, 4, 128], BF16)
        Dm = A.alloc("Dm", [128, 2, 4, 128], F32)
        Em = A.alloc("Em", [128, 2, 4, 128], F32)
        egr = A.alloc("egr", [128, 2, 2, 128], F32)
        P0b = [A.alloc("P0b%d" % i, [128, 2, 2, 128], BF16) for i in range(2)]
        vkb = [A.alloc("vkb%d" % i, [128, 2, 2, 2, 128], BF16) for i in range(2)]
        PR = [A.alloc("PR%d" % i, [128, 4, 2, 128], BF16) for i in range(2)]
        PT_ = [A.alloc("PT_%d" % i, [128, 4, 128], BF16) for i in range(2)]
        Sb = [A.alloc("Sb%d" % d_, [128, 128], BF16) for d_ in range(2)]
        vn = [A.alloc("vn%d" % d_, [128, 128], BF16) for d_ in range(2)]
        ident3 = bc(identb.unsqueeze(1), [128, 4, 128])

        def pair_t0(gi, pr):
            return 2 * gi if pr == 0 else 14 - 2 * gi

        for _once in range(1):
            def Egen(h, gi):
                par = gi % 2
                g3 = (8 * h + gi) % 3
                for pr in range(2):
                    t0 = pair_t0(gi, pr)
                    dh = 4 * pr + h
                    tsl = slice(t0 * 128, (t0 + 2) * 128)
                    gk = "_%d_%d" % (pr, gi)
                    POOL(lambda e: e.tensor_tensor(out=dgh[:, pr], in0=ident3, in1=bc(GGh[:, h, pr, 2 * t0:2 * t0 + 4].unsqueeze(2), [128, 4, 128]), op=ALU.mult), ["cbf", "GGh"], ["dgh%d" % pr])
                    POOL(lambda e: e.tensor_tensor(out=dgl[:, pr], in0=ident3, in1=bc(GGl[:, h, pr, 2 * t0:2 * t0 + 4].unsqueeze(2), [128, 4, 128]), op=ALU.mult), ["cbf", "GGl"], ["dgl%d" % pr])
                    pg, pgk = bank()
                    mm(pg, onesb, dgh[:, pr].rearrange("p a b -> p (a b)"), True, False, ["cbf", "dgh%d" % pr], [pgk])
                    mm(pg, onesb, dgl[:, pr].rearrange("p a b -> p (a b)"), False, True, ["cbf", "dgl%d" % pr], [pgk])
                    pg4 = pg.rearrange("p (a b) -> p a b", a=4)
                    for tt in range(2):
                        DVE(lambda e: e.scalar_tensor_tensor(out=Dm[:, pr, 2 * tt:2 * tt + 2, :], in0=pg4[:, 2 * tt:2 * tt + 2, :], scalar=gc[:, t0 + tt, dh:dh + 1], in1=mb2[:, pr, 2 * tt:2 * tt + 2, :], op0=ALU.subtract, op1=ALU.add), [pgk, "gc", "mb2"], ["Dm%d" % pr])
                    ACT(lambda e: e.activation(out=egr[:, pr], in_=pg4.rearrange("p (t k) n -> p t k n", t=2)[:, :, 0, :], func=AF.Exp), [pgk], ["egr%d" % pr])
                    ACT(lambda e: e.activation(out=Em[:, pr], in_=Dm[:, pr], func=AF.Exp), ["Dm%d" % pr], ["Em%d" % pr])
                    yield
                    POOL(lambda e: e.tensor_tensor(out=qgG[par][:, pr], in0=kqT[:, h, t0:t0 + 2, 1, :], in1=egr[:, pr], op=ALU.mult), ["kqT", "egr%d" % pr], ["qgG%d_%d" % (par, pr)])
                    pG, pGk = bank()
                    for tt in range(2):
                        ts_ = slice((t0 + tt) * 128, (t0 + tt + 1) * 128)
                        mm(pG[:, tt * 256:(tt + 1) * 256], kqT[:, h, t0 + tt, 0, :], kqT[:, h, t0 + tt, :, :].rearrange("p a n -> p (a n)"), True, True, ["kqT"], [pGk])
                    pG4 = pG.rearrange("p (t k n) -> p t k n", t=2, k=2)
                    Em4 = Em[:, pr].rearrange("p (t k) n -> p t k n", t=2)
                    DVE(lambda e: e.scalar_tensor_tensor(out=P0b[par][:, pr], in0=pG4[:, :, 0, :], scalar=-1.0, in1=Em4[:, :, 1, :], op0=ALU.mult, op1=ALU.mult), [pGk, "Em%d" % pr], ["P0b%d_%d" % (par, pr)])
                    DVE(lambda e: e.tensor_tensor(out=qkG[g3][:, pr], in0=pG4[:, :, 1, :], in1=Em4[:, :, 0, :], op=ALU.mult), [pGk, "Em%d" % pr], ["qkG%d_%d" % (g3, pr)])
                    yield
                    POOL(lambda e: e.tensor_tensor(out=vkb[par][:, pr, :, 0, :], in0=vtok[:, t0:t0 + 2, h, :], in1=bc(beta[:, t0:t0 + 2, dh:dh + 1], [128, 2, 128]), op=ALU.mult), ["vtok", "beta"], ["vbb%d_%d" % (par, pr)])
                    POOL(lambda e: e.tensor_tensor(out=vkb[par][:, pr, :, 1, :], in0=ktok[:, t0:t0 + 2, h, :], in1=bc(bgs[:, t0:t0 + 2, dh:dh + 1], [128, 2, 128]), op=ALU.mult), ["ktok", "bgs"], ["kbb%d_%d" % (par, pr)])
                    POOL(lambda e: e.tensor_tensor(out=kgG[g3][:, pr], in0=ktok[:, t0:t0 + 2, h, :], in1=bc(kgs[:, t0:t0 + 2, dh:dh + 1], [128, 2, 128]), op=ALU.mult), ["ktok", "kgs"], ["kgG%d_%d" % (g3, pr)])
                    yield

            def Ngen(h, gi):
                par = gi % 2
                P0 = P0b[par].rearrange("p a b n -> p (a b) n")
                p0k = ["P0b%d_%d" % (par, pr) for pr in range(2)]
                for pr in range(2):
                    ms = slice(2 * pr, 2 * pr + 2)
                    pa, pak = bank()
                    pv = pa.bitcast(BF16).rearrange("p (c n) -> p c n", c=8)
                    for tt in range(2):
                        tr(pv[:, tt, :], P0[:, 2 * pr + tt, :], identb, [p0k[pr], "cbf"], [pak])
                    ACT(lambda e: e.copy(out=PT_[0][:, ms, :], in_=pv[:, 0:2, :]), [pak], ["PT0_%d" % pr])
                    POOL(lambda e: e.tensor_tensor(out=PR[1][:, ms, 1, :], in0=P0[:, ms, :], in1=ident3[:, 0:2, :], op=ALU.add), [p0k[pr], "cbf"], ["PR1r_%d" % pr])
                yield
                for pr in range(2):
                    ms = slice(2 * pr, 2 * pr + 2)
                    pa, pak = bank()
                    for tt in range(2):
                        m_ = 2 * pr + tt
                        mm(pa[:, tt * 128:(tt + 1) * 128], PT_[0][:, m_, :], P0[:, m_, :], True, True, [p0k[pr], "PT0_%d" % pr], [pak])
                    for tt in range(2):
                        m_ = 2 * pr + tt
                        mm(pa[:, 256 + tt * 128:256 + (tt + 1) * 128], P0[:, m_, :], PT_[0][:, m_, :], True, True, [p0k[pr], "PT0_%d" % pr], [pak])
                    pav = pa.rearrange("p (a t n) -> p a t n", a=2, t=2)
                    ACT(lambda e: e.copy(out=PR[1][:, ms, 0, :], in_=pav[:, 0]), [pak], ["PR1p_%d" % pr])
                    ACT(lambda e: e.copy(out=PT_[1][:, ms, :], in_=pav[:, 1]), [pak], ["PT1_%d" % pr])
                yield
                for k in range(1, 6):
                    ci, ni = k % 2, (k + 1) % 2
                    cur, nxt = PR[ci], PR[ni]
                    ptc = PT_[ci]
                    for pr in range(2):
                        ms = slice(2 * pr, 2 * pr + 2)
                        ptk = "PT%d_%d" % (ci, pr)
                        kp, kr = "PR%dp_%d" % (ci, pr), "PR%dr_%d" % (ci, pr)
                        np_, nr = "PR%dp_%d" % (ni, pr), "PR%dr_%d" % (ni, pr)
                        if k <= 3:
                            p2, p2k = bank()
                            for tt in range(2):
                                m_ = 2 * pr + tt
                                mm(p2[:, tt * 256:(tt + 1) * 256], ptc[:, m_, :], cur[:, m_, :, :].rearrange("p a n -> p (a n)"), True, True, [ptk, kp, kr], [p2k])
                            p2v = p2.rearrange("p (t a n) -> p t a n", t=2, a=2)
                            ACT(lambda e: e.copy(out=nxt[:, ms, 0, :], in_=p2v[:, :, 0, :]), [p2k], [np_])
                            DVE(lambda e: e.tensor_tensor(out=nxt[:, ms, 1, :], in0=p2v[:, :, 1, :], in1=cur[:, ms, 1, :], op=ALU.add), [p2k, kr], [nr])
                        else:
                            pc, pck = bank()
                            for tt in range(2):
                                m_ = 2 * pr + tt
                                mm(pc[:, tt * 128:(tt + 1) * 128], ptc[:, m_, :], cur[:, m_, 1, :], True, True, [ptk, kr], [pck])
                            DVE(lambda e: e.tensor_tensor(out=nxt[:, ms, 1, :], in0=pc[:, 0:256].rearrange("p (t n) -> p t n", t=2), in1=cur[:, ms, 1, :], op=ALU.add), [pck, kr], [nr])
                        if k <= 4:
                            pb_, pbk = bank()
                            for tt in range(2):
                                m_ = 2 * pr + tt
                                mm(pb_[:, tt * 128:(tt + 1) * 128], cur[:, m_, 0, :], ptc[:, m_, :], True, True, [ptk, kp], [pbk])
                            ACT(lambda e: e.copy(out=PT_[ni][:, ms, :], in_=pb_[:, 0:256].rearrange("p (t n) -> p t n", t=2)), [pbk], ["PT%d_%d" % (ni, pr)])
                    yield
                XR = PR[0]
                g3 = (8 * h + gi) % 3
                for pr in range(2):
                    t0 = pair_t0(gi, pr)
                    dh = 4 * pr + h
                    pu, puk = bank()
                    for tt in range(2):
                        mm(pu[:, tt * 256:(tt + 1) * 256], XR[:, pr * 2 + tt, 1, :], vkb[par][:, pr, tt, :, :].rearrange("p a n -> p (a n)"), True, True, ["PR0r_%d" % pr, "vbb%d_%d" % (par, pr), "kbb%d_%d" % (par, pr)], [puk])
                    puv = pu.rearrange("p (t a n) -> p t a n", t=2, a=2)
                    ACT(lambda e: e.copy(out=uuG[g3][:, pr], in_=puv[:, :, 0, :]), [puk], ["uuG%d_%d" % (g3, pr)])
                    DVE(lambda e: e.tensor_copy(out=wtok[:, pr], in_=puv[:, :, 1, :]), [puk], ["wtok%d" % pr])
                    yield
                    pAs = [bank(), bank()]
                    pC, pCk = bank()
                    for tt in range(2):
                        for cp in range(2):
                            cs_ = slice(cp * 64, cp * 64 + 64)
                            mm(pAs[cp][0][:, tt * 128:(tt + 1) * 128], wtok[cs_, pr, tt, :], kgG[g3][cs_, pr, tt, :], True, True, ["wtok%d" % pr, "kgG%d_%d" % (g3, pr)], [pAs[cp][1]], tp=(cp * 64, 0))
                        mm(pC[:, tt * 128:(tt + 1) * 128], wtok[:, pr, tt, :], qkG[g3][:, pr, tt, :], True, True, ["wtok%d" % pr, "qkG%d_%d" % (g3, pr)], [pCk])
                    for tt in range(2):
                        for cp in range(2):
                            DVE(lambda e: e.scalar_tensor_tensor(out=atG[g3][:, pr, tt * 2 + cp, :], in0=identb, scalar=gl[:, cp, t0 + tt, dh:dh + 1], in1=pAs[cp][0][:, tt * 128:(tt + 1) * 128], op0=ALU.mult, op1=ALU.subtract), [pAs[cp][1], "gl", "cbf"], ["atG%d_%d" % (g3, pr)])
                    DVE(lambda e: e.tensor_tensor(out=ctG[g3][:, pr], in0=qgG[par][:, pr], in1=pC[:, 0:256].rearrange("p (t n) -> p t n", t=2), op=ALU.subtract), [pCk, "qgG%d_%d" % (par, pr)], ["ctG%d_%d" % (g3, pr)])
                    yield

            def Sgen(h, gi):
                g3 = (8 * h + gi) % 3
                hp = h % 2
                if gi == 0:
                    for d_ in range(2):
                        POOL(lambda e: e.memset(Sb[d_], 0.0), [], ["Sb%d" % d_])
                for s_ in range(4):
                    step = 4 * gi + s_
                    info = []
                    for d_ in range(2):
                        n = step if d_ == 0 else 31 - step
                        t, cp = n // 2, n % 2
                        tt = t - pair_t0(gi, d_)
                        info.append(dict(d=d_, t=t, cp=cp, tt=tt, ps=slice(cp * 64, cp * 64 + 64), sbk="Sb%d" % d_, kq="%d_%d" % (g3, d_), pS=bank(), po=bank()))
                    for x_ in info:
                        mm(x_["pS"][0][:, 0:128], atG[g3][:, x_["d"], x_["tt"] * 2 + x_["cp"], :], Sb[x_["d"]], True, False, ["atG" + x_["kq"], x_["sbk"]], [x_["pS"][1]])
                    for x_ in info:
                        mm(x_["pS"][0][:, 0:128], kgG[g3][x_["ps"], x_["d"], x_["tt"], :], uuG[g3][x_["ps"], x_["d"], x_["tt"], :], False, True, ["kgG" + x_["kq"], "uuG" + x_["kq"]], [x_["pS"][1]], tp=(x_["cp"] * 64, 0))
                    for x_ in info:
                        mm(x_["po"][0][x_["ps"], 0:128], ctG[g3][:, x_["d"], x_["tt"], x_["ps"]], Sb[x_["d"]], True, False, ["ctG" + x_["kq"], x_["sbk"]], [x_["po"][1]], tp=(0, x_["cp"] * 64))
                    for x_ in info:
                        mm(x_["po"][0][x_["ps"], 0:128], qkG[g3][x_["ps"], x_["d"], x_["tt"], x_["ps"]], uuG[g3][x_["ps"], x_["d"], x_["tt"], :], False, True, ["qkG" + x_["kq"], "uuG" + x_["kq"]], [x_["po"][1]], tp=(x_["cp"] * 64, x_["cp"] * 64))
                    for x_ in info:
                        DVE(lambda e: e.tensor_copy(out=Sb[x_["d"]], in_=x_["pS"][0][:, 0:128]), [x_["pS"][1]], [x_["sbk"]])
                    for x_ in info:
                        ACT(lambda e: e.copy(out=osum[hp][x_["ps"], x_["d"], x_["t"], :], in_=x_["po"][0][x_["ps"], 0:128]), [x_["po"][1]], ["osum%d_%d_%d" % (hp, x_["d"], x_["t"])])
                    yield

            def drive(gens):
                gens = [g_ for g_ in gens if g_ is not None]
                while gens:
                    for g_ in list(gens):
                        try:
                            next(g_)
                        except StopIteration:
                            gens.remove(g_)

            def gate_gen(h):
                hp = h % 2
                okeys = ["osum%d_%d_%d" % (hp, d_, t) for d_ in range(2) for t in range(NT)]
                POOL(lambda e: e.tensor_tensor(out=osf, in0=osum[hp][:, 0], in1=osum[hp][:, 1], op=ALU.add), okeys, ["osf"])
                yield
                POOL(lambda e: e.tensor_tensor(out=dno, in0=osf, in1=osf, op=ALU.mult), ["osf"], ["dno"])
                yield
                orn = small[:, 0:NT]
                DVE(lambda e: e.tensor_reduce(out=orn, in_=dno, axis=AX.X, op=ALU.add), ["dno"], ["orn"])
                ACT(lambda e: e.activation(out=orn, in_=orn, func=AF.Sqrt, scale=1.0 / 128, bias=EPS), ["orn"], ["orn"])
                DVE(lambda e: e.reciprocal(out=orn, in_=orn), ["orn"], ["orn"])
                yield
                DVE(lambda e: e.tensor_tensor(out=osf, in0=osf, in1=bc(orn.unsqueeze(2), [128, NT, 128]), op=ALU.mult), ["orn", "osf"], ["osf"])
                yield
                POOL(lambda e: e.tensor_tensor(out=osf, in0=osf, in1=bc(gdnb.unsqueeze(1), [128, NT, 128]), op=ALU.mult), ["osf", "gdnb"], ["osf"])
                yield
                DVE(lambda e: e.tensor_tensor(out=dno, in0=osf, in1=gate_s[:, :, h * 128:(h + 1) * 128], op=ALU.mult), ["osf", "gate_s", "dno"], ["dno"])
                yield
                for half in range(2):
                    pa, pak = bank()
                    pv = pa.bitcast(BF16).rearrange("p (c n) -> p c n", c=8)
                    for c in range(8):
                        t = half * 8 + c
                        tr(pv[:, c, :], dno[:, t, :], identb, ["dno", "cbf"], [pak])
                    ACT(lambda e: e.copy(out=dn_oT[:, h, half * 1024:(half + 1) * 1024], in_=pv.rearrange("p c n -> p (c n)")), [pak], ["dn_oT"])
                    yield

            wo = A.view("kqT", [128, 8, D], BF16)
            gates = []
            for k_ in range(32 + 2):
                gens = []
                if k_ < 32:
                    gens.append(Egen(*divmod(k_, 8)))
                if k_ == 32:
                    for g in range(2):
                        load_w(wo[g * 64:(g + 1) * 64, 0:4, :], w_out[g * 256:(g + 1) * 256, :].rearrange("(j d) n -> d j n", d=64), "kqT")
                    load_w(wo[:, 4:8, :], w_out[512:1024, :].rearrange("(h p) n -> p h n", p=128), "kqT")
                if 0 <= k_ - 1 < 32:
                    gens.append(Ngen(*divmod(k_ - 1, 8)))
                if 0 <= k_ - 2 < 32:
                    gens.append(Sgen(*divmod(k_ - 2, 8)))
                gens += gates
                drive(gens)
                gates = []
                if k_ - 2 >= 0 and (k_ - 2) % 8 == 7:
                    gates = [gate_gen((k_ - 2) // 8)]
            drive(gates)
            S.barrier()
        A.release("ktok", "vtok", "uuG0", "uuG1", "uuG2", "kgG0", "kgG1", "kgG2", "qkG0", "qkG1", "qkG2", "ctG0", "ctG1", "ctG2", "atG0", "atG1", "atG2", "qgG0", "qgG1", "wtok", "osf", "dno", "osum0", "osum1", "dgh", "dgl", "Dm", "Em", "egr", "PR0", "PR1", "PT_0", "PT_1",
                  "P0b0", "P0b1", "vkb0", "vkb1", "mb2", "Sb0", "Sb1", "vn0", "vn1", "gate_s", "gc", "gl", "kgs", "bgs", "beta", "gcb", "GGh", "GGl",
                  "alb" if "alb" in A.live else "gdnb", "cwt")
        if "gdnb" in A.live:
            A.release("gdnb")

        dump("dn_oT", dn_oT[:, 0, :], [])
        x1 = A.alloc("x1", [128, NT, D], F32)
        DMA("sp", gbc, gains["g_mix_post"].partition_broadcast(128), [], ["gbc"], "gbc")
        NE = 4
        etmps = [A.alloc("etmp" if i == 0 else "etmp%d" % i, [128, D], F32, top=True) for i in range(NE)]
        etk = ["etmp%d" % i for i in range(NE)]
        xbuf = [A.alloc("xb%d" % i, [128, D], F32) for i in range(NE)]

        def oproj_gen(t):
            ts_ = slice(t * 128, (t + 1) * 128)
            DMA("sp", xbuf[t % NE], x_d[ts_, :], [], ["xb%d" % (t % NE)], "xb%d" % (t % NE))
            p2, p2k = bank2()
            for c in range(8):
                for hf in range(2):
                    lhs = attn_oT[:, c, ts_] if c < 4 else dn_oT[:, c - 4, ts_]
                    mm(p2[:, hf * 512:(hf + 1) * 512], lhs, wo[:, c, hf * 512:(hf + 1) * 512], c == 0, c == 7, ["attn_oT", "dn_oT"], [p2k[hf]])
            yield
            yield from resid_epilogue(p2, p2k, None, xbuf[t % NE], ["xb%d" % (t % NE)], x1[:, t, :], ["x1_%d" % t], etmps[t % NE], etk[t % NE])

        pipeline([oproj_gen(t) for t in range(NT)], NE)
        S.barrier()
        A.release("attn_oT", "dn_oT", "kqT")
        dump("x1", x1[:, 0, :], [])

        h2T = A.alloc("h2T", [128, 8, S_LEN], BF16)
        wsl = [A.alloc("wsm%d" % i, [128, 8, 512], BF16) for i in range(2)]
        pre_ckv = []
        for q4 in range(2):
            w_, wk_ = wslot()
            load_w(w_, kview(w_ckv, q4 * 512, (q4 + 1) * 512), wk_)
            pre_ckv.append((w_, wk_))
        memT = A.alloc("memT", [128, 8, 256], BF16)

        def mem_tiles():
            out = []
            for t in range(2):
                def loader(t=t):
                    DMA("sp", xbuf[t % 2], mem_d[t * 128:(t + 1) * 128, :], [], ["xb%d" % (t % 2)], "xb%d" % (t % 2))
                out.append((xbuf[t % 2], ["xb%d" % (t % 2)], loader))
            return out
        norm_T(mem_tiles(), gains["g_mem"], memT, "memT", "c")
        kcT = A.alloc("kcT", [128, 8, 256], BF16)
        vc = A.alloc("vc", [128, 2, D], BF16)
        for q4 in range(2):
            w_, wk_ = pre_ckv[q4]
            for nn in range(4):
                n = q4 * 4 + nn
                pa, pak = bank()
                for c in range(8):
                    mm(pa[:, 0:256], w_[:, c, nn * 128:(nn + 1) * 128], memT[:, c, :], c == 0, c == 7, [wk_, "memT0", "memT1"], [pak])
                ACT(lambda e, n=n, pa=pa: e.copy(out=kcT[:, n, :], in_=pa[:, 0:256]), [pak], ["kcT"])
        for q4 in range(2):
            w_, wk_ = wslot()
            load_w(w_, kview(w_ckv, D + q4 * 512, D + (q4 + 1) * 512), wk_)
            for mt in range(2):
                pa, pak = bank()
                for c in range(8):
                    mm(pa, memT[:, c, mt * 128:(mt + 1) * 128], w_[:, c, :], c == 0, c == 7, [wk_, "memT%d" % mt], [pak])
                DVE(lambda e, mt=mt, q4=q4, pa=pa: e.tensor_copy(out=vc[:, mt, q4 * 512:(q4 + 1) * 512], in_=pa), [pak], ["vc"])
        norm_T([(x1[:, t, :], ["x1_%d" % t], None) for t in range(NT)], gains["g_cross_pre"], h2T, "h2T", "b")
        qcT = A.alloc("qcT", [128, 8, S_LEN], BF16)
        for q4 in range(2):
            w_, wk_ = wslot()
            load_w(w_, kview(w_cq, q4 * 512, (q4 + 1) * 512), wk_)
            for nn in range(4):
                n = q4 * 4 + nn
                bks = [bank() for _ in range(4)]
                for c in range(8):
                    for n4 in range(4):
                        mm(bks[n4][0], w_[:, c, nn * 128:(nn + 1) * 128], h2T[:, c, n4 * 512:(n4 + 1) * 512], c == 0, c == 7, [wk_] + hkeys("h2T", n4), [bks[n4][1]])
                for n4 in range(4):
                    cs = slice(n4 * 512, (n4 + 1) * 512)
                    if (n + n4) % 2 == 0:
                        ACT(lambda e: e.copy(out=qcT[:, n, cs], in_=bks[n4][0]), [bks[n4][1]], ["qcT%d" % n])
                    else:
                        DVE(lambda e: e.tensor_copy(out=qcT[:, n, cs], in_=bks[n4][0]), [bks[n4][1]], ["qcT%d" % n])
        S.barrier()
        A.release("h2T", "memT", "wsm0", "wsm1")
        ocT = A.alloc("ocT", [128, 8, S_LEN], BF16)
        wco = A.alloc("wco", [128, 8, D], BF16)
        load_w(wco[:, :, 0:512], kview(w_co, 0, 512), "wco")
        load_w(wco[:, :, 512:1024], kview(w_co, 512, 1024), "wco")
        Pc = [A.alloc("Pc%d" % i, [128, 512], BF16) for i in range(4)]
        rec = [A.alloc("rcc%d" % i, [128, 512], F32) for i in range(2)]
        pcc = [0]
        def cross_gen(hh, n4):
            cs = slice(n4 * 512, (n4 + 1) * 512)
            Ps = []
            for mt in range(2):
                pS, pSk = bank()
                for dc in range(2):
                    mm(pS, kcT[:, 2 * hh + dc, mt * 128:(mt + 1) * 128], qcT[:, 2 * hh + dc, cs], dc == 0, dc == 1, ["kcT", "qcT%d" % (2 * hh + dc)], [pSk])
                pi = pcc[0] % 4
                pcc[0] += 1
                ACT(lambda e: e.activation(out=Pc[pi], in_=pS, func=AF.Exp, scale=1.0 / 16), [pSk], ["Pc%d" % pi])
                Ps.append((Pc[pi], "Pc%d" % pi))
            yield
            pD, pDk = bank()
            for mt in range(2):
                mm(pD, onesb, Ps[mt][0], mt == 0, mt == 1, ["cbf", Ps[mt][1]], [pDk])
            pOs = []
            for dc in range(2):
                pO, pOk = bank()
                for mt in range(2):
                    mm(pO, vc[:, mt, (2 * hh + dc) * 128:(2 * hh + dc + 1) * 128], Ps[mt][0], mt == 0, mt == 1, ["vc", Ps[mt][1]], [pOk])
                pOs.append((pO, pOk))
            yield
            ri = (hh * 4 + n4) % 2
            ACT(lambda e: e.activation(out=rec[ri], in_=pD, func=AF.Ln), [pDk], ["rcc%d" % ri])
            ACT(lambda e: e.activation(out=rec[ri], in_=rec[ri], func=AF.Exp, scale=-1.0), ["rcc%d" % ri], ["rcc%d" % ri])
            yield
            for dc in range(2):
                pO, pOk = pOs[dc]
                DVE(lambda e: e.tensor_tensor(out=ocT[:, 2 * hh + dc, cs], in0=pO, in1=rec[ri], op=ALU.mult), [pOk, "rcc%d" % ri], ["ocT"])
            yield

        pipeline([cross_gen(hh, n4) for hh in range(4) for n4 in range(4)], 2)
        S.barrier()
        A.release("kcT", "vc", "qcT", "Pc0", "Pc1", "Pc2", "Pc3", "rcc0", "rcc1")
        wd = A.alloc("wd", [128, NFF, D], BF16, top=True)
        wdv = w_dn.rearrange("(c p) n -> p c n", p=128)
        for i in range(0, NFF, 6):
            load_w(wd[:, i:min(i + 6, NFF), :], wdv[:, i:min(i + 6, NFF), :], "wd")
        DMA("sp", gbc, gains["g_cross_post"].partition_broadcast(128), [], ["gbc"], "gbc")
        def coproj_gen(t):
            ts_ = slice(t * 128, (t + 1) * 128)
            p2, p2k = bank2()
            for c in range(8):
                for hf in range(2):
                    mm(p2[:, hf * 512:(hf + 1) * 512], ocT[:, c, ts_], wco[:, c, hf * 512:(hf + 1) * 512], c == 0, c == 7, ["ocT", "wco"], [p2k[hf]])
            yield
            yield from resid_epilogue(p2, p2k, None, x1[:, t, :], ["x1_%d" % t], x1[:, t, :], ["x1_%d" % t], etmps[t % NE], etk[t % NE])

        pipeline([coproj_gen(t) for t in range(NT)], NE)
        S.barrier()
        A.release("ocT", "wco")
        dump("x2", x1[:, 0, :], [])

        A.release("xb0", "xb1", "xb2", "xb3")
        A.release("etmp1", "etmp2", "etmp3")
        etmps = [etmps[0], etmps[0]]
        etk = ["etmp0", "etmp0"]
        h3T = A.alloc("h3T", [128, 8, S_LEN], BF16)
        wg = [A.alloc("wg%d" % i, [128, 8, 2, 128], BF16, top=True) for i in range(2)]
        for i in range(2):
            load_w(wg[i].rearrange("p c a n -> p (c a n)"), w_gu[i], "wg%d" % i)
        norm_T([(x1[:, t, :], ["x1_%d" % t], None) for t in range(NT)], gains["g_ffn_pre"], h3T, "h3T", "d")
        DMA("sp", gbc, gains["g_ffn_post"].partition_broadcast(128), [], ["gbc"], "gbc")
        aT = A.alloc("aT", [128, NFF, 1024], BF16)
        sg = [A.alloc("sg%d" % i, [128, 512], BF16) for i in range(3)]
        wgc = [0]
        sgc = [0]
        for tg in range(2):
            def gu_gen(i, wi):
                banks = {}
                for nn in range(2):
                    banks[("g", nn)] = bank()
                    banks[("u", nn)] = bank()
                for c in range(8):
                    for gu, a_ in (("g", 0), ("u", 1)):
                        for nn in range(2):
                            n4 = tg * 2 + nn
                            cs = slice(n4 * 512, (n4 + 1) * 512)
                            pb_, pbk = banks[(gu, nn)]
                            mm(pb_, wg[wi][:, c, a_, :], h3T[:, c, cs], c == 0, c == 7, ["wg%d" % wi] + hkeys("h3T", n4), [pbk])
                yield
                for nn in range(2):
                    si = (2 * i + nn) % 3
                    ACT(lambda e: e.activation(out=sg[si], in_=banks[("g", nn)][0], func=AF.Silu), [banks[("g", nn)][1]], ["sg%d" % si])
                yield
                for nn in range(2):
                    si = (2 * i + nn) % 3
                    DVE(lambda e: e.tensor_tensor(out=aT[:, i, nn * 512:(nn + 1) * 512], in0=banks[("u", nn)][0], in1=sg[si], op=ALU.mult), [banks[("u", nn)][1], "sg%d" % si], ["aT%d" % nn])
                yield

            gl_ = []
            for i in range(NFF):
                wi = i % 2
                def ldgen(i=i, wi=wi):
                    load_w(wg[wi].rearrange("p c a n -> p (c a n)"), w_gu[i], "wg%d" % wi)
                    yield
                if not (tg == 0 and i < 2):
                    gl_.append(ldgen())
                gl_.append(gu_gen(i, wi))
            pipeline(gl_, 2)
            def down_gen(tt):
                t = tg * 8 + tt
                p2, p2k = bank2()
                for i in range(NFF):
                    for hf in range(2):
                        mm(p2[:, hf * 512:(hf + 1) * 512], aT[:, i, tt * 128:(tt + 1) * 128], wd[:, i, hf * 512:(hf + 1) * 512], i == 0, i == NFF - 1, ["aT%d" % (tt // 4), "wd"], [p2k[hf]])
                yield
                yield from resid_epilogue(p2, p2k, None, x1[:, t, :], ["x1_%d" % t], x1[:, t, :], ["x1_%d" % t], etmps[t % 2], etk[t % 2])
                DMA("sp", out_d[t * 128:(t + 1) * 128, :], x1[:, t, :], ["x1_%d" % t], [], "out")
                yield

            pipeline([down_gen(tt) for tt in range(8)], 1)
        stats = S.finalize(final_waits=["out"] + ["dbg_" + k for k in dbg])
    return nc, stats


def _consts():
    bf = ml_dtypes.bfloat16
    j = np.arange(128)[:, None]
    i = np.arange(128)[None, :]
    cst_bf = np.zeros((128, 4, 128), np.float32)
    cst_bf[:, 0] = np.eye(128)
    cst_bf[:, 1] = 1.0
    cst_bf[:, 2] = (i <= j)
    cst_bf[:, 3] = (j <= i)
    same = (j // 64) == (i // 64)
    NEG = -30000.0
    cst_f = np.zeros((128, 10, 128), np.float32)
    cst_f[:, 0] = np.eye(128)
    cst_f[:, 1] = np.where(same & (i >= j), 0.0, NEG)
    cst_f[:, 2] = np.where(same & (i > j), 0.0, NEG)
    cst_f[:, 3] = np.where(same & (i <= j), 0.0, NEG)
    cst_f[:, 4] = np.where(same & (i < j), 0.0, NEG)
    cst_f[:, 5] = same & (j <= i)
    cst_f[:, 6] = same & (j >= i)
    cst_f[:, 7] = same
    cst_f[:, 8] = (j < 64) & (i >= 0)
    cst_f[:, 9] = (j >= 64) & (i >= 0)
    d = np.arange(128) % 64
    inv = (10000.0 ** (-(d % 32).astype(np.float32) / np.float32(32))).astype(np.float32)
    sign = np.where(d < 32, -1.0, 1.0).astype(np.float32)
    cst_c = np.stack([inv, sign], 1).astype(np.float32)
    return cst_bf.astype(bf), cst_f, cst_c


_CACHE = {}


def _prep_weights(w_in, conv_w):
    w = np.asarray(w_in)[0]
    cols = []
    q0, k0, v0, dq0, dg0, da0, db0 = 0, 512, 640, 768, 2304, 2816, 2824

    def head(base, hd):
        return list(range(base + hd * 64, base + hd * 64 + 64))

    def swp(c):
        return c[32:] + c[:32]
    for j in range(4):
        cols += head(q0, j) + head(q0, 4 + j)
    for j in range(4):
        cols += swp(head(q0, j)) + swp(head(q0, 4 + j))
    cols += head(k0, 0) + head(k0, 1)
    cols += swp(head(k0, 0)) + swp(head(k0, 1))
    cols += list(range(dq0, dq0 + 1536))
    cols += list(range(dg0, dg0 + 512))
    cols += list(range(v0, v0 + 128))
    cols += list(range(da0, da0 + 8)) + list(range(db0, db0 + 8))
    assert len(cols) == 3472
    w_inr = np.ascontiguousarray(w[:, np.array(cols)])
    cw = np.ascontiguousarray(np.asarray(conv_w)[0].T.reshape(12, 128, 5).transpose(1, 0, 2))
    return w_inr, cw


def kernel(x, mem, positions, g_mix_pre, w_in, conv_w, a_log, dt_bias, g_dn_out, attn_sink,
           w_out, g_mix_post, g_cross_pre, g_mem, w_cq, w_ckv, w_co, g_cross_post,
           g_ffn_pre, w_gate_up, w_down, g_ffn_post):
    f = lambda a: np.ascontiguousarray(np.asarray(a, dtype=np.float32))
    if "nc" not in _CACHE:
        _CACHE["nc"] = build_program()
    nc, stats = _CACHE["nc"]
    cst_bf, cst_f, cst_c = _consts()
    w_inr, cw = _prep_weights(w_in, conv_w)
    shared = dict(
        w_inr=w_inr, cw=cw, alog=f(a_log).reshape(1, 8), dtb=f(dt_bias).reshape(1, 8), gdn=f(g_dn_out).reshape(1, 128),
        sink=f(attn_sink).reshape(1, 8), w_out=f(w_out)[0], w_cq=f(w_cq)[0], w_ckv=f(w_ckv)[0], w_co=f(w_co)[0],
        w_gu=np.ascontiguousarray(f(w_gate_up)[0].reshape(8, 128, 2, NFF, 128).transpose(3, 1, 0, 2, 4)).reshape(NFF, 128, 2048), w_dn=f(w_down)[0],
        g_mix_pre=f(g_mix_pre).reshape(1, D), g_mix_post=f(g_mix_post).reshape(1, D), g_cross_pre=f(g_cross_pre).reshape(1, D),
        g_mem=f(g_mem).reshape(1, D), g_cross_post=f(g_cross_post).reshape(1, D), g_ffn_pre=f(g_ffn_pre).reshape(1, D),
        g_ffn_post=f(g_ffn_post).reshape(1, D), cst_bf=cst_bf, cst_f=cst_f, cst_c=cst_c,
    )
    xs = f(x)
    ms = f(mem)
    ps = np.ascontiguousarray(np.asarray(positions).astype(np.int32))
    in_maps = []
    for b in range(8):
        m = dict(shared)
        m["x"] = xs[b]
        m["mem"] = ms[b]
        m["pos"] = ps[b:b + 1]
        in_maps.append(m)
    res = run_bass_kernel_spmd(nc, in_maps, core_ids=list(range(8)))
    _CACHE["res"] = res
    return np.stack([np.asarray(r["out"]) for r in res.results], 0).astype(np.float32)
```

```python
import numpy as np
import ml_dtypes
import concourse.bass as bass
import concourse.mybir as mybir
from concourse.bass_utils import run_bass_kernel_spmd
from contextlib import ExitStack

F32 = mybir.dt.float32
BF16 = mybir.dt.bfloat16
I32 = mybir.dt.int32
AF = mybir.ActivationFunctionType
ALU = mybir.AluOpType
AX = mybir.AxisListType

S_LEN = 2048
NT = 16
D = 1024
EPS = 1e-6
DFF = 2816
NFF = 22
PI = float(np.pi)
TWO_PI = float(2 * np.pi)
DEBUG = {}


class Ins:
    __slots__ = ("eng", "fn", "deps", "needed", "sem", "val", "is_dma")


class _Rec:
    def __init__(self):
        self.call = None

    def __getattr__(self, name):
        def f(*a, **k):
            self.call = (name, a, k)
            return self
        return f


class Sched:
    ENG = ("pe", "act", "dve", "pool", "sp")

    def __init__(self, nc, es):
        self.nc = nc
        self.es = es
        self.streams = {e: [] for e in self.ENG}
        self.last_w = {}
        self.readers = {}
        self.esem = {e: es.enter_context(nc.semaphore("s_" + e)) for e in self.ENG}
        self.dsem = {}
        self.dcount = {}
        self.last = {}

    def op(self, eng, fn, reads=(), writes=(), dma=None):
        ins = Ins()
        ins.eng = eng
        rec = _Rec()
        fn(rec)
        assert rec.call is not None
        ins.fn = rec.call
        ins.needed = False
        ins.is_dma = dma is not None
        px = [k for k in reads if k.startswith("pb")]
        if px:
            reads = [k for k in reads if not k.startswith("pb")]
            writes = list(writes) + [k for k in px if k not in writes]
        deps = []
        for k in reads:
            w = self.last_w.get(k)
            if w is not None:
                deps.append(w)
        strict = eng == "pool"
        for k in writes:
            w = self.last_w.get(k)
            if w is not None and (w.eng != eng or w.is_dma or ins.is_dma or strict):
                deps.append(w)
            for e, r in self.readers.get(k, {}).items():
                if e != eng or r.is_dma or ins.is_dma or strict:
                    deps.append(r)
        for d in deps:
            d.needed = True
        ins.deps = deps
        if ins.is_dma:
            if dma not in self.dsem:
                self.dsem[dma] = self.es.enter_context(self.nc.semaphore("d_" + dma))
                self.dcount[dma] = 0
            self.dcount[dma] += 16
            ins.sem = self.dsem[dma]
            ins.val = self.dcount[dma]
        else:
            ins.sem = self.esem[eng]
            ins.val = None
            self.last[eng] = ins
        for k in writes:
            self.last_w[k] = ins
            self.readers[k] = {}
        for k in reads:
            rk = self.readers.setdefault(k, {})
            rk[eng if not ins.is_dma else (eng, dma)] = ins
        self.streams[eng].append(ins)
        return ins

    def barrier(self):
        lasts = [i for i in self.last.values()]
        dm = []
        for name in self.dsem:
            d = Ins()
            d.eng = "sp"
            d.is_dma = True
            d.sem = self.dsem[name]
            d.val = self.dcount[name]
            d.needed = True
            dm.append(d)
        for i in lasts:
            i.needed = True
        for e in self.ENG:
            ins = Ins()
            ins.eng = e
            ins.fn = None
            ins.needed = False
            ins.is_dma = False
            ins.deps = list(lasts) + dm
            ins.sem = self.esem[e]
            ins.val = None
            self.streams[e].append(ins)
        self.last_w = {}
        self.readers = {}

    def finalize(self, final_waits=()):
        for e in self.ENG:
            c = 0
            for ins in self.streams[e]:
                if not ins.is_dma and ins.needed and ins.fn is not None:
                    c += 1
                    ins.val = c
                elif not ins.is_dma and ins.fn is None:
                    ins.val = c
        streams = self.streams
        stats = {}

        def emit(engobj, ename):
            seen = {}
            nw = 0
            for ins in streams[ename]:
                need = {}
                for d in ins.deps:
                    if d.val is None or d.val == 0:
                        continue
                    sid = id(d.sem)
                    if sid not in need or need[sid][1] < d.val:
                        need[sid] = (d.sem, d.val)
                for sid, (sem, val) in need.items():
                    if seen.get(sid, 0) < val:
                        engobj.wait_ge(sem, val)
                        seen[sid] = val
                        nw += 1
                if ins.fn is None:
                    continue
                nm_, a_, k_ = ins.fn
                bi = getattr(engobj, nm_)(*a_, **k_)
                if ins.is_dma:
                    bi.then_inc(ins.sem, 16)
                elif ins.needed:
                    bi.then_inc(ins.sem, 1)
            if ename == "sp":
                for name in final_waits:
                    engobj.wait_ge(self.dsem[name], self.dcount[name])
            stats[ename] = (len(streams[ename]), nw)

        with self.nc.Block() as block:
            @block.tensor
            def _(e):
                emit(e, "pe")

            @block.scalar
            def _(e):
                emit(e, "act")

            @block.vector
            def _(e):
                emit(e, "dve")

            @block.gpsimd
            def _(e):
                emit(e, "pool")

            @block.sync
            def _(e):
                emit(e, "sp")
        return stats


class Arena:
    def __init__(self, ap, nwords):
        self.ap = ap
        self.free = [(0, nwords * 4)]
        self.live = {}

    def alloc(self, name, shape, dt, top=False):
        esz = 2 if dt == BF16 else 4
        n = 1
        for s in shape[1:]:
            n *= s
        nbytes = ((n * esz + 63) // 64) * 64
        order = range(len(self.free) - 1, -1, -1) if top else range(len(self.free))
        for idx in order:
            off, sz = self.free[idx]
            if sz >= nbytes:
                if top:
                    self.free[idx] = (off, sz - nbytes)
                    off = off + sz - nbytes
                else:
                    self.free[idx] = (off + nbytes, sz - nbytes)
                if sz == nbytes:
                    del self.free[idx]
                break
        else:
            raise RuntimeError("arena OOM for %s (%d bytes); free=%s live=%s" % (name, nbytes, self.free, sorted((v[0], v[1], k) for k, v in self.live.items())))
        self.live[name] = (off, nbytes)
        v = self.ap[:, off // 4:(off + nbytes) // 4]
        if dt != F32:
            v = v.bitcast(dt)
        v = v[:, 0:n]
        if len(shape) == 3:
            v = v.rearrange("p (a b) -> p a b", a=shape[1])
        elif len(shape) == 4:
            v = v.rearrange("p (a b c) -> p a b c", a=shape[1], b=shape[2])
        elif len(shape) == 5:
            v = v.rearrange("p (a b c d) -> p a b c d", a=shape[1], b=shape[2], c=shape[3])
        if shape[0] < 128:
            v = v[0:shape[0]]
        return v

    def view(self, name, shape, dt):
        off, nb = self.live[name]
        esz = 2 if dt == BF16 else 4
        n = 1
        for s in shape[1:]:
            n *= s
        assert n * esz <= nb
        v = self.ap[:, off // 4:(off + nb) // 4]
        if dt != F32:
            v = v.bitcast(dt)
        v = v[:, 0:n]
        if len(shape) == 3:
            v = v.rearrange("p (a b) -> p a b", a=shape[1])
        return v

    def release(self, *names):
        for name in names:
            off, nb = self.live.pop(name)
            self.free.append((off, nb))
        self.free.sort()
        m = []
        for off, sz in self.free:
            if m and m[-1][0] + m[-1][1] == off:
                m[-1] = (m[-1][0], m[-1][1] + sz)
            else:
                m.append((off, sz))
        self.free = m


def bc(ap, shape):
    return ap.to_broadcast(list(shape))


def build_program():
    nc = bass.Bass("TRN2", target_bir_lowering=False)

    def din(name, shape, dt=F32):
        return nc.dram_tensor(name, list(shape), dt, kind="ExternalInput").ap()

    x_d = din("x", [S_LEN, D])
    mem_d = din("mem", [256, D])
    pos_d = din("pos", [1, S_LEN], I32)
    w_inr = din("w_inr", [D, 3472])
    cw_d = din("cw", [128, 12, 5])
    alog_d = din("alog", [1, 8])
    dtb_d = din("dtb", [1, 8])
    gdn_d = din("gdn", [1, 128])
    sink_d = din("sink", [1, 8])
    w_out = din("w_out", [D, D])
    w_cq = din("w_cq", [D, D])
    w_ckv = din("w_ckv", [D, 2 * D])
    w_co = din("w_co", [D, D])
    w_gu = din("w_gu", [NFF, 128, 2048])
    w_dn = din("w_dn", [DFF, D])
    gains = {k: din(k, [1, D]) for k in ("g_mix_pre", "g_mix_post", "g_cross_pre", "g_mem", "g_cross_post", "g_ffn_pre", "g_ffn_post")}
    cst_bf = din("cst_bf", [128, 4, 128], BF16)
    cst_f = din("cst_f", [128, 10, 128])
    cst_c = din("cst_c", [128, 2])
    out_d = nc.dram_tensor("out", [S_LEN, D], F32, kind="ExternalOutput").ap()
    dbg = {}
    for k, shp in DEBUG.items():
        dbg[k] = nc.dram_tensor("dbg_" + k, list(shp), F32, kind="ExternalOutput").ap()

    es = ExitStack()
    with es:
        S = Sched(nc, es)
        NW = 52900
        arena_t = es.enter_context(nc.sbuf_tensor("arena", [128, NW], F32))
        A = Arena(arena_t[:], NW)
        pbig = [es.enter_context(nc.psum_tensor("pb%d" % i, [128, 1024], F32)) for i in range(4)]
        bank_ctr = [0]

        def bank():
            i = bank_ctr[0] % 8
            bank_ctr[0] += 1
            return pbig[i // 2][:, (i % 2) * 512:(i % 2) * 512 + 512], "pb%d" % i

        def bank_at(i):
            return pbig[i // 2][:, (i % 2) * 512:(i % 2) * 512 + 512], "pb%d" % i

        rot4 = [0]

        def bank_hi():
            i = 4 + rot4[0] % 4
            rot4[0] += 1
            return bank_at(i)

        def bank2():
            if bank_ctr[0] % 2:
                bank_ctr[0] += 1
            i = bank_ctr[0] % 8
            bank_ctr[0] += 2
            return pbig[i // 2][:], ["pb%d" % i, "pb%d" % (i + 1)]

        def pipeline(gens, depth):
            gens = list(gens)
            active = []
            while gens or active:
                if gens and len(active) < depth:
                    active.append(gens.pop(0))
                for g_ in list(active):
                    try:
                        next(g_)
                    except StopIteration:
                        active.remove(g_)

        def PE(fn, r, w):
            S.op("pe", fn, reads=r, writes=w)

        def ACT(fn, r, w):
            S.op("act", fn, reads=r, writes=w)

        def DVE(fn, r, w):
            S.op("dve", fn, reads=r, writes=w)

        def POOL(fn, r, w):
            S.op("pool", fn, reads=r, writes=w)

        def DMA(q, out, in_, r, w, grp):
            S.op(q, lambda e, o=out, i=in_: e.dma_start(out=o, in_=i), reads=r, writes=w, dma=grp)

        def mm(out, lhsT, rhs, start, stop, r, w, tp=None):
            if tp is None:
                PE(lambda e, o=out, l=lhsT, rr=rhs, s=start, t=stop: e.matmul(out=o, lhsT=l, rhs=rr, start=s, stop=t), r, w)
            else:
                PE(lambda e, o=out, l=lhsT, rr=rhs, s=start, t=stop, tp=tp: e.matmul(out=o, lhsT=l, rhs=rr, start=s, stop=t, tile_position=tp), r, w)

        def tr(out, in_, ident, r, w):
            PE(lambda e, o=out, i=in_, d=ident: e.transpose(out=o, in_=i, identity=d), r, w)

        def dump(name, ap, keys, rows=128):
            if name in dbg:
                tmp = A.alloc("dbgtmp_" + name, list(ap.shape), F32)
                DVE(lambda e, o=tmp, i=ap: e.tensor_copy(out=o, in_=i), keys, ["dbgtmp_" + name])
                DMA("sp", dbg[name], tmp, ["dbgtmp_" + name], [], "dbg_" + name)
                S.barrier()
                A.release("dbgtmp_" + name)

        cbf = A.alloc("cbf", [128, 4, 128], BF16, top=True)
        cf = A.alloc("cf", [128, 10, 128], F32, top=True)
        cc = A.alloc("cc", [128, 2], F32, top=True)
        DMA("sp", cbf, cst_bf, [], ["cbf"], "c0")
        DMA("sp", cf, cst_f, [], ["cf"], "c1")
        DMA("sp", cc, cst_c, [], ["cc"], "c2")
        identb, onesb, mprev, mnext = cbf[:, 0, :], cbf[:, 1, :], cbf[:, 2, :], cbf[:, 3, :]
        identf = cf[:, 0, :]
        small = A.alloc("small", [128, 64], F32, top=True)
        gbc = A.alloc("gbc", [128, D], F32, top=True)

        wq_rr = [0]

        def load_w(dst, src, key):
            DMA("pool", dst, src, [], [key], "w_" + key)

        def kview(w, c0, c1):
            return w.rearrange("(c p) n -> p c n", p=128)[:, :, c0:c1]

        stat_ctr = [0]

        def norm_T(tiles, gain_d, hT, hkey, tag):
            DMA("sp", gbc, gain_d.partition_broadcast(128), [], ["gbc"], "gbc")
            junk = A.alloc("junk_" + tag, [128, D], BF16)
            hb = [A.alloc("hb%d_%s" % (i, tag), [128, D], BF16) for i in range(4)]

            def ngen(t, xt, xk, loader):
                if loader is not None:
                    loader()
                sc = stat_ctr[0] % 32
                stat_ctr[0] += 1
                ssq = small[:, 2 * sc:2 * sc + 1]
                rs = small[:, 2 * sc + 1:2 * sc + 2]
                sk = "st%d" % sc
                ACT(lambda e: e.activation(out=junk, in_=xt, func=AF.Square, accum_out=ssq), xk, ["junk" + tag, sk])
                ACT(lambda e: e.activation(out=rs, in_=ssq, func=AF.Sqrt, scale=1.0 / D, bias=EPS), [sk], [sk + "r"])
                yield
                h = hb[t % 4]
                hk = "hb%d%s" % (t % 4, tag)
                DVE(lambda e: e.reciprocal(out=rs, in_=rs), [sk + "r"], [sk + "r"])
                DVE(lambda e: e.scalar_tensor_tensor(out=h, in0=xt, scalar=rs, in1=gbc, op0=ALU.mult, op1=ALU.mult), xk + [sk + "r", "gbc"], [hk])
                yield
                pb, pk = bank()
                pbv = pb.bitcast(BF16).rearrange("p (c n) -> p c n", c=8)
                for c in range(8):
                    tr(pbv[:, c, :], h[:, c * 128:(c + 1) * 128], identb, [hk, "cbf"], [pk])
                yield
                if t % 2 == 0:
                    ACT(lambda e: e.copy(out=hT[:, :, t * 128:(t + 1) * 128], in_=pbv), [pk], ["%s%d" % (hkey, t)])
                else:
                    DVE(lambda e: e.tensor_copy(out=hT[:, :, t * 128:(t + 1) * 128], in_=pbv), [pk], ["%s%d" % (hkey, t)])
                yield

            pipeline([ngen(t, xt, xk, ld) for t, (xt, xk, ld) in enumerate(tiles)], 4)
            S.barrier()
            A.release("junk_" + tag, "hb0_" + tag, "hb1_" + tag, "hb2_" + tag, "hb3_" + tag)

        def hkeys(hkey, n4):
            return ["%s%d" % (hkey, 4 * n4 + i) for i in range(4)]

        xbuf = [A.alloc("xb%d" % i, [128, D], F32) for i in range(4)]

        def x_tiles():
            out = []
            for t in range(NT):
                def loader(t=t):
                    DMA("sp", xbuf[t % 4], x_d[t * 128:(t + 1) * 128, :], [], ["xb%d" % (t % 4)], "xb%d" % (t % 4))
                out.append((xbuf[t % 4], ["xb%d" % (t % 4)], loader))
            return out

        def resid_epilogue(pb2, pk2, gkey_loaded, xin, xin_keys, xout, xout_keys, tmp, tmpk):
            sc = stat_ctr[0] % 32
            stat_ctr[0] += 1
            ssq = small[:, 2 * sc:2 * sc + 1]
            rs = small[:, 2 * sc + 1:2 * sc + 2]
            sk = "st%d" % sc
            ACT(lambda e: e.activation(out=tmp, in_=pb2, func=AF.Square, accum_out=ssq), pk2, [tmpk, sk])
            ACT(lambda e: e.activation(out=rs, in_=ssq, func=AF.Sqrt, scale=1.0 / D, bias=EPS), [sk], [sk + "r"])
            yield
            DVE(lambda e: e.reciprocal(out=rs, in_=rs), [sk + "r"], [sk + "r"])
            DVE(lambda e: e.scalar_tensor_tensor(out=tmp, in0=pb2, scalar=rs, in1=gbc, op0=ALU.mult, op1=ALU.mult), pk2 + [sk + "r", "gbc", tmpk], [tmpk])
            yield
            if sc % 2 == 0:
                POOL(lambda e: e.tensor_tensor(out=xout, in0=tmp, in1=xin, op=ALU.add), [tmpk] + xin_keys, xout_keys)
            else:
                DVE(lambda e: e.tensor_tensor(out=xout, in0=tmp, in1=xin, op=ALU.add), [tmpk] + xin_keys, xout_keys)
            yield

        hT = A.alloc("hT", [128, 8, S_LEN], BF16)
        wsl = [A.alloc("wsl%d" % i, [128, 8, 512], BF16) for i in range(2)]
        load_w(wsl[0], kview(w_inr, 0, 512), "wsl0")
        load_w(wsl[1], kview(w_inr, 512, 1024), "wsl1")
        cosT = A.alloc("cosT", [128, S_LEN], F32)
        sinT = A.alloc("sinT", [128, S_LEN], F32)
        posi = A.alloc("posi", [128, S_LEN], I32)
        ang = A.alloc("ang", [128, S_LEN], F32)
        rr = A.alloc("rr", [128, S_LEN], F32)
        kf = A.alloc("kf", [128, S_LEN], F32)
        DMA("sp", posi, pos_d.partition_broadcast(128), [], ["posi"], "posi")
        DVE(lambda e: e.tensor_copy(out=kf, in_=posi), ["posi"], ["kf"])
        DVE(lambda e: e.tensor_scalar(out=ang, in0=kf, scalar1=cc[:, 0:1], scalar2=None, op0=ALU.mult), ["kf", "cc"], ["ang"])
        for which, dst in (("sin", sinT), ("cos", cosT)):
            if which == "cos":
                DVE(lambda e: e.tensor_scalar(out=ang, in0=ang, scalar1=PI / 2, scalar2=None, op0=ALU.add), ["ang"], ["ang"])
            DVE(lambda e: e.tensor_scalar(out=posi, in0=ang, scalar1=1.0 / TWO_PI, scalar2=None, op0=ALU.mult), ["ang"], ["posi"])
            DVE(lambda e: e.tensor_copy(out=kf, in_=posi), ["posi"], ["kf"])
            DVE(lambda e: e.scalar_tensor_tensor(out=rr, in0=kf, scalar=-TWO_PI, in1=ang, op0=ALU.mult, op1=ALU.add), ["kf", "ang"], ["rr"])
            DVE(lambda e: e.tensor_scalar(out=kf, in0=rr, scalar1=PI, scalar2=TWO_PI, op0=ALU.is_gt, op1=ALU.mult), ["rr"], ["kf"])
            DVE(lambda e: e.tensor_tensor(out=rr, in0=rr, in1=kf, op=ALU.subtract), ["rr", "kf"], ["rr"])
            DVE(lambda e: e.tensor_scalar(out=rr, in0=rr, scalar1=-PI, scalar2=PI, op0=ALU.max, op1=ALU.min), ["rr"], ["rr"])
            if which == "sin":
                ACT(lambda e, d=dst: e.activation(out=d, in_=rr, func=AF.Sin, scale=cc[:, 1:2]), ["rr", "cc"], ["sinT"])
            else:
                ACT(lambda e, d=dst: e.activation(out=d, in_=rr, func=AF.Sin), ["rr"], ["cosT"])

        norm_T(x_tiles(), gains["g_mix_pre"], hT, "hT", "a")
        A.release("posi", "ang", "rr", "kf")
        A.release("xb0", "xb1", "xb2", "xb3")
        wctr = [0]

        def wslot():
            i = wctr[0] % 2
            wctr[0] += 1
            return wsl[i], "wsl%d" % i

        qT = A.alloc("qT", [128, 4, S_LEN], BF16)
        kT = A.alloc("kT", [128, S_LEN], BF16)
        ropeA = [A.alloc("ropeA%d" % i, [128, 512], F32) for i in range(2)]
        ropeB = [A.alloc("ropeB%d" % i, [128, 512], F32) for i in range(2)]
        rctr = [0]
        wq0, wq0k = wslot()
        wq1, wq1k = wslot()
        def rope_gen(wa, wak, ca, wb, wbk, cb, n4p, dst, dkey):
            n4s = (2 * n4p, 2 * n4p + 1)
            pas = [bank() for _ in n4s]
            pbs = [bank() for _ in n4s]
            for c in range(8):
                for i_, n4 in enumerate(n4s):
                    mm(pas[i_][0], wa[:, c, ca], hT[:, c, n4 * 512:(n4 + 1) * 512], c == 0, c == 7, [wak] + hkeys("hT", n4), [pas[i_][1]])
                for i_, n4 in enumerate(n4s):
                    mm(pbs[i_][0], wb[:, c, cb], hT[:, c, n4 * 512:(n4 + 1) * 512], c == 0, c == 7, [wbk] + hkeys("hT", n4), [pbs[i_][1]])
            yield
            for i_, n4 in enumerate(n4s):
                cs = slice(n4 * 512, (n4 + 1) * 512)
                DVE(lambda e: e.tensor_tensor(out=ropeA[i_], in0=pas[i_][0], in1=cosT[:, cs], op=ALU.mult), [pas[i_][1], "cosT"], ["ropeA%d" % i_])
                DVE(lambda e: e.tensor_tensor(out=ropeB[i_], in0=pbs[i_][0], in1=sinT[:, cs], op=ALU.mult), [pbs[i_][1], "sinT"], ["ropeB%d" % i_])
                POOL(lambda e: e.tensor_tensor(out=dst[:, cs], in0=ropeA[i_], in1=ropeB[i_], op=ALU.add), ["ropeA%d" % i_, "ropeB%d" % i_], [dkey])
            yield

        wk, wkk = wslot()
        gl_ = [rope_gen(wq0, wq0k, slice(j * 128, (j + 1) * 128), wq1, wq1k, slice(j * 128, (j + 1) * 128), n4p, qT[:, j, :], "qT") for j in range(4) for n4p in range(2)]
        pipeline(gl_, 2)
        load_w(wk[:, :, 0:256], kview(w_inr, 1024, 1280), wkk)
        gl_ = [rope_gen(wk, wkk, slice(0, 128), wk, wkk, slice(128, 256), n4p, kT, "kT") for n4p in range(2)]
        pipeline(gl_, 2)
        gate_s = A.alloc("gate_s", [128, NT, 512], BF16, top=True)
        vtokA = A.alloc("vtokA", [128, NT, 128], BF16)
        ab = A.alloc("ab", [128, NT, 16], F32, top=True)
        wt0, wt0k = wslot()
        load_w(wt0, kview(w_inr, 2816, 3328), wt0k)
        wt1, wt1k = wslot()
        load_w(wt1[:, :, 0:144], kview(w_inr, 3328, 3472), wt1k)
        def tokm_gen(t):
            pa, pak = bank()
            pb_, pbk = bank()
            ts_ = slice(t * 128, (t + 1) * 128)
            for c in range(8):
                mm(pa, hT[:, c, ts_], wt0[:, c, :], c == 0, c == 7, [wt0k, "hT%d" % t], [pak])
                mm(pb_[:, 0:144], hT[:, c, ts_], wt1[:, c, 0:144], c == 0, c == 7, [wt1k, "hT%d" % t], [pbk])
            yield
            ACT(lambda e: e.activation(out=gate_s[:, t, :], in_=pa, func=AF.Silu), [pak], ["gate_s"])
            DVE(lambda e: e.tensor_copy(out=vtokA[:, t, :], in_=pb_[:, 0:128]), [pbk], ["vtokA"])
            DVE(lambda e: e.tensor_copy(out=ab[:, t, :], in_=pb_[:, 128:144]), [pbk], ["ab"])
            yield

        pipeline([tokm_gen(t) for t in range(NT)], 2)
        S.barrier()
        A.release("cosT", "sinT", "ropeA0", "ropeA1", "ropeB0", "ropeB1")
        dump("qT", qT[:, 0, :], [])
        dump("kT", kT, [])

        attn_oT = A.alloc("attn_oT", [128, 4, S_LEN], BF16, top=True)
        sk_f = A.alloc("sk_f", [1, 8], F32)
        sinkrow = A.alloc("sinkrow", [1, 2, 512], BF16)
        sinkrow_lo = A.alloc("sinkrow_lo", [1, 2, 512], BF16)
        sk_t = A.alloc("sk_t", [1, 2, 512], F32)
        DMA("sp", sk_f, sink_d, [], ["sk_f"], "sk")
        ACT(lambda e: e.activation(out=sk_f, in_=sk_f, func=AF.Exp), ["sk_f"], ["sk_f"])
        for g in range(2):
            DVE(lambda e, g=g: e.tensor_copy(out=sk_t[:, g, :].rearrange("p (j q) -> p j q", j=4), in_=bc(sk_f[:, 4 * g:4 * g + 4].unsqueeze(2), [1, 4, 128])), ["sk_f"], ["sk_t"])
        DVE(lambda e: e.tensor_copy(out=sinkrow, in_=sk_t), ["sk_t"], ["sinkrow"])
        DVE(lambda e: e.tensor_tensor(out=sk_t, in0=sk_t, in1=sinkrow, op=ALU.subtract), ["sk_t", "sinkrow"], ["sk_t"])
        DVE(lambda e: e.tensor_copy(out=sinkrow_lo, in_=sk_t), ["sk_t"], ["sinkrow_lo"])
        Pt = [A.alloc("Pt%d" % i, [128, 512], BF16) for i in range(8)]
        rec = [A.alloc("rec%d" % i, [128, 512], F32) for i in range(2)]
        pctr = [0]
        def attn_gen(qb):
            pO, pOk = bank_at((qb % 2) * 2)
            pD, pDk = bank_at((qb % 2) * 2 + 1)
            qs_ = slice(qb * 128, (qb + 1) * 128)
            kbs = [kb for kb in (qb - 1, qb, qb + 1) if 0 <= kb < NT]
            items = [(ki, kb, len(kbs)) for ki, kb in enumerate(kbs)]
            prep = {}

            def stage1(it):
                ki, kb, nk = it
                Ps = []
                pss = []
                for g in range(2):
                    gs = slice(g * 64, (g + 1) * 64)
                    pS, pSk = bank_hi()
                    mm(pS.rearrange("p (j q) -> p j q", j=4), kT[gs, kb * 128:(kb + 1) * 128], qT[gs, :, qs_], True, True, ["kT", "qT"], [pSk])
                    pss.append((pS, pSk))
                for g in range(2):
                    pS, pSk = pss[g]
                    pi = pctr[0] % 8
                    pctr[0] += 1
                    P = Pt[pi]
                    Pk = "Pt%d" % pi
                    ACT(lambda e: e.activation(out=P, in_=pS, func=AF.Exp, scale=0.125), [pSk], [Pk])
                    if kb != qb:
                        m = mprev if kb < qb else mnext
                        POOL(lambda e: e.tensor_tensor(out=P.rearrange("p (j q) -> p j q", j=4), in0=P.rearrange("p (j q) -> p j q", j=4), in1=bc(m.unsqueeze(1), [128, 4, 128]), op=ALU.mult), [Pk, "cbf"], [Pk])
                    Ps.append((P, Pk))
                prep[it] = Ps

            def stage2(it):
                ki, kb, nk = it
                Ps = prep[it]
                for g in range(2):
                    gs = slice(g * 64, (g + 1) * 64)
                    mm(pO[gs, :], vtokA[:, kb, gs], Ps[g][0], ki == 0, ki == nk - 1, ["vtokA", Ps[g][1]], [pOk], tp=(0, g * 64))
                for g in range(2):
                    gs = slice(g * 64, (g + 1) * 64)
                    mm(pD[gs, :], onesb[:, 0:64], Ps[g][0], ki == 0, False, ["cbf", Ps[g][1]], [pDk], tp=(0, g * 64))
                if ki == nk - 1:
                    for g in range(2):
                        gs = slice(g * 64, (g + 1) * 64)
                        mm(pD[gs, :], onesb[0:1, 0:64], sinkrow[0:1, g, :], False, False, ["cbf", "sinkrow"], [pDk], tp=(0, g * 64))
                    for g in range(2):
                        gs = slice(g * 64, (g + 1) * 64)
                        mm(pD[gs, :], onesb[0:1, 0:64], sinkrow_lo[0:1, g, :], False, True, ["cbf", "sinkrow_lo"], [pDk], tp=(0, g * 64))

            stage1(items[0])
            for i, it in enumerate(items):
                if i + 1 < len(items):
                    stage1(items[i + 1])
                yield
                stage2(it)
            yield
            ri = qb % 2
            ACT(lambda e: e.activation(out=rec[ri], in_=pD, func=AF.Ln), [pDk], ["rec%d" % ri])
            ACT(lambda e: e.activation(out=rec[ri], in_=rec[ri], func=AF.Exp, scale=-1.0), ["rec%d" % ri], ["rec%d" % ri])
            DVE(lambda e: e.tensor_tensor(out=attn_oT[:, :, qs_], in0=pO.rearrange("p (j q) -> p j q", j=4), in1=rec[ri].rearrange("p (j q) -> p j q", j=4), op=ALU.mult), [pOk, "rec%d" % ri], ["attn_oT"])
            yield

        pipeline([attn_gen(qb) for qb in range(NT)], 2)
        S.barrier()
        A.release("qT", "kT", "vtokA", "Pt0", "Pt1", "Pt2", "Pt3", "Pt4", "Pt5", "Pt6", "Pt7", "rec0", "rec1", "sk_f", "sinkrow", "sinkrow_lo", "sk_t")
        dump("attn_oT", attn_oT[:, 0, :], [])

        cwt = A.alloc("cwt", [128, 12, 5], F32)
        DMA("sp", cwt, cw_d, [], ["cwt"], "cwt")
        kqT = A.alloc("kqT", [128, 4, NT, 2, 128], BF16, top=True)
        ktok = A.alloc("ktok", [128, NT, 4, 128], BF16, top=True)
        vtok = A.alloc("vtok", [128, NT, 4, 128], BF16, top=True)
        xbp = [A.alloc("xbp%d" % i, [128, S_LEN + 4], BF16) for i in range(2)]
        prb = [A.alloc("prb%d" % i, [128, S_LEN], BF16) for i in range(2)]
        sqb = [A.alloc("sqb%d" % i, [128, S_LEN], BF16) for i in range(2)]
        dwb = [A.alloc("dwb%d" % i, [128, 5, 128], BF16) for i in range(2)]
        rtmp = [A.alloc("rtmp%d" % i, [128, 512], F32) for i in range(2)]
        for i in range(2):
            POOL(lambda e: e.memset(xbp[i], 0.0), [], ["xbp%d" % i])
        wdn = {}
        for m0 in (0, 4, 8):
            wd_, wdk = wslot() if m0 < 8 else (None, None)
            if m0 < 8:
                load_w(wd_, kview(w_inr, 1280 + m0 * 128, 1280 + (m0 + 4) * 128), wdk)
                wdn[m0] = (wd_, wdk)

        def conv_gen(m):
            kind, h = m // 4, m % 4
            bi = m % 2
            if m == 8:
                wd_, wdk = wslot()
                load_w(wd_, kview(w_inr, 1280 + 8 * 128, 1280 + 12 * 128), wdk)
                wdn[8] = (wd_, wdk)
            wd_, wdk = wdn[(m // 4) * 4]
            xb_, xbk = xbp[bi], "xbp%d" % bi
            pr, prk = prb[bi], "prb%d" % bi
            sq, sqk = sqb[bi], "sqb%d" % bi
            dw, dwk = dwb[bi], "dwb%d" % bi
            rt, rtk = rtmp[bi], "rtmp%d" % bi
            POOL(lambda e: e.tensor_tensor(out=dw, in0=bc(identb.unsqueeze(1), [128, 5, 128]), in1=bc(cwt[:, m, :].unsqueeze(2), [128, 5, 128]), op=ALU.mult), ["cbf", "cwt"], [dwk])
            for n4 in range(4):
                pa, pak = bank()
                cs = slice(n4 * 512, (n4 + 1) * 512)
                for c in range(8):
                    mm(pa, wd_[:, c, (m % 4) * 128:(m % 4 + 1) * 128], hT[:, c, cs], c == 0, c == 7, [wdk] + hkeys("hT", n4), [pak])
                ACT(lambda e: e.copy(out=xb_[:, 2 + n4 * 512:2 + (n4 + 1) * 512], in_=pa), [pak], [xbk])
                if n4 % 2 == 1:
                    yield
            for n4 in range(4):
                pa, pak = bank()
                cs = slice(n4 * 512, (n4 + 1) * 512)
                for j in range(5):
                    mm(pa, dw[:, j, :], xb_[:, n4 * 512 + j:n4 * 512 + j + 512], j == 0, j == 4, [dwk, xbk], [pak])
                dsto = sq if kind == 2 else pr
                ACT(lambda e: e.activation(out=dsto[:, cs], in_=pa, func=AF.Silu), [pak], [sqk if kind == 2 else prk])
                if n4 % 2 == 1:
                    yield
            if kind < 2:
                POOL(lambda e: e.tensor_tensor(out=sq, in0=pr, in1=pr, op=ALU.mult), [prk], [sqk])
                yield
                kqi = 1 if kind == 0 else 0
                dk_ = ("qnT%d" if kind == 0 else "knT%d") % h
                sc_ = float(128 ** -0.5) if kind == 0 else 1.0
                for n4 in range(4):
                    cs = slice(n4 * 512, (n4 + 1) * 512)
                    pa, pak = bank()
                    mm(pa, onesb, sq[:, cs], True, True, ["cbf", sqk], [pak])
                    ACT(lambda e: e.activation(out=rt, in_=pa, func=AF.Ln, bias=EPS), [pak], [rtk])
                    ACT(lambda e: e.activation(out=rt, in_=rt, func=AF.Exp, scale=-0.5), [rtk], [rtk])
                    DVE(lambda e: e.scalar_tensor_tensor(out=kqT[:, h, 4 * n4:4 * n4 + 4, kqi, :], in0=pr[:, cs].rearrange("p (t n) -> p t n", t=4), scalar=sc_, in1=rt.rearrange("p (t n) -> p t n", t=4), op0=ALU.mult, op1=ALU.mult), [prk, rtk], [dk_])
                    yield
                srcf = (lambda t: kqT[:, h, t, 0, :])
                srck = dk_
            else:
                srcf = (lambda t: sq[:, t * 128:(t + 1) * 128])
                srck = sqk
            if kind >= 1:
                dtok = ktok if kind == 1 else vtok
                dtk = "ktok" if kind == 1 else "vtok"
                for half in range(2):
                    pa, pak = bank()
                    pv = pa.bitcast(BF16).rearrange("p (c n) -> p c n", c=8)
                    for c in range(8):
                        t = half * 8 + c
                        tr(pv[:, c, :], srcf(t), identb, [srck, "cbf"], [pak])
                    ACT(lambda e: e.copy(out=dtok[:, half * 8:(half + 1) * 8, h, :], in_=pv), [pak], [dtk])
                    yield

        pipeline([conv_gen(m) for m in range(12)], 2)
        S.barrier()
        A.release("hT", "xbp0", "xbp1", "prb0", "prb1", "sqb0", "sqb1", "dwb0", "dwb1", "rtmp0", "rtmp1", "wsl0", "wsl1")

        alb = A.alloc("alb", [128, 8], F32)
        dtb = A.alloc("dtb", [128, 8], F32)
        gdnb = A.alloc("gdnb", [128, 128], F32)
        DMA("sp", alb, alog_d.partition_broadcast(128), [], ["alb"], "alb")
        DMA("sp", dtb, dtb_d.partition_broadcast(128), [], ["dtb"], "dtb")
        DMA("sp", gdnb, gdn_d.partition_broadcast(128), [], ["gdnb"], "gdnb")
        g_ = A.alloc("g_", [128, NT, 8], F32)
        lnb = A.alloc("lnb", [128, NT, 8], F32)
        gc = A.alloc("gc", [128, NT, 8], F32)
        gtot = A.alloc("gtot", [128, NT, 8], F32)
        gl = A.alloc("gl", [128, 2, NT, 8], F32)
        tmp8 = A.alloc("tmp8", [128, NT, 8], F32)
        ghl = A.alloc("ghl", [128, 2, NT, 8], BF16)
        DVE(lambda e: e.tensor_tensor(out=g_, in0=ab[:, :, 0:8], in1=bc(dtb.unsqueeze(1), [128, NT, 8]), op=ALU.add), ["ab", "dtb"], ["g_"])
        ACT(lambda e: e.activation(out=g_, in_=g_, func=AF.Exp), ["g_"], ["g_"])
        ACT(lambda e: e.activation(out=g_, in_=g_, func=AF.Ln, bias=1.0), ["g_"], ["g_"])
        ACT(lambda e: e.activation(out=alb, in_=alb, func=AF.Exp), ["alb"], ["alb"])
        DVE(lambda e: e.scalar_tensor_tensor(out=g_, in0=g_, scalar=-1.0, in1=bc(alb.unsqueeze(1), [128, NT, 8]), op0=ALU.mult, op1=ALU.mult), ["g_", "alb"], ["g_"])
        ACT(lambda e: e.activation(out=lnb, in_=ab[:, :, 8:16], func=AF.Exp, scale=-1.0), ["ab"], ["lnb"])
        ACT(lambda e: e.activation(out=lnb, in_=lnb, func=AF.Ln, bias=1.0), ["lnb"], ["lnb"])
        DVE(lambda e: e.tensor_scalar(out=lnb, in0=lnb, scalar1=-1.0, scalar2=None, op0=ALU.mult), ["lnb"], ["lnb"])
        DVE(lambda e: e.tensor_copy(out=ghl[:, 0], in_=g_), ["g_"], ["ghl"])
        DVE(lambda e: e.tensor_tensor(out=tmp8, in0=g_, in1=ghl[:, 0], op=ALU.subtract), ["g_", "ghl"], ["tmp8"])
        DVE(lambda e: e.tensor_copy(out=ghl[:, 1], in_=tmp8), ["tmp8"], ["ghl"])
        cfb = A.alloc("cfb", [128, 5, 128], BF16)
        DVE(lambda e: e.tensor_copy(out=cfb, in_=cf[:, 5:10, :]), ["cf"], ["cfb"])
        pa, pak = bank()
        for s_ in range(2):
            mm(pa[:, 0:64].rearrange("p (t h) -> p t h", t=NT), cfb[:, 0, :], ghl[:, s_, :, 0:4], s_ == 0, s_ == 1, ["cfb", "ghl"], [pak])
        for s_ in range(2):
            mm(pa[:, 64:128].rearrange("p (t h) -> p t h", t=NT), cfb[:, 1, :], ghl[:, s_, :, 4:8], s_ == 0, s_ == 1, ["cfb", "ghl"], [pak])
        for k_, cidx in ((0, 2), (1, 3), (2, 4)):
            for s_ in range(2):
                mm(pa[:, 128 + 128 * k_:256 + 128 * k_].rearrange("p (t h) -> p t h", t=NT), cfb[:, cidx, :], ghl[:, s_, :, :], s_ == 0, s_ == 1, ["cfb", "ghl"], [pak])
        DVE(lambda e, p=pa: e.tensor_copy(out=gc[:, :, 0:4], in_=p[:, 0:64].rearrange("p (t h) -> p t h", t=NT)), [pak], ["gc"])
        DVE(lambda e, p=pa: e.tensor_copy(out=gc[:, :, 4:8], in_=p[:, 64:128].rearrange("p (t h) -> p t h", t=NT)), [pak], ["gc"])
        DVE(lambda e, p=pa: e.tensor_copy(out=gtot, in_=p[:, 128:256].rearrange("p (t h) -> p t h", t=NT)), [pak], ["gtot"])
        ACT(lambda e, p=pa: e.activation(out=gl, in_=p[:, 256:512].rearrange("p (a t h) -> p a t h", a=2, t=NT), func=AF.Exp), [pak], ["gl"])
        kgs = A.alloc("kgs", [128, NT, 8], F32)
        bgs = A.alloc("bgs", [128, NT, 8], F32)
        beta = A.alloc("beta", [128, NT, 8], F32)
        gcb = A.alloc("gcb", [128, NT, 8], F32)
        DVE(lambda e: e.tensor_tensor(out=kgs, in0=gtot, in1=gc, op=ALU.subtract), ["gtot", "gc"], ["kgs"])
        ACT(lambda e: e.activation(out=kgs, in_=kgs, func=AF.Exp), ["kgs"], ["kgs"])
        DVE(lambda e: e.tensor_tensor(out=gcb, in0=gc, in1=lnb, op=ALU.add), ["gc", "lnb"], ["gcb"])
        ACT(lambda e: e.activation(out=bgs, in_=gcb, func=AF.Exp), ["gcb"], ["bgs"])
        ACT(lambda e: e.activation(out=beta, in_=lnb, func=AF.Exp), ["lnb"], ["beta"])
        GGf = A.alloc("GGf", [128, 4, 2, 2 * NT], F32)
        GGh = A.alloc("GGh", [128, 4, 2, 2 * NT], BF16)
        GGl = A.alloc("GGl", [128, 4, 2, 2 * NT], BF16)
        for h in range(4):
            for d_ in range(2):
                gv = GGf[:, h, d_, :].rearrange("p (t k) -> p t k", k=2)
                DVE(lambda e: e.tensor_copy(out=gv[:, :, 0], in_=gc[:, :, 4 * d_ + h]), ["gc"], ["GGf"])
                DVE(lambda e: e.tensor_copy(out=gv[:, :, 1], in_=gcb[:, :, 4 * d_ + h]), ["gcb"], ["GGf"])
        DVE(lambda e: e.tensor_copy(out=GGh, in_=GGf), ["GGf"], ["GGh"])
        DVE(lambda e: e.tensor_tensor(out=GGf, in0=GGf, in1=GGh, op=ALU.subtract), ["GGf", "GGh"], ["GGf"])
        DVE(lambda e: e.tensor_copy(out=GGl, in_=GGf), ["GGf"], ["GGl"])
        mb2 = A.alloc("mb2", [128, 2, 4, 128], F32)
        for pr in range(2):
            for a_ in range(4):
                POOL(lambda e: e.tensor_copy(out=mb2[:, pr, a_, :], in_=cf[:, 1 + 2 * pr + (a_ % 2), :]), ["cf"], ["mb2"])
        S.barrier()
        A.release("g_", "lnb", "gtot", "tmp8", "ghl", "alb", "dtb", "GGf", "ab", "cfb", "cf")
        dump("gc", gc.rearrange("p t h -> p (t h)"), [])

        dn_oT = A.alloc("dn_oT", [128, 4, S_LEN], BF16, top=True)
        uuG = [A.alloc("uuG%d" % i, [128, 2, 2, 128], BF16) for i in range(3)]
        kgG = [A.alloc("kgG%d" % i, [128, 2, 2, 128], BF16) for i in range(3)]
        qkG = [A.alloc("qkG%d" % i, [128, 2, 2, 128], BF16) for i in range(3)]
        ctG = [A.alloc("ctG%d" % i, [128, 2, 2, 128], BF16) for i in range(3)]
        atG = [A.alloc("atG%d" % i, [128, 2, 4, 128], BF16) for i in range(3)]
        qgG = [A.alloc("qgG%d" % i, [128, 2, 2, 128], BF16) for i in range(2)]
        wtok = A.alloc("wtok", [128, 2, 2, 128], BF16)
        osum = [A.alloc("osum%d" % i, [128, 2, NT, 128], BF16) for i in range(2)]
        osf = A.alloc("osf", [128, NT, 128], F32)
        dno = A.alloc("dno", [128, NT, 128], BF16)
        dgh = A.alloc("dgh", [128, 2, 4, 128], BF16)
        dgl = A.alloc("dgl", [128, 2, 4, 128], BF16)
        Dm = A.alloc("Dm", [128, 2, 4, 128], F32)
        Em = A.alloc("Em", [128, 2, 4, 128], F32)
        egr = A.alloc("egr", [128, 2, 2, 128], F32)
        P0b = [A.alloc("P0b%d" % i, [128, 2, 2, 128], BF16) for i in range(2)]
        vkb = [A.alloc("vkb%d" % i, [128, 2, 2, 2, 128], BF16) for i in range(2)]
        PR = [A.alloc("PR%d" % i, [128, 4, 2, 128], BF16) for i in range(2)]
        PT_ = [A.alloc("PT_%d" % i, [128, 4, 128], BF16) for i in range(2)]
        Sb = [A.alloc("Sb%d" % d_, [128, 128], BF16) for d_ in range(2)]
        vn = [A.alloc("vn%d" % d_, [128, 128], BF16) for d_ in range(2)]
        ident3 = bc(identb.unsqueeze(1), [128, 4, 128])

        def pair_t0(gi, pr):
            return 2 * gi if pr == 0 else 14 - 2 * gi

        for _once in range(1):
            def Egen(h, gi):
                par = gi % 2
                g3 = (8 * h + gi) % 3
                for pr in range(2):
                    t0 = pair_t0(gi, pr)
                    dh = 4 * pr + h
                    tsl = slice(t0 * 128, (t0 + 2) * 128)
                    gk = "_%d_%d" % (pr, gi)
                    POOL(lambda e: e.tensor_tensor(out=dgh[:, pr], in0=ident3, in1=bc(GGh[:, h, pr, 2 * t0:2 * t0 + 4].unsqueeze(2), [128, 4, 128]), op=ALU.mult), ["cbf", "GGh"], ["dgh%d" % pr])
                    POOL(lambda e: e.tensor_tensor(out=dgl[:, pr], in0=ident3, in1=bc(GGl[:, h, pr, 2 * t0:2 * t0 + 4].unsqueeze(2), [128, 4, 128]), op=ALU.mult), ["cbf", "GGl"], ["dgl%d" % pr])
                    pg, pgk = bank()
                    mm(pg, onesb, dgh[:, pr].rearrange("p a b -> p (a b)"), True, False, ["cbf", "dgh%d" % pr], [pgk])
                    mm(pg, onesb, dgl[:, pr].rearrange("p a b -> p (a b)"), False, True, ["cbf", "dgl%d" % pr], [pgk])
                    pg4 = pg.rearrange("p (a b) -> p a b", a=4)
                    for tt in range(2):
                        DVE(lambda e: e.scalar_tensor_tensor(out=Dm[:, pr, 2 * tt:2 * tt + 2, :], in0=pg4[:, 2 * tt:2 * tt + 2, :], scalar=gc[:, t0 + tt, dh:dh + 1], in1=mb2[:, pr, 2 * tt:2 * tt + 2, :], op0=ALU.subtract, op1=ALU.add), [pgk, "gc", "mb2"], ["Dm%d" % pr])
                    ACT(lambda e: e.activation(out=egr[:, pr], in_=pg4.rearrange("p (t k) n -> p t k n", t=2)[:, :, 0, :], func=AF.Exp), [pgk], ["egr%d" % pr])
                    ACT(lambda e: e.activation(out=Em[:, pr], in_=Dm[:, pr], func=AF.Exp), ["Dm%d" % pr], ["Em%d" % pr])
                    yield
                    POOL(lambda e: e.tensor_tensor(out=qgG[par][:, pr], in0=kqT[:, h, t0:t0 + 2, 1, :], in1=egr[:, pr], op=ALU.mult), ["kqT", "egr%d" % pr], ["qgG%d_%d" % (par, pr)])
                    pG, pGk = bank()
                    for tt in range(2):
                        ts_ = slice((t0 + tt) * 128, (t0 + tt + 1) * 128)
                        mm(pG[:, tt * 256:(tt + 1) * 256], kqT[:, h, t0 + tt, 0, :], kqT[:, h, t0 + tt, :, :].rearrange("p a n -> p (a n)"), True, True, ["kqT"], [pGk])
                    pG4 = pG.rearrange("p (t k n) -> p t k n", t=2, k=2)
                    Em4 = Em[:, pr].rearrange("p (t k) n -> p t k n", t=2)
                    DVE(lambda e: e.scalar_tensor_tensor(out=P0b[par][:, pr], in0=pG4[:, :, 0, :], scalar=-1.0, in1=Em4[:, :, 1, :], op0=ALU.mult, op1=ALU.mult), [pGk, "Em%d" % pr], ["P0b%d_%d" % (par, pr)])
                    DVE(lambda e: e.tensor_tensor(out=qkG[g3][:, pr], in0=pG4[:, :, 1, :], in1=Em4[:, :, 0, :], op=ALU.mult), [pGk, "Em%d" % pr], ["qkG%d_%d" % (g3, pr)])
                    yield
                    POOL(lambda e: e.tensor_tensor(out=vkb[par][:, pr, :, 0, :], in0=vtok[:, t0:t0 + 2, h, :], in1=bc(beta[:, t0:t0 + 2, dh:dh + 1], [128, 2, 128]), op=ALU.mult), ["vtok", "beta"], ["vbb%d_%d" % (par, pr)])
                    POOL(lambda e: e.tensor_tensor(out=vkb[par][:, pr, :, 1, :], in0=ktok[:, t0:t0 + 2, h, :], in1=bc(bgs[:, t0:t0 + 2, dh:dh + 1], [128, 2, 128]), op=ALU.mult), ["ktok", "bgs"], ["kbb%d_%d" % (par, pr)])
                    POOL(lambda e: e.tensor_tensor(out=kgG[g3][:, pr], in0=ktok[:, t0:t0 + 2, h, :], in1=bc(kgs[:, t0:t0 + 2, dh:dh + 1], [128, 2, 128]), op=ALU.mult), ["ktok", "kgs"], ["kgG%d_%d" % (g3, pr)])
                    yield

            def Ngen(h, gi):
                par = gi % 2
                P0 = P0b[par].rearrange("p a b n -> p (a b) n")
                p0k = ["P0b%d_%d" % (par, pr) for pr in range(2)]
                for pr in range(2):
                    ms = slice(2 * pr, 2 * pr + 2)
                    pa, pak = bank()
                    pv = pa.bitcast(BF16).rearrange("p (c n) -> p c n", c=8)
                    for tt in range(2):
                        tr(pv[:, tt, :], P0[:, 2 * pr + tt, :], identb, [p0k[pr], "cbf"], [pak])
                    ACT(lambda e: e.copy(out=PT_[0][:, ms, :], in_=pv[:, 0:2, :]), [pak], ["PT0_%d" % pr])
                    POOL(lambda e: e.tensor_tensor(out=PR[1][:, ms, 1, :], in0=P0[:, ms, :], in1=ident3[:, 0:2, :], op=ALU.add), [p0k[pr], "cbf"], ["PR1r_%d" % pr])
                yield
                for pr in range(2):
                    ms = slice(2 * pr, 2 * pr + 2)
                    pa, pak = bank()
                    for tt in range(2):
                        m_ = 2 * pr + tt
                        mm(pa[:, tt * 128:(tt + 1) * 128], PT_[0][:, m_, :], P0[:, m_, :], True, True, [p0k[pr], "PT0_%d" % pr], [pak])
                    for tt in range(2):
                        m_ = 2 * pr + tt
                        mm(pa[:, 256 + tt * 128:256 + (tt + 1) * 128], P0[:, m_, :], PT_[0][:, m_, :], True, True, [p0k[pr], "PT0_%d" % pr], [pak])
                    pav = pa.rearrange("p (a t n) -> p a t n", a=2, t=2)
                    ACT(lambda e: e.copy(out=PR[1][:, ms, 0, :], in_=pav[:, 0]), [pak], ["PR1p_%d" % pr])
                    ACT(lambda e: e.copy(out=PT_[1][:, ms, :], in_=pav[:, 1]), [pak], ["PT1_%d" % pr])
                yield
                for k in range(1, 6):
                    ci, ni = k % 2, (k + 1) % 2
                    cur, nxt = PR[ci], PR[ni]
                    ptc = PT_[ci]
                    for pr in range(2):
                        ms = slice(2 * pr, 2 * pr + 2)
                        ptk = "PT%d_%d" % (ci, pr)
                        kp, kr = "PR%dp_%d" % (ci, pr), "PR%dr_%d" % (ci, pr)
                        np_, nr = "PR%dp_%d" % (ni, pr), "PR%dr_%d" % (ni, pr)
                        if k <= 3:
                            p2, p2k = bank()
                            for tt in range(2):
                                m_ = 2 * pr + tt
                                mm(p2[:, tt * 256:(tt + 1) * 256], ptc[:, m_, :], cur[:, m_, :, :].rearrange("p a n -> p (a n)"), True, True, [ptk, kp, kr], [p2k])
                            p2v = p2.rearrange("p (t a n) -> p t a n", t=2, a=2)
                            ACT(lambda e: e.copy(out=nxt[:, ms, 0, :], in_=p2v[:, :, 0, :]), [p2k], [np_])
                            DVE(lambda e: e.tensor_tensor(out=nxt[:, ms, 1, :], in0=p2v[:, :, 1, :], in1=cur[:, ms, 1, :], op=ALU.add), [p2k, kr], [nr])
                        else:
                            pc, pck = bank()
                            for tt in range(2):
                                m_ = 2 * pr + tt
                                mm(pc[:, tt * 128:(tt + 1) * 128], ptc[:, m_, :], cur[:, m_, 1, :], True, True, [ptk, kr], [pck])
                            DVE(lambda e: e.tensor_tensor(out=nxt[:, ms, 1, :], in0=pc[:, 0:256].rearrange("p (t n) -> p t n", t=2), in1=cur[:, ms, 1, :], op=ALU.add), [pck, kr], [nr])
                        if k <= 4:
                            pb_, pbk = bank()
                            for tt in range(2):
                                m_ = 2 * pr + tt
                                mm(pb_[:, tt * 128:(tt + 1) * 128], cur[:, m_, 0, :], ptc[:, m_, :], True, True, [ptk, kp], [pbk])
                            ACT(lambda e: e.copy(out=PT_[ni][:, ms, :], in_=pb_[:, 0:256].rearrange("p (t n) -> p t n", t=2)), [pbk], ["PT%d_%d" % (ni, pr)])
                    yield
                XR = PR[0]
                g3 = (8 * h + gi) % 3
                for pr in range(2):
                    t0 = pair_t0(gi, pr)
                    dh = 4 * pr + h
                    pu, puk = bank()
                    for tt in range(2):
                        mm(pu[:, tt * 256:(tt + 1) * 256], XR[:, pr * 2 + tt, 1, :], vkb[par][:, pr, tt, :, :].rearrange("p a n -> p (a n)"), True, True, ["PR0r_%d" % pr, "vbb%d_%d" % (par, pr), "kbb%d_%d" % (par, pr)], [puk])
                    puv = pu.rearrange("p (t a n) -> p t a n", t=2, a=2)
                    ACT(lambda e: e.copy(out=uuG[g3][:, pr], in_=puv[:, :, 0, :]), [puk], ["uuG%d_%d" % (g3, pr)])
                    DVE(lambda e: e.tensor_copy(out=wtok[:, pr], in_=puv[:, :, 1, :]), [puk], ["wtok%d" % pr])
                    yield
                    pAs = [bank(), bank()]
                    pC, pCk = bank()
                    for tt in range(2):
                        for cp in range(2):
                            cs_ = slice(cp * 64, cp * 64 + 64)
                            mm(pAs[cp][0][:, tt * 128:(tt + 1) * 128], wtok[cs_, pr, tt, :], kgG[g3][cs_, pr, tt, :], True, True, ["wtok%d" % pr, "kgG%d_%d" % (g3, pr)], [pAs[cp][1]], tp=(cp * 64, 0))
                        mm(pC[:, tt * 128:(tt + 1) * 128], wtok[:, pr, tt, :], qkG[g3][:, pr, tt, :], True, True, ["wtok%d" % pr, "qkG%d_%d" % (g3, pr)], [pCk])
                    for tt in range(2):
                        for cp in range(2):
                            DVE(lambda e: e.scalar_tensor_tensor(out=atG[g3][:, pr, tt * 2 + cp, :], in0=identb, scalar=gl[:, cp, t0 + tt, dh:dh + 1], in1=pAs[cp][0][:, tt * 128:(tt + 1) * 128], op0=ALU.mult, op1=ALU.subtract), [pAs[cp][1], "gl", "cbf"], ["atG%d_%d" % (g3, pr)])
                    DVE(lambda e: e.tensor_tensor(out=ctG[g3][:, pr], in0=qgG[par][:, pr], in1=pC[:, 0:256].rearrange("p (t n) -> p t n", t=2), op=ALU.subtract), [pCk, "qgG%d_%d" % (par, pr)], ["ctG%d_%d" % (g3, pr)])
                    yield

            def Sgen(h, gi):
                g3 = (8 * h + gi) % 3
                hp = h % 2
                if gi == 0:
                    for d_ in range(2):
                        POOL(lambda e: e.memset(Sb[d_], 0.0), [], ["Sb%d" % d_])
                for s_ in range(4):
                    step = 4 * gi + s_
                    info = []
                    for d_ in range(2):
                        n = step if d_ == 0 else 31 - step
                        t, cp = n // 2, n % 2
                        tt = t - pair_t0(gi, d_)
                        info.append(dict(d=d_, t=t, cp=cp, tt=tt, ps=slice(cp * 64, cp * 64 + 64), sbk="Sb%d" % d_, kq="%d_%d" % (g3, d_), pS=bank(), po=bank()))
                    for x_ in info:
                        mm(x_["pS"][0][:, 0:128], atG[g3][:, x_["d"], x_["tt"] * 2 + x_["cp"], :], Sb[x_["d"]], True, False, ["atG" + x_["kq"], x_["sbk"]], [x_["pS"][1]])
                    for x_ in info:
                        mm(x_["pS"][0][:, 0:128], kgG[g3][x_["ps"], x_["d"], x_["tt"], :], uuG[g3][x_["ps"], x_["d"], x_["tt"], :], False, True, ["kgG" + x_["kq"], "uuG" + x_["kq"]], [x_["pS"][1]], tp=(x_["cp"] * 64, 0))
                    for x_ in info:
                        mm(x_["po"][0][x_["ps"], 0:128], ctG[g3][:, x_["d"], x_["tt"], x_["ps"]], Sb[x_["d"]], True, False, ["ctG" + x_["kq"], x_["sbk"]], [x_["po"][1]], tp=(0, x_["cp"] * 64))
                    for x_ in info:
                        mm(x_["po"][0][x_["ps"], 0:128], qkG[g3][x_["ps"], x_["d"], x_["tt"], x_["ps"]], uuG[g3][x_["ps"], x_["d"], x_["tt"], :], False, True, ["qkG" + x_["kq"], "uuG" + x_["kq"]], [x_["po"][1]], tp=(x_["cp"] * 64, x_["cp"] * 64))
                    for x_ in info:
                        DVE(lambda e: e.tensor_copy(out=Sb[x_["d"]], in_=x_["pS"][0][:, 0:128]), [x_["pS"][1]], [x_["sbk"]])
                    for x_ in info:
                        ACT(lambda e: e.copy(out=osum[hp][x_["ps"], x_["d"], x_["t"], :], in_=x_["po"][0][x_["ps"], 0:128]), [x_["po"][1]], ["osum%d_%d_%d" % (hp, x_["d"], x_["t"])])
                    yield

            def drive(gens):
                gens = [g_ for g_ in gens if g_ is not None]
                while gens:
                    for g_ in list(gens):
                        try:
                            next(g_)
                        except StopIteration:
                            gens.remove(g_)

            def gate_gen(h):
                hp = h % 2
                okeys = ["osum%d_%d_%d" % (hp, d_, t) for d_ in range(2) for t in range(NT)]
                POOL(lambda e: e.tensor_tensor(out=osf, in0=osum[hp][:, 0], in1=osum[hp][:, 1], op=ALU.add), okeys, ["osf"])
                yield
                POOL(lambda e: e.tensor_tensor(out=dno, in0=osf, in1=osf, op=ALU.mult), ["osf"], ["dno"])
                yield
                orn = small[:, 0:NT]
                DVE(lambda e: e.tensor_reduce(out=orn, in_=dno, axis=AX.X, op=ALU.add), ["dno"], ["orn"])
                ACT(lambda e: e.activation(out=orn, in_=orn, func=AF.Sqrt, scale=1.0 / 128, bias=EPS), ["orn"], ["orn"])
                DVE(lambda e: e.reciprocal(out=orn, in_=orn), ["orn"], ["orn"])
                yield
                DVE(lambda e: e.tensor_tensor(out=osf, in0=osf, in1=bc(orn.unsqueeze(2), [128, NT, 128]), op=ALU.mult), ["orn", "osf"], ["osf"])
                yield
                POOL(lambda e: e.tensor_tensor(out=osf, in0=osf, in1=bc(gdnb.unsqueeze(1), [128, NT, 128]), op=ALU.mult), ["osf", "gdnb"], ["osf"])
                yield
                DVE(lambda e: e.tensor_tensor(out=dno, in0=osf, in1=gate_s[:, :, h * 128:(h + 1) * 128], op=ALU.mult), ["osf", "gate_s", "dno"], ["dno"])
                yield
                for half in range(2):
                    pa, pak = bank()
                    pv = pa.bitcast(BF16).rearrange("p (c n) -> p c n", c=8)
                    for c in range(8):
                        t = half * 8 + c
                        tr(pv[:, c, :], dno[:, t, :], identb, ["dno", "cbf"], [pak])
                    ACT(lambda e: e.copy(out=dn_oT[:, h, half * 1024:(half + 1) * 1024], in_=pv.rearrange("p c n -> p (c n)")), [pak], ["dn_oT"])
                    yield

            wo = A.view("kqT", [128, 8, D], BF16)
            gates = []
            for k_ in range(32 + 2):
                gens = []
                if k_ < 32:
                    gens.append(Egen(*divmod(k_, 8)))
                if k_ == 32:
                    for g in range(2):
                        load_w(wo[g * 64:(g + 1) * 64, 0:4, :], w_out[g * 256:(g + 1) * 256, :].rearrange("(j d) n -> d j n", d=64), "kqT")
                    load_w(wo[:, 4:8, :], w_out[512:1024, :].rearrange("(h p) n -> p h n", p=128), "kqT")
                if 0 <= k_ - 1 < 32:
                    gens.append(Ngen(*divmod(k_ - 1, 8)))
                if 0 <= k_ - 2 < 32:
                    gens.append(Sgen(*divmod(k_ - 2, 8)))
                gens += gates
                drive(gens)
                gates = []
                if k_ - 2 >= 0 and (k_ - 2) % 8 == 7:
                    gates = [gate_gen((k_ - 2) // 8)]
            drive(gates)
            S.barrier()
        A.release("ktok", "vtok", "uuG0", "uuG1", "uuG2", "kgG0", "kgG1", "kgG2", "qkG0", "qkG1", "qkG2", "ctG0", "ctG1", "ctG2", "atG0", "atG1", "atG2", "qgG0", "qgG1", "wtok", "osf", "dno", "osum0", "osum1", "dgh", "dgl", "Dm", "Em", "egr", "PR0", "PR1", "PT_0", "PT_1",
                  "P0b0", "P0b1", "vkb0", "vkb1", "mb2", "Sb0", "Sb1", "vn0", "vn1", "gate_s", "gc", "gl", "kgs", "bgs", "beta", "gcb", "GGh", "GGl",
                  "alb" if "alb" in A.live else "gdnb", "cwt")
        if "gdnb" in A.live:
            A.release("gdnb")

        dump("dn_oT", dn_oT[:, 0, :], [])
        x1 = A.alloc("x1", [128, NT, D], F32)
        DMA("sp", gbc, gains["g_mix_post"].partition_broadcast(128), [], ["gbc"], "gbc")
        NE = 4
        etmps = [A.alloc("etmp" if i == 0 else "etmp%d" % i, [128, D], F32, top=True) for i in range(NE)]
        etk = ["etmp%d" % i for i in range(NE)]
        xbuf = [A.alloc("xb%d" % i, [128, D], F32) for i in range(NE)]

        def oproj_gen(t):
            ts_ = slice(t * 128, (t + 1) * 128)
            DMA("sp", xbuf[t % NE], x_d[ts_, :], [], ["xb%d" % (t % NE)], "xb%d" % (t % NE))
            p2, p2k = bank2()
            for c in range(8):
                for hf in range(2):
                    lhs = attn_oT[:, c, ts_] if c < 4 else dn_oT[:, c - 4, ts_]
                    mm(p2[:, hf * 512:(hf + 1) * 512], lhs, wo[:, c, hf * 512:(hf + 1) * 512], c == 0, c == 7, ["attn_oT", "dn_oT"], [p2k[hf]])
            yield
            yield from resid_epilogue(p2, p2k, None, xbuf[t % NE], ["xb%d" % (t % NE)], x1[:, t, :], ["x1_%d" % t], etmps[t % NE], etk[t % NE])

        pipeline([oproj_gen(t) for t in range(NT)], NE)
        S.barrier()
        A.release("attn_oT", "dn_oT", "kqT")
        dump("x1", x1[:, 0, :], [])

        h2T = A.alloc("h2T", [128, 8, S_LEN], BF16)
        wsl = [A.alloc("wsm%d" % i, [128, 8, 512], BF16) for i in range(2)]
        pre_ckv = []
        for q4 in range(2):
            w_, wk_ = wslot()
            load_w(w_, kview(w_ckv, q4 * 512, (q4 + 1) * 512), wk_)
            pre_ckv.append((w_, wk_))
        memT = A.alloc("memT", [128, 8, 256], BF16)

        def mem_tiles():
            out = []
            for t in range(2):
                def loader(t=t):
                    DMA("sp", xbuf[t % 2], mem_d[t * 128:(t + 1) * 128, :], [], ["xb%d" % (t % 2)], "xb%d" % (t % 2))
                out.append((xbuf[t % 2], ["xb%d" % (t % 2)], loader))
            return out
        norm_T(mem_tiles(), gains["g_mem"], memT, "memT", "c")
        kcT = A.alloc("kcT", [128, 8, 256], BF16)
        vc = A.alloc("vc", [128, 2, D], BF16)
        for q4 in range(2):
            w_, wk_ = pre_ckv[q4]
            for nn in range(4):
                n = q4 * 4 + nn
                pa, pak = bank()
                for c in range(8):
                    mm(pa[:, 0:256], w_[:, c, nn * 128:(nn + 1) * 128], memT[:, c, :], c == 0, c == 7, [wk_, "memT0", "memT1"], [pak])
                ACT(lambda e, n=n, pa=pa: e.copy(out=kcT[:, n, :], in_=pa[:, 0:256]), [pak], ["kcT"])
        for q4 in range(2):
            w_, wk_ = wslot()
            load_w(w_, kview(w_ckv, D + q4 * 512, D + (q4 + 1) * 512), wk_)
            for mt in range(2):
                pa, pak = bank()
                for c in range(8):
                    mm(pa, memT[:, c, mt * 128:(mt + 1) * 128], w_[:, c, :], c == 0, c == 7, [wk_, "memT%d" % mt], [pak])
                DVE(lambda e, mt=mt, q4=q4, pa=pa: e.tensor_copy(out=vc[:, mt, q4 * 512:(q4 + 1) * 512], in_=pa), [pak], ["vc"])
        norm_T([(x1[:, t, :], ["x1_%d" % t], None) for t in range(NT)], gains["g_cross_pre"], h2T, "h2T", "b")
        qcT = A.alloc("qcT", [128, 8, S_LEN], BF16)
        for q4 in range(2):
            w_, wk_ = wslot()
            load_w(w_, kview(w_cq, q4 * 512, (q4 + 1) * 512), wk_)
            for nn in range(4):
                n = q4 * 4 + nn
                bks = [bank() for _ in range(4)]
                for c in range(8):
                    for n4 in range(4):
                        mm(bks[n4][0], w_[:, c, nn * 128:(nn + 1) * 128], h2T[:, c, n4 * 512:(n4 + 1) * 512], c == 0, c == 7, [wk_] + hkeys("h2T", n4), [bks[n4][1]])
                for n4 in range(4):
                    cs = slice(n4 * 512, (n4 + 1) * 512)
                    if (n + n4) % 2 == 0:
                        ACT(lambda e: e.copy(out=qcT[:, n, cs], in_=bks[n4][0]), [bks[n4][1]], ["qcT%d" % n])
                    else:
                        DVE(lambda e: e.tensor_copy(out=qcT[:, n, cs], in_=bks[n4][0]), [bks[n4][1]], ["qcT%d" % n])
        S.barrier()
        A.release("h2T", "memT", "wsm0", "wsm1")
        ocT = A.alloc("ocT", [128, 8, S_LEN], BF16)
        wco = A.alloc("wco", [128, 8, D], BF16)
        load_w(wco[:, :, 0:512], kview(w_co, 0, 512), "wco")
        load_w(wco[:, :, 512:1024], kview(w_co, 512, 1024), "wco")
        Pc = [A.alloc("Pc%d" % i, [128, 512], BF16) for i in range(4)]
        rec = [A.alloc("rcc%d" % i, [128, 512], F32) for i in range(2)]
        pcc = [0]
        def cross_gen(hh, n4):
            cs = slice(n4 * 512, (n4 + 1) * 512)
            Ps = []
            for mt in range(2):
                pS, pSk = bank()
                for dc in range(2):
                    mm(pS, kcT[:, 2 * hh + dc, mt * 128:(mt + 1) * 128], qcT[:, 2 * hh + dc, cs], dc == 0, dc == 1, ["kcT", "qcT%d" % (2 * hh + dc)], [pSk])
                pi = pcc[0] % 4
                pcc[0] += 1
                ACT(lambda e: e.activation(out=Pc[pi], in_=pS, func=AF.Exp, scale=1.0 / 16), [pSk], ["Pc%d" % pi])
                Ps.append((Pc[pi], "Pc%d" % pi))
            yield
            pD, pDk = bank()
            for mt in range(2):
                mm(pD, onesb, Ps[mt][0], mt == 0, mt == 1, ["cbf", Ps[mt][1]], [pDk])
            pOs = []
            for dc in range(2):
                pO, pOk = bank()
                for mt in range(2):
                    mm(pO, vc[:, mt, (2 * hh + dc) * 128:(2 * hh + dc + 1) * 128], Ps[mt][0], mt == 0, mt == 1, ["vc", Ps[mt][1]], [pOk])
                pOs.append((pO, pOk))
            yield
            ri = (hh * 4 + n4) % 2
            ACT(lambda e: e.activation(out=rec[ri], in_=pD, func=AF.Ln), [pDk], ["rcc%d" % ri])
            ACT(lambda e: e.activation(out=rec[ri], in_=rec[ri], func=AF.Exp, scale=-1.0), ["rcc%d" % ri], ["rcc%d" % ri])
            yield
            for dc in range(2):
                pO, pOk = pOs[dc]
                DVE(lambda e: e.tensor_tensor(out=ocT[:, 2 * hh + dc, cs], in0=pO, in1=rec[ri], op=ALU.mult), [pOk, "rcc%d" % ri], ["ocT"])
            yield

        pipeline([cross_gen(hh, n4) for hh in range(4) for n4 in range(4)], 2)
        S.barrier()
        A.release("kcT", "vc", "qcT", "Pc0", "Pc1", "Pc2", "Pc3", "rcc0", "rcc1")
        wd = A.alloc("wd", [128, NFF, D], BF16, top=True)
        wdv = w_dn.rearrange("(c p) n -> p c n", p=128)
        for i in range(0, NFF, 6):
            load_w(wd[:, i:min(i + 6, NFF), :], wdv[:, i:min(i + 6, NFF), :], "wd")
        DMA("sp", gbc, gains["g_cross_post"].partition_broadcast(128), [], ["gbc"], "gbc")
        def coproj_gen(t):
            ts_ = slice(t * 128, (t + 1) * 128)
            p2, p2k = bank2()
            for c in range(8):
                for hf in range(2):
                    mm(p2[:, hf * 512:(hf + 1) * 512], ocT[:, c, ts_], wco[:, c, hf * 512:(hf + 1) * 512], c == 0, c == 7, ["ocT", "wco"], [p2k[hf]])
            yield
            yield from resid_epilogue(p2, p2k, None, x1[:, t, :], ["x1_%d" % t], x1[:, t, :], ["x1_%d" % t], etmps[t % NE], etk[t % NE])

        pipeline([coproj_gen(t) for t in range(NT)], NE)
        S.barrier()
        A.release("ocT", "wco")
        dump("x2", x1[:, 0, :], [])

        A.release("xb0", "xb1", "xb2", "xb3")
        A.release("etmp1", "etmp2", "etmp3")
        etmps = [etmps[0], etmps[0]]
        etk = ["etmp0", "etmp0"]
        h3T = A.alloc("h3T", [128, 8, S_LEN], BF16)
        wg = [A.alloc("wg%d" % i, [128, 8, 2, 128], BF16, top=True) for i in range(2)]
        for i in range(2):
            load_w(wg[i].rearrange("p c a n -> p (c a n)"), w_gu[i], "wg%d" % i)
        norm_T([(x1[:, t, :], ["x1_%d" % t], None) for t in range(NT)], gains["g_ffn_pre"], h3T, "h3T", "d")
        DMA("sp", gbc, gains["g_ffn_post"].partition_broadcast(128), [], ["gbc"], "gbc")
        aT = A.alloc("aT", [128, NFF, 1024], BF16)
        sg = [A.alloc("sg%d" % i, [128, 512], BF16) for i in range(3)]
        wgc = [0]
        sgc = [0]
        for tg in range(2):
            def gu_gen(i, wi):
                banks = {}
                for nn in range(2):
                    banks[("g", nn)] = bank()
                    banks[("u", nn)] = bank()
                for c in range(8):
                    for gu, a_ in (("g", 0), ("u", 1)):
                        for nn in range(2):
                            n4 = tg * 2 + nn
                            cs = slice(n4 * 512, (n4 + 1) * 512)
                            pb_, pbk = banks[(gu, nn)]
                            mm(pb_, wg[wi][:, c, a_, :], h3T[:, c, cs], c == 0, c == 7, ["wg%d" % wi] + hkeys("h3T", n4), [pbk])
                yield
                for nn in range(2):
                    si = (2 * i + nn) % 3
                    ACT(lambda e: e.activation(out=sg[si], in_=banks[("g", nn)][0], func=AF.Silu), [banks[("g", nn)][1]], ["sg%d" % si])
                yield
                for nn in range(2):
                    si = (2 * i + nn) % 3
                    DVE(lambda e: e.tensor_tensor(out=aT[:, i, nn * 512:(nn + 1) * 512], in0=banks[("u", nn)][0], in1=sg[si], op=ALU.mult), [banks[("u", nn)][1], "sg%d" % si], ["aT%d" % nn])
                yield

            gl_ = []
            for i in range(NFF):
                wi = i % 2
                def ldgen(i=i, wi=wi):
                    load_w(wg[wi].rearrange("p c a n -> p (c a n)"), w_gu[i], "wg%d" % wi)
                    yield
                if not (tg == 0 and i < 2):
                    gl_.append(ldgen())
                gl_.append(gu_gen(i, wi))
            pipeline(gl_, 2)
            def down_gen(tt):
                t = tg * 8 + tt
                p2, p2k = bank2()
                for i in range(NFF):
                    for hf in range(2):
                        mm(p2[:, hf * 512:(hf + 1) * 512], aT[:, i, tt * 128:(tt + 1) * 128], wd[:, i, hf * 512:(hf + 1) * 512], i == 0, i == NFF - 1, ["aT%d" % (tt // 4), "wd"], [p2k[hf]])
                yield
                yield from resid_epilogue(p2, p2k, None, x1[:, t, :], ["x1_%d" % t], x1[:, t, :], ["x1_%d" % t], etmps[t % 2], etk[t % 2])
                DMA("sp", out_d[t * 128:(t + 1) * 128, :], x1[:, t, :], ["x1_%d" % t], [], "out")
                yield

            pipeline([down_gen(tt) for tt in range(8)], 1)
        stats = S.finalize(final_waits=["out"] + ["dbg_" + k for k in dbg])
    return nc, stats


def _consts():
    bf = ml_dtypes.bfloat16
    j = np.arange(128)[:, None]
    i = np.arange(128)[None, :]
    cst_bf = np.zeros((128, 4, 128), np.float32)
    cst_bf[:, 0] = np.eye(128)
    cst_bf[:, 1] = 1.0
    cst_bf[:, 2] = (i <= j)
    cst_bf[:, 3] = (j <= i)
    same = (j // 64) == (i // 64)
    NEG = -30000.0
    cst_f = np.zeros((128, 10, 128), np.float32)
    cst_f[:, 0] = np.eye(128)
    cst_f[:, 1] = np.where(same & (i >= j), 0.0, NEG)
    cst_f[:, 2] = np.where(same & (i > j), 0.0, NEG)
    cst_f[:, 3] = np.where(same & (i <= j), 0.0, NEG)
    cst_f[:, 4] = np.where(same & (i < j), 0.0, NEG)
    cst_f[:, 5] = same & (j <= i)
    cst_f[:, 6] = same & (j >= i)
    cst_f[:, 7] = same
    cst_f[:, 8] = (j < 64) & (i >= 0)
    cst_f[:, 9] = (j >= 64) & (i >= 0)
    d = np.arange(128) % 64
    inv = (10000.0 ** (-(d % 32).astype(np.float32) / np.float32(32))).astype(np.float32)
    sign = np.where(d < 32, -1.0, 1.0).astype(np.float32)
    cst_c = np.stack([inv, sign], 1).astype(np.float32)
    return cst_bf.astype(bf), cst_f, cst_c


_CACHE = {}


def _prep_weights(w_in, conv_w):
    w = np.asarray(w_in)[0]
    cols = []
    q0, k0, v0, dq0, dg0, da0, db0 = 0, 512, 640, 768, 2304, 2816, 2824

    def head(base, hd):
        return list(range(base + hd * 64, base + hd * 64 + 64))

    def swp(c):
        return c[32:] + c[:32]
    for j in range(4):
        cols += head(q0, j) + head(q0, 4 + j)
    for j in range(4):
        cols += swp(head(q0, j)) + swp(head(q0, 4 + j))
    cols += head(k0, 0) + head(k0, 1)
    cols += swp(head(k0, 0)) + swp(head(k0, 1))
    cols += list(range(dq0, dq0 + 1536))
    cols += list(range(dg0, dg0 + 512))
    cols += list(range(v0, v0 + 128))
    cols += list(range(da0, da0 + 8)) + list(range(db0, db0 + 8))
    assert len(cols) == 3472
    w_inr = np.ascontiguousarray(w[:, np.array(cols)])
    cw = np.ascontiguousarray(np.asarray(conv_w)[0].T.reshape(12, 128, 5).transpose(1, 0, 2))
    return w_inr, cw


def kernel(x, mem, positions, g_mix_pre, w_in, conv_w, a_log, dt_bias, g_dn_out, attn_sink,
           w_out, g_mix_post, g_cross_pre, g_mem, w_cq, w_ckv, w_co, g_cross_post,
           g_ffn_pre, w_gate_up, w_down, g_ffn_post):
    f = lambda a: np.ascontiguousarray(np.asarray(a, dtype=np.float32))
    if "nc" not in _CACHE:
        _CACHE["nc"] = build_program()
    nc, stats = _CACHE["nc"]
    cst_bf, cst_f, cst_c = _consts()
    w_inr, cw = _prep_weights(w_in, conv_w)
    shared = dict(
        w_inr=w_inr, cw=cw, alog=f(a_log).reshape(1, 8), dtb=f(dt_bias).reshape(1, 8), gdn=f(g_dn_out).reshape(1, 128),
        sink=f(attn_sink).reshape(1, 8), w_out=f(w_out)[0], w_cq=f(w_cq)[0], w_ckv=f(w_ckv)[0], w_co=f(w_co)[0],
        w_gu=np.ascontiguousarray(f(w_gate_up)[0].reshape(8, 128, 2, NFF, 128).transpose(3, 1, 0, 2, 4)).reshape(NFF, 128, 2048), w_dn=f(w_down)[0],
        g_mix_pre=f(g_mix_pre).reshape(1, D), g_mix_post=f(g_mix_post).reshape(1, D), g_cross_pre=f(g_cross_pre).reshape(1, D),
        g_mem=f(g_mem).reshape(1, D), g_cross_post=f(g_cross_post).reshape(1, D), g_ffn_pre=f(g_ffn_pre).reshape(1, D),
        g_ffn_post=f(g_ffn_post).reshape(1, D), cst_bf=cst_bf, cst_f=cst_f, cst_c=cst_c,
    )
    xs = f(x)
    ms = f(mem)
    ps = np.ascontiguousarray(np.asarray(positions).astype(np.int32))
    in_maps = []
    for b in range(8):
        m = dict(shared)
        m["x"] = xs[b]
        m["mem"] = ms[b]
        m["pos"] = ps[b:b + 1]
        in_maps.append(m)
    res = run_bass_kernel_spmd(nc, in_maps, core_ids=list(range(8)))
    _CACHE["res"] = res
    return np.stack([np.asarray(r["out"]) for r in res.results], 0).astype(np.float32)
```

```python
import numpy as np
import ml_dtypes
import concourse.bass as bass
import concourse.mybir as mybir
from concourse.bass_utils import run_bass_kernel_spmd
from contextlib import ExitStack

F32 = mybir.dt.float32
BF16 = mybir.dt.bfloat16
I32 = mybir.dt.int32
AF = mybir.ActivationFunctionType
ALU = mybir.AluOpType
AX = mybir.AxisListType

S_LEN = 2048
NT = 16
D = 1024
EPS = 1e-6
DFF = 2816
NFF = 22
PI = float(np.pi)
TWO_PI = float(2 * np.pi)
DEBUG = {}


class Ins:
    __slots__ = ("eng", "fn", "deps", "needed", "sem", "val", "is_dma")


class _Rec:
    def __init__(self):
        self.call = None

    def __getattr__(self, name):
        def f(*a, **k):
            self.call = (name, a, k)
            return self
        return f


class Sched:
    ENG = ("pe", "act", "dve", "pool", "sp")

    def __init__(self, nc, es):
        self.nc = nc
        self.es = es
        self.streams = {e: [] for e in self.ENG}
        self.last_w = {}
        self.readers = {}
        self.esem = {e: es.enter_context(nc.semaphore("s_" + e)) for e in self.ENG}
        self.dsem = {}
        self.dcount = {}
        self.last = {}

    def op(self, eng, fn, reads=(), writes=(), dma=None):
        ins = Ins()
        ins.eng = eng
        rec = _Rec()
        fn(rec)
        assert rec.call is not None
        ins.fn = rec.call
        ins.needed = False
        ins.is_dma = dma is not None
        px = [k for k in reads if k.startswith("pb")]
        if px:
            reads = [k for k in reads if not k.startswith("pb")]
            writes = list(writes) + [k for k in px if k not in writes]
        deps = []
        for k in reads:
            w = self.last_w.get(k)
            if w is not None:
                deps.append(w)
        strict = eng == "pool"
        for k in writes:
            w = self.last_w.get(k)
            if w is not None and (w.eng != eng or w.is_dma or ins.is_dma or strict):
                deps.append(w)
            for e, r in self.readers.get(k, {}).items():
                if e != eng or r.is_dma or ins.is_dma or strict:
                    deps.append(r)
        for d in deps:
            d.needed = True
        ins.deps = deps
        if ins.is_dma:
            if dma not in self.dsem:
                self.dsem[dma] = self.es.enter_context(self.nc.semaphore("d_" + dma))
                self.dcount[dma] = 0
            self.dcount[dma] += 16
            ins.sem = self.dsem[dma]
            ins.val = self.dcount[dma]
        else:
            ins.sem = self.esem[eng]
            ins.val = None
            self.last[eng] = ins
        for k in writes:
            self.last_w[k] = ins
            self.readers[k] = {}
        for k in reads:
            rk = self.readers.setdefault(k, {})
            rk[eng if not ins.is_dma else (eng, dma)] = ins
        self.streams[eng].append(ins)
        return ins

    def barrier(self):
        lasts = [i for i in self.last.values()]
        dm = []
        for name in self.dsem:
            d = Ins()
            d.eng = "sp"
            d.is_dma = True
            d.sem = self.dsem[name]
            d.val = self.dcount[name]
            d.needed = True
            dm.append(d)
        for i in lasts:
            i.needed = True
        for e in self.ENG:
            ins = Ins()
            ins.eng = e
            ins.fn = None
            ins.needed = False
            ins.is_dma = False
            ins.deps = list(lasts) + dm
            ins.sem = self.esem[e]
            ins.val = None
            self.streams[e].append(ins)
        self.last_w = {}
        self.readers = {}

    def finalize(self, final_waits=()):
        for e in self.ENG:
            c = 0
            for ins in self.streams[e]:
                if not ins.is_dma and ins.needed and ins.fn is not None:
                    c += 1
                    ins.val = c
                elif not ins.is_dma and ins.fn is None:
                    ins.val = c
        streams = self.streams
        stats = {}

        def emit(engobj, ename):
            seen = {}
            nw = 0
            for ins in streams[ename]:
                need = {}
                for d in ins.deps:
                    if d.val is None or d.val == 0:
                        continue
                    sid = id(d.sem)
                    if sid not in need or need[sid][1] < d.val:
                        need[sid] = (d.sem, d.val)
                for sid, (sem, val) in need.items():
                    if seen.get(sid, 0) < val:
                        engobj.wait_ge(sem, val)
                        seen[sid] = val
                        nw += 1
                if ins.fn is None:
                    continue
                nm_, a_, k_ = ins.fn
                bi = getattr(engobj, nm_)(*a_, **k_)
                if ins.is_dma:
                    bi.then_inc(ins.sem, 16)
                elif ins.needed:
                    bi.then_inc(ins.sem, 1)
            if ename == "sp":
                for name in final_waits:
                    engobj.wait_ge(self.dsem[name], self.dcount[name])
            stats[ename] = (len(streams[ename]), nw)

        with self.nc.Block() as block:
            @block.tensor
            def _(e):
                emit(e, "pe")

            @block.scalar
            def _(e):
                emit(e, "act")

            @block.vector
            def _(e):
                emit(e, "dve")

            @block.gpsimd
            def _(e):
                emit(e, "pool")

            @block.sync
            def _(e):
                emit(e, "sp")
        return stats


class Arena:
    def __init__(self, ap, nwords):
        self.ap = ap
        self.free = [(0, nwords * 4)]
        self.live = {}

    def alloc(self, name, shape, dt, top=False):
        esz = 2 if dt == BF16 else 4
        n = 1
        for s in shape[1:]:
            n *= s
        nbytes = ((n * esz + 63) // 64) * 64
        order = range(len(self.free) - 1, -1, -1) if top else range(len(self.free))
        for idx in order:
            off, sz = self.free[idx]
            if sz >= nbytes:
                if top:
                    self.free[idx] = (off, sz - nbytes)
                    off = off + sz - nbytes
                else:
                    self.free[idx] = (off + nbytes, sz - nbytes)
                if sz == nbytes:
                    del self.free[idx]
                break
        else:
            raise RuntimeError("arena OOM for %s (%d bytes); free=%s live=%s" % (name, nbytes, self.free, sorted((v[0], v[1], k) for k, v in self.live.items())))
        self.live[name] = (off, nbytes)
        v = self.ap[:, off // 4:(off + nbytes) // 4]
        if dt != F32:
            v = v.bitcast(dt)
        v = v[:, 0:n]
        if len(shape) == 3:
            v = v.rearrange("p (a b) -> p a b", a=shape[1])
        elif len(shape) == 4:
            v = v.rearrange("p (a b c) -> p a b c", a=shape[1], b=shape[2])
        elif len(shape) == 5:
            v = v.rearrange("p (a b c d) -> p a b c d", a=shape[1], b=shape[2], c=shape[3])
        if shape[0] < 128:
            v = v[0:shape[0]]
        return v

    def view(self, name, shape, dt):
        off, nb = self.live[name]
        esz = 2 if dt == BF16 else 4
        n = 1
        for s in shape[1:]:
            n *= s
        assert n * esz <= nb
        v = self.ap[:, off // 4:(off + nb) // 4]
        if dt != F32:
            v = v.bitcast(dt)
        v = v[:, 0:n]
        if len(shape) == 3:
            v = v.rearrange("p (a b) -> p a b", a=shape[1])
        return v

    def release(self, *names):
        for name in names:
            off, nb = self.live.pop(name)
            self.free.append((off, nb))
        self.free.sort()
        m = []
        for off, sz in self.free:
            if m and m[-1][0] + m[-1][1] == off:
                m[-1] = (m[-1][0], m[-1][1] + sz)
            else:
                m.append((off, sz))
        self.free = m


def bc(ap, shape):
    return ap.to_broadcast(list(shape))


def build_program():
    nc = bass.Bass("TRN2", target_bir_lowering=False)

    def din(name, shape, dt=F32):
        return nc.dram_tensor(name, list(shape), dt, kind="ExternalInput").ap()

    x_d = din("x", [S_LEN, D])
    mem_d = din("mem", [256, D])
    pos_d = din("pos", [1, S_LEN], I32)
    w_inr = din("w_inr", [D, 3472])
    cw_d = din("cw", [128, 12, 5])
    alog_d = din("alog", [1, 8])
    dtb_d = din("dtb", [1, 8])
    gdn_d = din("gdn", [1, 128])
    sink_d = din("sink", [1, 8])
    w_out = din("w_out", [D, D])
    w_cq = din("w_cq", [D, D])
    w_ckv = din("w_ckv", [D, 2 * D])
    w_co = din("w_co", [D, D])
    w_gu = din("w_gu", [NFF, 128, 2048])
    w_dn = din("w_dn", [DFF, D])
    gains = {k: din(k, [1, D]) for k in ("g_mix_pre", "g_mix_post", "g_cross_pre", "g_mem", "g_cross_post", "g_ffn_pre", "g_ffn_post")}
    cst_bf = din("cst_bf", [128, 4, 128], BF16)
    cst_f = din("cst_f", [128, 10, 128])
    cst_c = din("cst_c", [128, 2])
    out_d = nc.dram_tensor("out", [S_LEN, D], F32, kind="ExternalOutput").ap()
    dbg = {}
    for k, shp in DEBUG.items():
        dbg[k] = nc.dram_tensor("dbg_" + k, list(shp), F32, kind="ExternalOutput").ap()

    es = ExitStack()
    with es:
        S = Sched(nc, es)
        NW = 52900
        arena_t = es.enter_context(nc.sbuf_tensor("arena", [128, NW], F32))
        A = Arena(arena_t[:], NW)
        pbig = [es.enter_context(nc.psum_tensor("pb%d" % i, [128, 1024], F32)) for i in range(4)]
        bank_ctr = [0]

        def bank():
            i = bank_ctr[0] % 8
            bank_ctr[0] += 1
            return pbig[i // 2][:, (i % 2) * 512:(i % 2) * 512 + 512], "pb%d" % i

        def bank_at(i):
            return pbig[i // 2][:, (i % 2) * 512:(i % 2) * 512 + 512], "pb%d" % i

        rot4 = [0]

        def bank_hi():
            i = 4 + rot4[0] % 4
            rot4[0] += 1
            return bank_at(i)

        def bank2():
            if bank_ctr[0] % 2:
                bank_ctr[0] += 1
            i = bank_ctr[0] % 8
            bank_ctr[0] += 2
            return pbig[i // 2][:], ["pb%d" % i, "pb%d" % (i + 1)]

        def pipeline(gens, depth):
            gens = list(gens)
            active = []
            while gens or active:
                if gens and len(active) < depth:
                    active.append(gens.pop(0))
                for g_ in list(active):
                    try:
                        next(g_)
                    except StopIteration:
                        active.remove(g_)

        def PE(fn, r, w):
            S.op("pe", fn, reads=r, writes=w)

        def ACT(fn, r, w):
            S.op("act", fn, reads=r, writes=w)

        def DVE(fn, r, w):
            S.op("dve", fn, reads=r, writes=w)

        def POOL(fn, r, w):
            S.op("pool", fn, reads=r, writes=w)

        def DMA(q, out, in_, r, w, grp):
            S.op(q, lambda e, o=out, i=in_: e.dma_start(out=o, in_=i), reads=r, writes=w, dma=grp)

        def mm(out, lhsT, rhs, start, stop, r, w, tp=None):
            if tp is None:
                PE(lambda e, o=out, l=lhsT, rr=rhs, s=start, t=stop: e.matmul(out=o, lhsT=l, rhs=rr, start=s, stop=t), r, w)
            else:
                PE(lambda e, o=out, l=lhsT, rr=rhs, s=start, t=stop, tp=tp: e.matmul(out=o, lhsT=l, rhs=rr, start=s, stop=t, tile_position=tp), r, w)

        def tr(out, in_, ident, r, w):
            PE(lambda e, o=out, i=in_, d=ident: e.transpose(out=o, in_=i, identity=d), r, w)

        def dump(name, ap, keys, rows=128):
            if name in dbg:
                tmp = A.alloc("dbgtmp_" + name, list(ap.shape), F32)
                DVE(lambda e, o=tmp, i=ap: e.tensor_copy(out=o, in_=i), keys, ["dbgtmp_" + name])
                DMA("sp", dbg[name], tmp, ["dbgtmp_" + name], [], "dbg_" + name)
                S.barrier()
                A.release("dbgtmp_" + name)

        cbf = A.alloc("cbf", [128, 4, 128], BF16, top=True)
        cf = A.alloc("cf", [128, 10, 128], F32, top=True)
        cc = A.alloc("cc", [128, 2], F32, top=True)
        DMA("sp", cbf, cst_bf, [], ["cbf"], "c0")
        DMA("sp", cf, cst_f, [], ["cf"], "c1")
        DMA("sp", cc, cst_c, [], ["cc"], "c2")
        identb, onesb, mprev, mnext = cbf[:, 0, :], cbf[:, 1, :], cbf[:, 2, :], cbf[:, 3, :]
        identf = cf[:, 0, :]
        small = A.alloc("small", [128, 64], F32, top=True)
        gbc = A.alloc("gbc", [128, D], F32, top=True)

        wq_rr = [0]

        def load_w(dst, src, key):
            DMA("pool", dst, src, [], [key], "w_" + key)

        def kview(w, c0, c1):
            return w.rearrange("(c p) n -> p c n", p=128)[:, :, c0:c1]

        stat_ctr = [0]

        def norm_T(tiles, gain_d, hT, hkey, tag):
            DMA("sp", gbc, gain_d.partition_broadcast(128), [], ["gbc"], "gbc")
            junk = A.alloc("junk_" + tag, [128, D], BF16)
            hb = [A.alloc("hb%d_%s" % (i, tag), [128, D], BF16) for i in range(4)]

            def ngen(t, xt, xk, loader):
                if loader is not None:
                    loader()
                sc = stat_ctr[0] % 32
                stat_ctr[0] += 1
                ssq = small[:, 2 * sc:2 * sc + 1]
                rs = small[:, 2 * sc + 1:2 * sc + 2]
                sk = "st%d" % sc
                ACT(lambda e: e.activation(out=junk, in_=xt, func=AF.Square, accum_out=ssq), xk, ["junk" + tag, sk])
                ACT(lambda e: e.activation(out=rs, in_=ssq, func=AF.Sqrt, scale=1.0 / D, bias=EPS), [sk], [sk + "r"])
                yield
                h = hb[t % 4]
                hk = "hb%d%s" % (t % 4, tag)
                DVE(lambda e: e.reciprocal(out=rs, in_=rs), [sk + "r"], [sk + "r"])
                DVE(lambda e: e.scalar_tensor_tensor(out=h, in0=xt, scalar=rs, in1=gbc, op0=ALU.mult, op1=ALU.mult), xk + [sk + "r", "gbc"], [hk])
                yield
                pb, pk = bank()
                pbv = pb.bitcast(BF16).rearrange("p (c n) -> p c n", c=8)
                for c in range(8):
                    tr(pbv[:, c, :], h[:, c * 128:(c + 1) * 128], identb, [hk, "cbf"], [pk])
                yield
                if t % 2 == 0:
                    ACT(lambda e: e.copy(out=hT[:, :, t * 128:(t + 1) * 128], in_=pbv), [pk], ["%s%d" % (hkey, t)])
                else:
                    DVE(lambda e: e.tensor_copy(out=hT[:, :, t * 128:(t + 1) * 128], in_=pbv), [pk], ["%s%d" % (hkey, t)])
                yield

            pipeline([ngen(t, xt, xk, ld) for t, (xt, xk, ld) in enumerate(tiles)], 4)
            S.barrier()
            A.release("junk_" + tag, "hb0_" + tag, "hb1_" + tag, "hb2_" + tag, "hb3_" + tag)

        def hkeys(hkey, n4):
            return ["%s%d" % (hkey, 4 * n4 + i) for i in range(4)]

        xbuf = [A.alloc("xb%d" % i, [128, D], F32) for i in range(4)]

        def x_tiles():
            out = []
            for t in range(NT):
                def loader(t=t):
                    DMA("sp", xbuf[t % 4], x_d[t * 128:(t + 1) * 128, :], [], ["xb%d" % (t % 4)], "xb%d" % (t % 4))
                out.append((xbuf[t % 4], ["xb%d" % (t % 4)], loader))
            return out

        def resid_epilogue(pb2, pk2, gkey_loaded, xin, xin_keys, xout, xout_keys, tmp, tmpk):
            sc = stat_ctr[0] % 32
            stat_ctr[0] += 1
            ssq = small[:, 2 * sc:2 * sc + 1]
            rs = small[:, 2 * sc + 1:2 * sc + 2]
            sk = "st%d" % sc
            ACT(lambda e: e.activation(out=tmp, in_=pb2, func=AF.Square, accum_out=ssq), pk2, [tmpk, sk])
            ACT(lambda e: e.activation(out=rs, in_=ssq, func=AF.Sqrt, scale=1.0 / D, bias=EPS), [sk], [sk + "r"])
            yield
            DVE(lambda e: e.reciprocal(out=rs, in_=rs), [sk + "r"], [sk + "r"])
            DVE(lambda e: e.scalar_tensor_tensor(out=tmp, in0=pb2, scalar=rs, in1=gbc, op0=ALU.mult, op1=ALU.mult), pk2 + [sk + "r", "gbc", tmpk], [tmpk])
            yield
            if sc % 2 == 0:
                POOL(lambda e: e.tensor_tensor(out=xout, in0=tmp, in1=xin, op=ALU.add), [tmpk] + xin_keys, xout_keys)
            else:
                DVE(lambda e: e.tensor_tensor(out=xout, in0=tmp, in1=xin, op=ALU.add), [tmpk] + xin_keys, xout_keys)
            yield

        hT = A.alloc("hT", [128, 8, S_LEN], BF16)
        wsl = [A.alloc("wsl%d" % i, [128, 8, 512], BF16) for i in range(2)]
        load_w(wsl[0], kview(w_inr, 0, 512), "wsl0")
        load_w(wsl[1], kview(w_inr, 512, 1024), "wsl1")
        cosT = A.alloc("cosT", [128, S_LEN], F32)
        sinT = A.alloc("sinT", [128, S_LEN], F32)
        posi = A.alloc("posi", [128, S_LEN], I32)
        ang = A.alloc("ang", [128, S_LEN], F32)
        rr = A.alloc("rr", [128, S_LEN], F32)
        kf = A.alloc("kf", [128, S_LEN], F32)
        DMA("sp", posi, pos_d.partition_broadcast(128), [], ["posi"], "posi")
        DVE(lambda e: e.tensor_copy(out=kf, in_=posi), ["posi"], ["kf"])
        DVE(lambda e: e.tensor_scalar(out=ang, in0=kf, scalar1=cc[:, 0:1], scalar2=None, op0=ALU.mult), ["kf", "cc"], ["ang"])
        for which, dst in (("sin", sinT), ("cos", cosT)):
            if which == "cos":
                DVE(lambda e: e.tensor_scalar(out=ang, in0=ang, scalar1=PI / 2, scalar2=None, op0=ALU.add), ["ang"], ["ang"])
            DVE(lambda e: e.tensor_scalar(out=posi, in0=ang, scalar1=1.0 / TWO_PI, scalar2=None, op0=ALU.mult), ["ang"], ["posi"])
            DVE(lambda e: e.tensor_copy(out=kf, in_=posi), ["posi"], ["kf"])
            DVE(lambda e: e.scalar_tensor_tensor(out=rr, in0=kf, scalar=-TWO_PI, in1=ang, op0=ALU.mult, op1=ALU.add), ["kf", "ang"], ["rr"])
            DVE(lambda e: e.tensor_scalar(out=kf, in0=rr, scalar1=PI, scalar2=TWO_PI, op0=ALU.is_gt, op1=ALU.mult), ["rr"], ["kf"])
            DVE(lambda e: e.tensor_tensor(out=rr, in0=rr, in1=kf, op=ALU.subtract), ["rr", "kf"], ["rr"])
            DVE(lambda e: e.tensor_scalar(out=rr, in0=rr, scalar1=-PI, scalar2=PI, op0=ALU.max, op1=ALU.min), ["rr"], ["rr"])
            if which == "sin":
                ACT(lambda e, d=dst: e.activation(out=d, in_=rr, func=AF.Sin, scale=cc[:, 1:2]), ["rr", "cc"], ["sinT"])
            else:
                ACT(lambda e, d=dst: e.activation(out=d, in_=rr, func=AF.Sin), ["rr"], ["cosT"])

        norm_T(x_tiles(), gains["g_mix_pre"], hT, "hT", "a")
        A.release("posi", "ang", "rr", "kf")
        A.release("xb0", "xb1", "xb2", "xb3")
        wctr = [0]

        def wslot():
            i = wctr[0] % 2
            wctr[0] += 1
            return wsl[i], "wsl%d" % i

        qT = A.alloc("qT", [128, 4, S_LEN], BF16)
        kT = A.alloc("kT", [128, S_LEN], BF16)
        ropeA = [A.alloc("ropeA%d" % i, [128, 512], F32) for i in range(2)]
        ropeB = [A.alloc("ropeB%d" % i, [128, 512], F32) for i in range(2)]
        rctr = [0]
        wq0, wq0k = wslot()
        wq1, wq1k = wslot()
        def rope_gen(wa, wak, ca, wb, wbk, cb, n4p, dst, dkey):
            n4s = (2 * n4p, 2 * n4p + 1)
            pas = [bank() for _ in n4s]
            pbs = [bank() for _ in n4s]
            for c in range(8):
                for i_, n4 in enumerate(n4s):
                    mm(pas[i_][0], wa[:, c, ca], hT[:, c, n4 * 512:(n4 + 1) * 512], c == 0, c == 7, [wak] + hkeys("hT", n4), [pas[i_][1]])
                for i_, n4 in enumerate(n4s):
                    mm(pbs[i_][0], wb[:, c, cb], hT[:, c, n4 * 512:(n4 + 1) * 512], c == 0, c == 7, [wbk] + hkeys("hT", n4), [pbs[i_][1]])
            yield
            for i_, n4 in enumerate(n4s):
                cs = slice(n4 * 512, (n4 + 1) * 512)
                DVE(lambda e: e.tensor_tensor(out=ropeA[i_], in0=pas[i_][0], in1=cosT[:, cs], op=ALU.mult), [pas[i_][1], "cosT"], ["ropeA%d" % i_])
                DVE(lambda e: e.tensor_tensor(out=ropeB[i_], in0=pbs[i_][0], in1=sinT[:, cs], op=ALU.mult), [pbs[i_][1], "sinT"], ["ropeB%d" % i_])
                POOL(lambda e: e.tensor_tensor(out=dst[:, cs], in0=ropeA[i_], in1=ropeB[i_], op=ALU.add), ["ropeA%d" % i_, "ropeB%d" % i_], [dkey])
            yield

        wk, wkk = wslot()
        gl_ = [rope_gen(wq0, wq0k, slice(j * 128, (j + 1) * 128), wq1, wq1k, slice(j * 128, (j + 1) * 128), n4p, qT[:, j, :], "qT") for j in range(4) for n4p in range(2)]
        pipeline(gl_, 2)
        load_w(wk[:, :, 0:256], kview(w_inr, 1024, 1280), wkk)
        gl_ = [rope_gen(wk, wkk, slice(0, 128), wk, wkk, slice(128, 256), n4p, kT, "kT") for n4p in range(2)]
        pipeline(gl_, 2)
        gate_s = A.alloc("gate_s", [128, NT, 512], BF16, top=True)
        vtokA = A.alloc("vtokA", [128, NT, 128], BF16)
        ab = A.alloc("ab", [128, NT, 16], F32, top=True)
        wt0, wt0k = wslot()
        load_w(wt0, kview(w_inr, 2816, 3328), wt0k)
        wt1, wt1k = wslot()
        load_w(wt1[:, :, 0:144], kview(w_inr, 3328, 3472), wt1k)
        def tokm_gen(t):
            pa, pak = bank()
            pb_, pbk = bank()
            ts_ = slice(t * 128, (t + 1) * 128)
            for c in range(8):
                mm(pa, hT[:, c, ts_], wt0[:, c, :], c == 0, c == 7, [wt0k, "hT%d" % t], [pak])
                mm(pb_[:, 0:144], hT[:, c, ts_], wt1[:, c, 0:144], c == 0, c == 7, [wt1k, "hT%d" % t], [pbk])
            yield
            ACT(lambda e: e.activation(out=gate_s[:, t, :], in_=pa, func=AF.Silu), [pak], ["gate_s"])
            DVE(lambda e: e.tensor_copy(out=vtokA[:, t, :], in_=pb_[:, 0:128]), [pbk], ["vtokA"])
            DVE(lambda e: e.tensor_copy(out=ab[:, t, :], in_=pb_[:, 128:144]), [pbk], ["ab"])
            yield

        pipeline([tokm_gen(t) for t in range(NT)], 2)
        S.barrier()
        A.release("cosT", "sinT", "ropeA0", "ropeA1", "ropeB0", "ropeB1")
        dump("qT", qT[:, 0, :], [])
        dump("kT", kT, [])

        attn_oT = A.alloc("attn_oT", [128, 4, S_LEN], BF16, top=True)
        sk_f = A.alloc("sk_f", [1, 8], F32)
        sinkrow = A.alloc("sinkrow", [1, 2, 512], BF16)
        sinkrow_lo = A.alloc("sinkrow_lo", [1, 2, 512], BF16)
        sk_t = A.alloc("sk_t", [1, 2, 512], F32)
        DMA("sp", sk_f, sink_d, [], ["sk_f"], "sk")
        ACT(lambda e: e.activation(out=sk_f, in_=sk_f, func=AF.Exp), ["sk_f"], ["sk_f"])
        for g in range(2):
            DVE(lambda e, g=g: e.tensor_copy(out=sk_t[:, g, :].rearrange("p (j q) -> p j q", j=4), in_=bc(sk_f[:, 4 * g:4 * g + 4].unsqueeze(2), [1, 4, 128])), ["sk_f"], ["sk_t"])
        DVE(lambda e: e.tensor_copy(out=sinkrow, in_=sk_t), ["sk_t"], ["sinkrow"])
        DVE(lambda e: e.tensor_tensor(out=sk_t, in0=sk_t, in1=sinkrow, op=ALU.subtract), ["sk_t", "sinkrow"], ["sk_t"])
        DVE(lambda e: e.tensor_copy(out=sinkrow_lo, in_=sk_t), ["sk_t"], ["sinkrow_lo"])
        Pt = [A.alloc("Pt%d" % i, [128, 512], BF16) for i in range(8)]
        rec = [A.alloc("rec%d" % i, [128, 512], F32) for i in range(2)]
        pctr = [0]
        def attn_gen(qb):
            pO, pOk = bank_at((qb % 2) * 2)
            pD, pDk = bank_at((qb % 2) * 2 + 1)
            qs_ = slice(qb * 128, (qb + 1) * 128)
            kbs = [kb for kb in (qb - 1, qb, qb + 1) if 0 <= kb < NT]
            items = [(ki, kb, len(kbs)) for ki, kb in enumerate(kbs)]
            prep = {}

            def stage1(it):
                ki, kb, nk = it
                Ps = []
                pss = []
                for g in range(2):
                    gs = slice(g * 64, (g + 1) * 64)
                    pS, pSk = bank_hi()
                    mm(pS.rearrange("p (j q) -> p j q", j=4), kT[gs, kb * 128:(kb + 1) * 128], qT[gs, :, qs_], True, True, ["kT", "qT"], [pSk])
                    pss.append((pS, pSk))
                for g in range(2):
                    pS, pSk = pss[g]
                    pi = pctr[0] % 8
                    pctr[0] += 1
                    P = Pt[pi]
                    Pk = "Pt%d" % pi
                    ACT(lambda e: e.activation(out=P, in_=pS, func=AF.Exp, scale=0.125), [pSk], [Pk])
                    if kb != qb:
                        m = mprev if kb < qb else mnext
                        POOL(lambda e: e.tensor_tensor(out=P.rearrange("p (j q) -> p j q", j=4), in0=P.rearrange("p (j q) -> p j q", j=4), in1=bc(m.unsqueeze(1), [128, 4, 128]), op=ALU.mult), [Pk, "cbf"], [Pk])
                    Ps.append((P, Pk))
                prep[it] = Ps

            def stage2(it):
                ki, kb, nk = it
                Ps = prep[it]
                for g in range(2):
                    gs = slice(g * 64, (g + 1) * 64)
                    mm(pO[gs, :], vtokA[:, kb, gs], Ps[g][0], ki == 0, ki == nk - 1, ["vtokA", Ps[g][1]], [pOk], tp=(0, g * 64))
                for g in range(2):
                    gs = slice(g * 64, (g + 1) * 64)
                    mm(pD[gs, :], onesb[:, 0:64], Ps[g][0], ki == 0, False, ["cbf", Ps[g][1]], [pDk], tp=(0, g * 64))
                if ki == nk - 1:
                    for g in range(2):
                        gs = slice(g * 64, (g + 1) * 64)
                        mm(pD[gs, :], onesb[0:1, 0:64], sinkrow[0:1, g, :], False, False, ["cbf", "sinkrow"], [pDk], tp=(0, g * 64))
                    for g in range(2):
                        gs = slice(g * 64, (g + 1) * 64)
                        mm(pD[gs, :], onesb[0:1, 0:64], sinkrow_lo[0:1, g, :], False, True, ["cbf", "sinkrow_lo"], [pDk], tp=(0, g * 64))

            stage1(items[0])
            for i, it in enumerate(items):
                if i + 1 < len(items):
                    stage1(items[i + 1])
                yield
                stage2(it)
            yield
            ri = qb % 2
            ACT(lambda e: e.activation(out=rec[ri], in_=pD, func=AF.Ln), [pDk], ["rec%d" % ri])
            ACT(lambda e: e.activation(out=rec[ri], in_=rec[ri], func=AF.Exp, scale=-1.0), ["rec%d" % ri], ["rec%d" % ri])
            DVE(lambda e: e.tensor_tensor(out=attn_oT[:, :, qs_], in0=pO.rearrange("p (j q) -> p j q", j=4), in1=rec[ri].rearrange("p (j q) -> p j q", j=4), op=ALU.mult), [pOk, "rec%d" % ri], ["attn_oT"])
            yield

        pipeline([attn_gen(qb) for qb in range(NT)], 2)
        S.barrier()
        A.release("qT", "kT", "vtokA", "Pt0", "Pt1", "Pt2", "Pt3", "Pt4", "Pt5", "Pt6", "Pt7", "rec0", "rec1", "sk_f", "sinkrow", "sinkrow_lo", "sk_t")
        dump("attn_oT", attn_oT[:, 0, :], [])

        cwt = A.alloc("cwt", [128, 12, 5], F32)
        DMA("sp", cwt, cw_d, [], ["cwt"], "cwt")
        kqT = A.alloc("kqT", [128, 4, NT, 2, 128], BF16, top=True)
        ktok = A.alloc("ktok", [128, NT, 4, 128], BF16, top=True)
        vtok = A.alloc("vtok", [128, NT, 4, 128], BF16, top=True)
        xbp = [A.alloc("xbp%d" % i, [128, S_LEN + 4], BF16) for i in range(2)]
        prb = [A.alloc("prb%d" % i, [128, S_LEN], BF16) for i in range(2)]
        sqb = [A.alloc("sqb%d" % i, [128, S_LEN], BF16) for i in range(2)]
        dwb = [A.alloc("dwb%d" % i, [128, 5, 128], BF16) for i in range(2)]
        rtmp = [A.alloc("rtmp%d" % i, [128, 512], F32) for i in range(2)]
        for i in range(2):
            POOL(lambda e: e.memset(xbp[i], 0.0), [], ["xbp%d" % i])
        wdn = {}
        for m0 in (0, 4, 8):
            wd_, wdk = wslot() if m0 < 8 else (None, None)
            if m0 < 8:
                load_w(wd_, kview(w_inr, 1280 + m0 * 128, 1280 + (m0 + 4) * 128), wdk)
                wdn[m0] = (wd_, wdk)

        def conv_gen(m):
            kind, h = m // 4, m % 4
            bi = m % 2
            if m == 8:
                wd_, wdk = wslot()
                load_w(wd_, kview(w_inr, 1280 + 8 * 128, 1280 + 12 * 128), wdk)
                wdn[8] = (wd_, wdk)
            wd_, wdk = wdn[(m // 4) * 4]
            xb_, xbk = xbp[bi], "xbp%d" % bi
            pr, prk = prb[bi], "prb%d" % bi
            sq, sqk = sqb[bi], "sqb%d" % bi
            dw, dwk = dwb[bi], "dwb%d" % bi
            rt, rtk = rtmp[bi], "rtmp%d" % bi
            POOL(lambda e: e.tensor_tensor(out=dw, in0=bc(identb.unsqueeze(1), [128, 5, 128]), in1=bc(cwt[:, m, :].unsqueeze(2), [128, 5, 128]), op=ALU.mult), ["cbf", "cwt"], [dwk])
            for n4 in range(4):
                pa, pak = bank()
                cs = slice(n4 * 512, (n4 + 1) * 512)
                for c in range(8):
                    mm(pa, wd_[:, c, (m % 4) * 128:(m % 4 + 1) * 128], hT[:, c, cs], c == 0, c == 7, [wdk] + hkeys("hT", n4), [pak])
                ACT(lambda e: e.copy(out=xb_[:, 2 + n4 * 512:2 + (n4 + 1) * 512], in_=pa), [pak], [xbk])
                if n4 % 2 == 1:
                    yield
            for n4 in range(4):
                pa, pak = bank()
                cs = slice(n4 * 512, (n4 + 1) * 512)
                for j in range(5):
                    mm(pa, dw[:, j, :], xb_[:, n4 * 512 + j:n4 * 512 + j + 512], j == 0, j == 4, [dwk, xbk], [pak])
                dsto = sq if kind == 2 else pr
                ACT(lambda e: e.activation(out=dsto[:, cs], in_=pa, func=AF.Silu), [pak], [sqk if kind == 2 else prk])
                if n4 % 2 == 1:
                    yield
            if kind < 2:
                POOL(lambda e: e.tensor_tensor(out=sq, in0=pr, in1=pr, op=ALU.mult), [prk], [sqk])
                yield
                kqi = 1 if kind == 0 else 0
                dk_ = ("qnT%d" if kind == 0 else "knT%d") % h
                sc_ = float(128 ** -0.5) if kind == 0 else 1.0
                for n4 in range(4):
                    cs = slice(n4 * 512, (n4 + 1) * 512)
                    pa, pak = bank()
                    mm(pa, onesb, sq[:, cs], True, True, ["cbf", sqk], [pak])
                    ACT(lambda e: e.activation(out=rt, in_=pa, func=AF.Ln, bias=EPS), [pak], [rtk])
                    ACT(lambda e: e.activation(out=rt, in_=rt, func=AF.Exp, scale=-0.5), [rtk], [rtk])
                    DVE(lambda e: e.scalar_tensor_tensor(out=kqT[:, h, 4 * n4:4 * n4 + 4, kqi, :], in0=pr[:, cs].rearrange("p (t n) -> p t n", t=4), scalar=sc_, in1=rt.rearrange("p (t n) -> p t n", t=4), op0=ALU.mult, op1=ALU.mult), [prk, rtk], [dk_])
                    yield
                srcf = (lambda t: kqT[:, h, t, 0, :])
                srck = dk_
            else:
                srcf = (lambda t: sq[:, t * 128:(t + 1) * 128])
                srck = sqk
            if kind >= 1:
                dtok = ktok if kind == 1 else vtok
                dtk = "ktok" if kind == 1 else "vtok"
                for half in range(2):
                    pa, pak = bank()
                    pv = pa.bitcast(BF16).rearrange("p (c n) -> p c n", c=8)
                    for c in range(8):
                        t = half * 8 + c
                        tr(pv[:, c, :], srcf(t), identb, [srck, "cbf"], [pak])
                    ACT(lambda e: e.copy(out=dtok[:, half * 8:(half + 1) * 8, h, :], in_=pv), [pak], [dtk])
                    yield

        pipeline([conv_gen(m) for m in range(12)], 2)
        S.barrier()
        A.release("hT", "xbp0", "xbp1", "prb0", "prb1", "sqb0", "sqb1", "dwb0", "dwb1", "rtmp0", "rtmp1", "wsl0", "wsl1")

        alb = A.alloc("alb", [128, 8], F32)
        dtb = A.alloc("dtb", [128, 8], F32)
        gdnb = A.alloc("gdnb", [128, 128], F32)
        DMA("sp", alb, alog_d.partition_broadcast(128), [], ["alb"], "alb")
        DMA("sp", dtb, dtb_d.partition_broadcast(128), [], ["dtb"], "dtb")
        DMA("sp", gdnb, gdn_d.partition_broadcast(128), [], ["gdnb"], "gdnb")
        g_ = A.alloc("g_", [128, NT, 8], F32)
        lnb = A.alloc("lnb", [128, NT, 8], F32)
        gc = A.alloc("gc", [128, NT, 8], F32)
        gtot = A.alloc("gtot", [128, NT, 8], F32)
        gl = A.alloc("gl", [128, 2, NT, 8], F32)
        tmp8 = A.alloc("tmp8", [128, NT, 8], F32)
        ghl = A.alloc("ghl", [128, 2, NT, 8], BF16)
        DVE(lambda e: e.tensor_tensor(out=g_, in0=ab[:, :, 0:8], in1=bc(dtb.unsqueeze(1), [128, NT, 8]), op=ALU.add), ["ab", "dtb"], ["g_"])
        ACT(lambda e: e.activation(out=g_, in_=g_, func=AF.Exp), ["g_"], ["g_"])
        ACT(lambda e: e.activation(out=g_, in_=g_, func=AF.Ln, bias=1.0), ["g_"], ["g_"])
        ACT(lambda e: e.activation(out=alb, in_=alb, func=AF.Exp), ["alb"], ["alb"])
        DVE(lambda e: e.scalar_tensor_tensor(out=g_, in0=g_, scalar=-1.0, in1=bc(alb.unsqueeze(1), [128, NT, 8]), op0=ALU.mult, op1=ALU.mult), ["g_", "alb"], ["g_"])
        ACT(lambda e: e.activation(out=lnb, in_=ab[:, :, 8:16], func=AF.Exp, scale=-1.0), ["ab"], ["lnb"])
        ACT(lambda e: e.activation(out=lnb, in_=lnb, func=AF.Ln, bias=1.0), ["lnb"], ["lnb"])
        DVE(lambda e: e.tensor_scalar(out=lnb, in0=lnb, scalar1=-1.0, scalar2=None, op0=ALU.mult), ["lnb"], ["lnb"])
        DVE(lambda e: e.tensor_copy(out=ghl[:, 0], in_=g_), ["g_"], ["ghl"])
        DVE(lambda e: e.tensor_tensor(out=tmp8, in0=g_, in1=ghl[:, 0], op=ALU.subtract), ["g_", "ghl"], ["tmp8"])
        DVE(lambda e: e.tensor_copy(out=ghl[:, 1], in_=tmp8), ["tmp8"], ["ghl"])
        cfb = A.alloc("cfb", [128, 5, 128], BF16)
        DVE(lambda e: e.tensor_copy(out=cfb, in_=cf[:, 5:10, :]), ["cf"], ["cfb"])
        pa, pak = bank()
        for s_ in range(2):
            mm(pa[:, 0:64].rearrange("p (t h) -> p t h", t=NT), cfb[:, 0, :], ghl[:, s_, :, 0:4], s_ == 0, s_ == 1, ["cfb", "ghl"], [pak])
        for s_ in range(2):
            mm(pa[:, 64:128].rearrange("p (t h) -> p t h", t=NT), cfb[:, 1, :], ghl[:, s_, :, 4:8], s_ == 0, s_ == 1, ["cfb", "ghl"], [pak])
        for k_, cidx in ((0, 2), (1, 3), (2, 4)):
            for s_ in range(2):
                mm(pa[:, 128 + 128 * k_:256 + 128 * k_].rearrange("p (t h) -> p t h", t=NT), cfb[:, cidx, :], ghl[:, s_, :, :], s_ == 0, s_ == 1, ["cfb", "ghl"], [pak])
        DVE(lambda e, p=pa: e.tensor_copy(out=gc[:, :, 0:4], in_=p[:, 0:64].rearrange("p (t h) -> p t h", t=NT)), [pak], ["gc"])
        DVE(lambda e, p=pa: e.tensor_copy(out=gc[:, :, 4:8], in_=p[:, 64:128].rearrange("p (t h) -> p t h", t=NT)), [pak], ["gc"])
        DVE(lambda e, p=pa: e.tensor_copy(out=gtot, in_=p[:, 128:256].rearrange("p (t h) -> p t h", t=NT)), [pak], ["gtot"])
        ACT(lambda e, p=pa: e.activation(out=gl, in_=p[:, 256:512].rearrange("p (a t h) -> p a t h", a=2, t=NT), func=AF.Exp), [pak], ["gl"])
        kgs = A.alloc("kgs", [128, NT, 8], F32)
        bgs = A.alloc("bgs", [128, NT, 8], F32)
        beta = A.alloc("beta", [128, NT, 8], F32)
        gcb = A.alloc("gcb", [128, NT, 8], F32)
        DVE(lambda e: e.tensor_tensor(out=kgs, in0=gtot, in1=gc, op=ALU.subtract), ["gtot", "gc"], ["kgs"])
        ACT(lambda e: e.activation(out=kgs, in_=kgs, func=AF.Exp), ["kgs"], ["kgs"])
        DVE(lambda e: e.tensor_tensor(out=gcb, in0=gc, in1=lnb, op=ALU.add), ["gc", "lnb"], ["gcb"])
        ACT(lambda e: e.activation(out=bgs, in_=gcb, func=AF.Exp), ["gcb"], ["bgs"])
        ACT(lambda e: e.activation(out=beta, in_=lnb, func=AF.Exp), ["lnb"], ["beta"])
        GGf = A.alloc("GGf", [128, 4, 2, 2 * NT], F32)
        GGh = A.alloc("GGh", [128, 4, 2, 2 * NT], BF16)
        GGl = A.alloc("GGl", [128, 4, 2, 2 * NT], BF16)
        for h in range(4):
            for d_ in range(2):
                gv = GGf[:, h, d_, :].rearrange("p (t k) -> p t k", k=2)
                DVE(lambda e: e.tensor_copy(out=gv[:, :, 0], in_=gc[:, :, 4 * d_ + h]), ["gc"], ["GGf"])
                DVE(lambda e: e.tensor_copy(out=gv[:, :, 1], in_=gcb[:, :, 4 * d_ + h]), ["gcb"], ["GGf"])
        DVE(lambda e: e.tensor_copy(out=GGh, in_=GGf), ["GGf"], ["GGh"])
        DVE(lambda e: e.tensor_tensor(out=GGf, in0=GGf, in1=GGh, op=ALU.subtract), ["GGf", "GGh"], ["GGf"])
        DVE(lambda e: e.tensor_copy(out=GGl, in_=GGf), ["GGf"], ["GGl"])
        mb2 = A.alloc("mb2", [128, 2, 4, 128], F32)
        for pr in range(2):
            for a_ in range(4):
                POOL(lambda e: e.tensor_copy(out=mb2[:, pr, a_, :], in_=cf[:, 1 + 2 * pr + (a_ % 2), :]), ["cf"], ["mb2"])
        S.barrier()
        A.release("g_", "lnb", "gtot", "tmp8", "ghl", "alb", "dtb", "GGf", "ab", "cfb", "cf")
        dump("gc", gc.rearrange("p t h -> p (t h)"), [])

        dn_oT = A.alloc("dn_oT", [128, 4, S_LEN], BF16, top=True)
        uuG = [A.alloc("uuG%d" % i, [128, 2, 2, 128], BF16) for i in range(3)]
        kgG = [A.alloc("kgG%d" % i, [128, 2, 2, 128], BF16) for i in range(3)]
        qkG = [A.alloc("qkG%d" % i, [128, 2, 2, 128], BF16) for i in range(3)]
        ctG = [A.alloc("ctG%d" % i, [128, 2, 2, 128], BF16) for i in range(3)]
        atG = [A.alloc("atG%d" % i, [128, 2, 4, 128], BF16) for i in range(3)]
        qgG = [A.alloc("qgG%d" % i, [128, 2, 2, 128], BF16) for i in range(2)]
        wtok = A.alloc("wtok", [128, 2, 2, 128], BF16)
        osum = [A.alloc("osum%d" % i, [128, 2, NT, 128], BF16) for i in range(2)]
        osf = A.alloc("osf", [128, NT, 128], F32)
        dno = A.alloc("dno", [128, NT, 128], BF16)
        dgh = A.alloc("dgh", [128, 2, 4, 128], BF16)
        dgl = A.alloc("dgl", [128, 2, 4, 128], BF16)
        Dm = A.alloc("Dm", [128, 2, 4, 128], F32)
        Em = A.alloc("Em", [128, 2, 4, 128], F32)
        egr = A.alloc("egr", [128, 2, 2, 128], F32)
        P0b = [A.alloc("P0b%d" % i, [128, 2, 2, 128], BF16) for i in range(2)]
        vkb = [A.alloc("vkb%d" % i, [128, 2, 2, 2, 128], BF16) for i in range(2)]
        PR = [A.alloc("PR%d" % i, [128, 4, 2, 128], BF16) for i in range(2)]
        PT_ = [A.alloc("PT_%d" % i, [128, 4, 128], BF16) for i in range(2)]
        Sb = [A.alloc("Sb%d" % d_, [128, 128], BF16) for d_ in range(2)]
        vn = [A.alloc("vn%d" % d_, [128, 128], BF16) for d_ in range(2)]
        ident3 = bc(identb.unsqueeze(1), [128, 4, 128])

        def pair_t0(gi, pr):
            return 2 * gi if pr == 0 else 14 - 2 * gi

        for _once in range(1):
            def Egen(h, gi):
                par = gi % 2
                g3 = (8 * h + gi) % 3
                for pr in range(2):
                    t0 = pair_t0(gi, pr)
                    dh = 4 * pr + h
                    tsl = slice(t0 * 128, (t0 + 2) * 128)
                    gk = "_%d_%d" % (pr, gi)
                    POOL(lambda e: e.tensor_tensor(out=dgh[:, pr], in0=ident3, in1=bc(GGh[:, h, pr, 2 * t0:2 * t0 + 4].unsqueeze(2), [128, 4, 128]), op=ALU.mult), ["cbf", "GGh"], ["dgh%d" % pr])
                    POOL(lambda e: e.tensor_tensor(out=dgl[:, pr], in0=ident3, in1=bc(GGl[:, h, pr, 2 * t0:2 * t0 + 4].unsqueeze(2), [128, 4, 128]), op=ALU.mult), ["cbf", "GGl"], ["dgl%d" % pr])
                    pg, pgk = bank()
                    mm(pg, onesb, dgh[:, pr].rearrange("p a b -> p (a b)"), True, False, ["cbf", "dgh%d" % pr], [pgk])
                    mm(pg, onesb, dgl[:, pr].rearrange("p a b -> p (a b)"), False, True, ["cbf", "dgl%d" % pr], [pgk])
                    pg4 = pg.rearrange("p (a b) -> p a b", a=4)
                    for tt in range(2):
                        DVE(lambda e: e.scalar_tensor_tensor(out=Dm[:, pr, 2 * tt:2 * tt + 2, :], in0=pg4[:, 2 * tt:2 * tt + 2, :], scalar=gc[:, t0 + tt, dh:dh + 1], in1=mb2[:, pr, 2 * tt:2 * tt + 2, :], op0=ALU.subtract, op1=ALU.add), [pgk, "gc", "mb2"], ["Dm%d" % pr])
                    ACT(lambda e: e.activation(out=egr[:, pr], in_=pg4.rearrange("p (t k) n -> p t k n", t=2)[:, :, 0, :], func=AF.Exp), [pgk], ["egr%d" % pr])
                    ACT(lambda e: e.activation(out=Em[:, pr], in_=Dm[:, pr], func=AF.Exp), ["Dm%d" % pr], ["Em%d" % pr])
                    yield
                    POOL(lambda e: e.tensor_tensor(out=qgG[par][:, pr], in0=kqT[:, h, t0:t0 + 2, 1, :], in1=egr[:, pr], op=ALU.mult), ["kqT", "egr%d" % pr], ["qgG%d_%d" % (par, pr)])
                    pG, pGk = bank()
                    for tt in range(2):
                        ts_ = slice((t0 + tt) * 128, (t0 + tt + 1) * 128)
                        mm(pG[:, tt * 256:(tt + 1) * 256], kqT[:, h, t0 + tt, 0, :], kqT[:, h, t0 + tt, :, :].rearrange("p a n -> p (a n)"), True, True, ["kqT"], [pGk])
                    pG4 = pG.rearrange("p (t k n) -> p t k n", t=2, k=2)
                    Em4 = Em[:, pr].rearrange("p (t k) n -> p t k n", t=2)
                    DVE(lambda e: e.scalar_tensor_tensor(out=P0b[par][:, pr], in0=pG4[:, :, 0, :], scalar=-1.0, in1=Em4[:, :, 1, :], op0=ALU.mult, op1=ALU.mult), [pGk, "Em%d" % pr], ["P0b%d_%d" % (par, pr)])
                    DVE(lambda e: e.tensor_tensor(out=qkG[g3][:, pr], in0=pG4[:, :, 1, :], in1=Em4[:, :, 0, :], op=ALU.mult), [pGk, "Em%d" % pr], ["qkG%d_%d" % (g3, pr)])
                    yield
                    POOL(lambda e: e.tensor_tensor(out=vkb[par][:, pr, :, 0, :], in0=vtok[:, t0:t0 + 2, h, :], in1=bc(beta[:, t0:t0 + 2, dh:dh + 1], [128, 2, 128]), op=ALU.mult), ["vtok", "beta"], ["vbb%d_%d" % (par, pr)])
                    POOL(lambda e: e.tensor_tensor(out=vkb[par][:, pr, :, 1, :], in0=ktok[:, t0:t0 + 2, h, :], in1=bc(bgs[:, t0:t0 + 2, dh:dh + 1], [128, 2, 128]), op=ALU.mult), ["ktok", "bgs"], ["kbb%d_%d" % (par, pr)])
                    POOL(lambda e: e.tensor_tensor(out=kgG[g3][:, pr], in0=ktok[:, t0:t0 + 2, h, :], in1=bc(kgs[:, t0:t0 + 2, dh:dh + 1], [128, 2, 128]), op=ALU.mult), ["ktok", "kgs"], ["kgG%d_%d" % (g3, pr)])
                    yield

            def Ngen(h, gi):
                par = gi % 2
                P0 = P0b[par].rearrange("p a b n -> p (a b) n")
                p0k = ["P0b%d_%d" % (par, pr) for pr in range(2)]
                for pr in range(2):
                    ms = slice(2 * pr, 2 * pr + 2)
                    pa, pak = bank()
                    pv = pa.bitcast(BF16).rearrange("p (c n) -> p c n", c=8)
                    for tt in range(2):
                        tr(pv[:, tt, :], P0[:, 2 * pr + tt, :], identb, [p0k[pr], "cbf"], [pak])
                    ACT(lambda e: e.copy(out=PT_[0][:, ms, :], in_=pv[:, 0:2, :]), [pak], ["PT0_%d" % pr])
                    POOL(lambda e: e.tensor_tensor(out=PR[1][:, ms, 1, :], in0=P0[:, ms, :], in1=ident3[:, 0:2, :], op=ALU.add), [p0k[pr], "cbf"], ["PR1r_%d" % pr])
                yield
                for pr in range(2):
                    ms = slice(2 * pr, 2 * pr + 2)
                    pa, pak = bank()
                    for tt in range(2):
                        m_ = 2 * pr + tt
                        mm(pa[:, tt * 128:(tt + 1) * 128], PT_[0][:, m_, :], P0[:, m_, :], True, True, [p0k[pr], "PT0_%d" % pr], [pak])
                    for tt in range(2):
                        m_ = 2 * pr + tt
                        mm(pa[:, 256 + tt * 128:256 + (tt + 1) * 128], P0[:, m_, :], PT_[0][:, m_, :], True, True, [p0k[pr], "PT0_%d" % pr], [pak])
                    pav = pa.rearrange("p (a t n) -> p a t n", a=2, t=2)
                    ACT(lambda e: e.copy(out=PR[1][:, ms, 0, :], in_=pav[:, 0]), [pak], ["PR1p_%d" % pr])
                    ACT(lambda e: e.copy(out=PT_[1][:, ms, :], in_=pav[:, 1]), [pak], ["PT1_%d" % pr])
                yield
                for k in range(1, 6):
                    ci, ni = k % 2, (k + 1) % 2
                    cur, nxt = PR[ci], PR[ni]
                    ptc = PT_[ci]
                    for pr in range(2):
                        ms = slice(2 * pr, 2 * pr + 2)
                        ptk = "PT%d_%d" % (ci, pr)
                        kp, kr = "PR%dp_%d" % (ci, pr), "PR%dr_%d" % (ci, pr)
                        np_, nr = "PR%dp_%d" % (ni, pr), "PR%dr_%d" % (ni, pr)
                        if k <= 3:
                            p2, p2k = bank()
                            for tt in range(2):
                                m_ = 2 * pr + tt
                                mm(p2[:, tt * 256:(tt + 1) * 256], ptc[:, m_, :], cur[:, m_, :, :].rearrange("p a n -> p (a n)"), True, True, [ptk, kp, kr], [p2k])
                            p2v = p2.rearrange("p (t a n) -> p t a n", t=2, a=2)
                            ACT(lambda e: e.copy(out=nxt[:, ms, 0, :], in_=p2v[:, :, 0, :]), [p2k], [np_])
                            DVE(lambda e: e.tensor_tensor(out=nxt[:, ms, 1, :], in0=p2v[:, :, 1, :], in1=cur[:, ms, 1, :], op=ALU.add), [p2k, kr], [nr])
                        else:
                            pc, pck = bank()
                            for tt in range(2):
                                m_ = 2 * pr + tt
                                mm(pc[:, tt * 128:(tt + 1) * 128], ptc[:, m_, :], cur[:, m_, 1, :], True, True, [ptk, kr], [pck])
                            DVE(lambda e: e.tensor_tensor(out=nxt[:, ms, 1, :], in0=pc[:, 0:256].rearrange("p (t n) -> p t n", t=2), in1=cur[:, ms, 1, :], op=ALU.add), [pck, kr], [nr])
                        if k <= 4:
                            pb_, pbk = bank()
                            for tt in range(2):
                                m_ = 2 * pr + tt
                                mm(pb_[:, tt * 128:(tt + 1) * 128], cur[:, m_, 0, :], ptc[:, m_, :], True, True, [ptk, kp], [pbk])
                            ACT(lambda e: e.copy(out=PT_[ni][:, ms, :], in_=pb_[:, 0:256].rearrange("p (t n) -> p t n", t=2)), [pbk], ["PT%d_%d" % (ni, pr)])
                    yield
                XR = PR[0]
                g3 = (8 * h + gi) % 3
                for pr in range(2):
                    t0 = pair_t0(gi, pr)
                    dh = 4 * pr + h
                    pu, puk = bank()
                    for tt in range(2):
                        mm(pu[:, tt * 256:(tt + 1) * 256], XR[:, pr * 2 + tt, 1, :], vkb[par][:, pr, tt, :, :].rearrange("p a n -> p (a n)"), True, True, ["PR0r_%d" % pr, "vbb%d_%d" % (par, pr), "kbb%d_%d" % (par, pr)], [puk])
                    puv = pu.rearrange("p (t a n) -> p t a n", t=2, a=2)
                    ACT(lambda e: e.copy(out=uuG[g3][:, pr], in_=puv[:, :, 0, :]), [puk], ["uuG%d_%d" % (g3, pr)])
                    DVE(lambda e: e.tensor_copy(out=wtok[:, pr], in_=puv[:, :, 1, :]), [puk], ["wtok%d" % pr])
                    yield
                    pAs = [bank(), bank()]
                    pC, pCk = bank()
                    for tt in range(2):
                        for cp in range(2):
                            cs_ = slice(cp * 64, cp * 64 + 64)
                            mm(pAs[cp][0][:, tt * 128:(tt + 1) * 128], wtok[cs_, pr, tt, :], kgG[g3][cs_, pr, tt, :], True, True, ["wtok%d" % pr, "kgG%d_%d" % (g3, pr)], [pAs[cp][1]], tp=(cp * 64, 0))
                        mm(pC[:, tt * 128:(tt + 1) * 128], wtok[:, pr, tt, :], qkG[g3][:, pr, tt, :], True, True, ["wtok%d" % pr, "qkG%d_%d" % (g3, pr)], [pCk])
                    for tt in range(2):
                        for cp in range(2):
                            DVE(lambda e: e.scalar_tensor_tensor(out=atG[g3][:, pr, tt * 2 + cp, :], in0=identb, scalar=gl[:, cp, t0 + tt, dh:dh + 1], in1=pAs[cp][0][:, tt * 128:(tt + 1) * 128], op0=ALU.mult, op1=ALU.subtract), [pAs[cp][1], "gl", "cbf"], ["atG%d_%d" % (g3, pr)])
                    DVE(lambda e: e.tensor_tensor(out=ctG[g3][:, pr], in0=qgG[par][:, pr], in1=pC[:, 0:256].rearrange("p (t n) -> p t n", t=2), op=ALU.subtract), [pCk, "qgG%d_%d" % (par, pr)], ["ctG%d_%d" % (g3, pr)])
                    yield

            def Sgen(h, gi):
                g3 = (8 * h + gi) % 3
                hp = h % 2
                if gi == 0:
                    for d_ in range(2):
                        POOL(lambda e: e.memset(Sb[d_], 0.0), [], ["Sb%d" % d_])
                for s_ in range(4):
                    step = 4 * gi + s_
                    info = []
                    for d_ in range(2):
                        n = step if d_ == 0 else 31 - step
                        t, cp = n // 2, n % 2
                        tt = t - pair_t0(gi, d_)
                        info.append(dict(d=d_, t=t, cp=cp, tt=tt, ps=slice(cp * 64, cp * 64 + 64), sbk="Sb%d" % d_, kq="%d_%d" % (g3, d_), pS=bank(), po=bank()))
                    for x_ in info:
                        mm(x_["pS"][0][:, 0:128], atG[g3][:, x_["d"], x_["tt"] * 2 + x_["cp"], :], Sb[x_["d"]], True, False, ["atG" + x_["kq"], x_["sbk"]], [x_["pS"][1]])
                    for x_ in info:
                        mm(x_["pS"][0][:, 0:128], kgG[g3][x_["ps"], x_["d"], x_["tt"], :], uuG[g3][x_["ps"], x_["d"], x_["tt"], :], False, True, ["kgG" + x_["kq"], "uuG" + x_["kq"]], [x_["pS"][1]], tp=(x_["cp"] * 64, 0))
                    for x_ in info:
                        mm(x_["po"][0][x_["ps"], 0:128], ctG[g3][:, x_["d"], x_["tt"], x_["ps"]], Sb[x_["d"]], True, False, ["ctG" + x_["kq"], x_["sbk"]], [x_["po"][1]], tp=(0, x_["cp"] * 64))
                    for x_ in info:
                        mm(x_["po"][0][x_["ps"], 0:128], qkG[g3][x_["ps"], x_["d"], x_["tt"], x_["ps"]], uuG[g3][x_["ps"], x_["d"], x_["tt"], :], False, True, ["qkG" + x_["kq"], "uuG" + x_["kq"]], [x_["po"][1]], tp=(x_["cp"] * 64, x_["cp"] * 64))
                    for x_ in info:
                        DVE(lambda e: e.tensor_copy(out=Sb[x_["d"]], in_=x_["pS"][0][:, 0:128]), [x_["pS"][1]], [x_["sbk"]])
                    for x_ in info:
                        ACT(lambda e: e.copy(out=osum[hp][x_["ps"], x_["d"], x_["t"], :], in_=x_["po"][0][x_["ps"], 0:128]), [x_["po"][1]], ["osum%d_%d_%d" % (hp, x_["d"], x_["t"])])
                    yield

            def drive(gens):
                gens = [g_ for g_ in gens if g_ is not None]
                while gens:
                    for g_ in list(gens):
                        try:
                            next(g_)
                        except StopIteration:
                            gens.remove(g_)

            def gate_gen(h):
                hp = h % 2
                okeys = ["osum%d_%d_%d" % (hp, d_, t) for d_ in range(2) for t in range(NT)]
                POOL(lambda e: e.tensor_tensor(out=osf, in0=osum[hp][:, 0], in1=osum[hp][:, 1], op=ALU.add), okeys, ["osf"])
                yield
                POOL(lambda e: e.tensor_tensor(out=dno, in0=osf, in1=osf, op=ALU.mult), ["osf"], ["dno"])
                yield
                orn = small[:, 0:NT]
                DVE(lambda e: e.tensor_reduce(out=orn, in_=dno, axis=AX.X, op=ALU.add), ["dno"], ["orn"])
                ACT(lambda e: e.activation(out=orn, in_=orn, func=AF.Sqrt, scale=1.0 / 128, bias=EPS), ["orn"], ["orn"])
                DVE(lambda e: e.reciprocal(out=orn, in_=orn), ["orn"], ["orn"])
                yield
                DVE(lambda e: e.tensor_tensor(out=osf, in0=osf, in1=bc(orn.unsqueeze(2), [128, NT, 128]), op=ALU.mult), ["orn", "osf"], ["osf"])
                yield
                POOL(lambda e: e.tensor_tensor(out=osf, in0=osf, in1=bc(gdnb.unsqueeze(1), [128, NT, 128]), op=ALU.mult), ["osf", "gdnb"], ["osf"])
                yield
                DVE(lambda e: e.tensor_tensor(out=dno, in0=osf, in1=gate_s[:, :, h * 128:(h + 1) * 128], op=ALU.mult), ["osf", "gate_s", "dno"], ["dno"])
                yield
                for half in range(2):
                    pa, pak = bank()
                    pv = pa.bitcast(BF16).rearrange("p (c n) -> p c n", c=8)
                    for c in range(8):
                        t = half * 8 + c
                        tr(pv[:, c, :], dno[:, t, :], identb, ["dno", "cbf"], [pak])
                    ACT(lambda e: e.copy(out=dn_oT[:, h, half * 1024:(half + 1) * 1024], in_=pv.rearrange("p c n -> p (c n)")), [pak], ["dn_oT"])
                    yield

            wo = A.view("kqT", [128, 8, D], BF16)
            gates = []
            for k_ in range(32 + 2):
                gens = []
                if k_ < 32:
                    gens.append(Egen(*divmod(k_, 8)))
                if k_ == 32:
                    for g in range(2):
                        load_w(wo[g * 64:(g + 1) * 64, 0:4, :], w_out[g * 256:(g + 1) * 256, :].rearrange("(j d) n -> d j n", d=64), "kqT")
                    load_w(wo[:, 4:8, :], w_out[512:1024, :].rearrange("(h p) n -> p h n", p=128), "kqT")
                if 0 <= k_ - 1 < 32:
                    gens.append(Ngen(*divmod(k_ - 1, 8)))
                if 0 <= k_ - 2 < 32:
                    gens.append(Sgen(*divmod(k_ - 2, 8)))
                gens += gates
                drive(gens)
                gates = []
                if k_ - 2 >= 0 and (k_ - 2) % 8 == 7:
                    gates = [gate_gen((k_ - 2) // 8)]
            drive(gates)
            S.barrier()
        A.release("ktok", "vtok", "uuG0", "uuG1", "uuG2", "kgG0", "kgG1", "kgG2", "qkG0", "qkG1", "qkG2", "ctG0", "ctG1", "ctG2", "atG0", "atG1", "atG2", "qgG0", "qgG1", "wtok", "osf", "dno", "osum0", "osum1", "dgh", "dgl", "Dm", "Em", "egr", "PR0", "PR1", "PT_0", "PT_1",
                  "P0b0", "P0b1", "vkb0", "vkb1", "mb2", "Sb0", "Sb1", "vn0", "vn1", "gate_s", "gc", "gl", "kgs", "bgs", "beta", "gcb", "GGh", "GGl",
                  "alb" if "alb" in A.live else "gdnb", "cwt")
        if "gdnb" in A.live:
            A.release("gdnb")

        dump("dn_oT", dn_oT[:, 0, :], [])
        x1 = A.alloc("x1", [128, NT, D], F32)
        DMA("sp", gbc, gains["g_mix_post"].partition_broadcast(128), [], ["gbc"], "gbc")
        NE = 4
        etmps = [A.alloc("etmp" if i == 0 else "etmp%d" % i, [128, D], F32, top=True) for i in range(NE)]
        etk = ["etmp%d" % i for i in range(NE)]
        xbuf = [A.alloc("xb%d" % i, [128, D], F32) for i in range(NE)]

        def oproj_gen(t):
            ts_ = slice(t * 128, (t + 1) * 128)
            DMA("sp", xbuf[t % NE], x_d[ts_, :], [], ["xb%d" % (t % NE)], "xb%d" % (t % NE))
            p2, p2k = bank2()
            for c in range(8):
                for hf in range(2):
                    lhs = attn_oT[:, c, ts_] if c < 4 else dn_oT[:, c - 4, ts_]
                    mm(p2[:, hf * 512:(hf + 1) * 512], lhs, wo[:, c, hf * 512:(hf + 1) * 512], c == 0, c == 7, ["attn_oT", "dn_oT"], [p2k[hf]])
            yield
            yield from resid_epilogue(p2, p2k, None, xbuf[t % NE], ["xb%d" % (t % NE)], x1[:, t, :], ["x1_%d" % t], etmps[t % NE], etk[t % NE])

        pipeline([oproj_gen(t) for t in range(NT)], NE)
        S.barrier()
        A.release("attn_oT", "dn_oT", "kqT")
        dump("x1", x1[:, 0, :], [])

        h2T = A.alloc("h2T", [128, 8, S_LEN], BF16)
        wsl = [A.alloc("wsm%d" % i, [128, 8, 512], BF16) for i in range(2)]
        pre_ckv = []
        for q4 in range(2):
            w_, wk_ = wslot()
            load_w(w_, kview(w_ckv, q4 * 512, (q4 + 1) * 512), wk_)
            pre_ckv.append((w_, wk_))
        memT = A.alloc("memT", [128, 8, 256], BF16)

        def mem_tiles():
            out = []
            for t in range(2):
                def loader(t=t):
                    DMA("sp", xbuf[t % 2], mem_d[t * 128:(t + 1) * 128, :], [], ["xb%d" % (t % 2)], "xb%d" % (t % 2))
                out.append((xbuf[t % 2], ["xb%d" % (t % 2)], loader))
            return out
        norm_T(mem_tiles(), gains["g_mem"], memT, "memT", "c")
        kcT = A.alloc("kcT", [128, 8, 256], BF16)
        vc = A.alloc("vc", [128, 2, D], BF16)
        for q4 in range(2):
            w_, wk_ = pre_ckv[q4]
            for nn in range(4):
                n = q4 * 4 + nn
                pa, pak = bank()
                for c in range(8):
                    mm(pa[:, 0:256], w_[:, c, nn * 128:(nn + 1) * 128], memT[:, c, :], c == 0, c == 7, [wk_, "memT0", "memT1"], [pak])
                ACT(lambda e, n=n, pa=pa: e.copy(out=kcT[:, n, :], in_=pa[:, 0:256]), [pak], ["kcT"])
        for q4 in range(2):
            w_, wk_ = wslot()
            load_w(w_, kview(w_ckv, D + q4 * 512, D + (q4 + 1) * 512), wk_)
            for mt in range(2):
                pa, pak = bank()
                for c in range(8):
                    mm(pa, memT[:, c, mt * 128:(mt + 1) * 128], w_[:, c, :], c == 0, c == 7, [wk_, "memT%d" % mt], [pak])
                DVE(lambda e, mt=mt, q4=q4, pa=pa: e.tensor_copy(out=vc[:, mt, q4 * 512:(q4 + 1) * 512], in_=pa), [pak], ["vc"])
        norm_T([(x1[:, t, :], ["x1_%d" % t], None) for t in range(NT)], gains["g_cross_pre"], h2T, "h2T", "b")
        qcT = A.alloc("qcT", [128, 8, S_LEN], BF16)
        for q4 in range(2):
            w_, wk_ = wslot()
            load_w(w_, kview(w_cq, q4 * 512, (q4 + 1) * 512), wk_)
            for nn in range(4):
                n = q4 * 4 + nn
                bks = [bank() for _ in range(4)]
                for c in range(8):
                    for n4 in range(4):
                        mm(bks[n4][0], w_[:, c, nn * 128:(nn + 1) * 128], h2T[:, c, n4 * 512:(n4 + 1) * 512], c == 0, c == 7, [wk_] + hkeys("h2T", n4), [bks[n4][1]])
                for n4 in range(4):
                    cs = slice(n4 * 512, (n4 + 1) * 512)
                    if (n + n4) % 2 == 0:
                        ACT(lambda e: e.copy(out=qcT[:, n, cs], in_=bks[n4][0]), [bks[n4][1]], ["qcT%d" % n])
                    else:
                        DVE(lambda e: e.tensor_copy(out=qcT[:, n, cs], in_=bks[n4][0]), [bks[n4][1]], ["qcT%d" % n])
        S.barrier()
        A.release("h2T", "memT", "wsm0", "wsm1")
        ocT = A.alloc("ocT", [128, 8, S_LEN], BF16)
        wco = A.alloc("wco", [128, 8, D], BF16)
        load_w(wco[:, :, 0:512], kview(w_co, 0, 512), "wco")
        load_w(wco[:, :, 512:1024], kview(w_co, 512, 1024), "wco")
        Pc = [A.alloc("Pc%d" % i, [128, 512], BF16) for i in range(4)]
        rec = [A.alloc("rcc%d" % i, [128, 512], F32) for i in range(2)]
        pcc = [0]
        def cross_gen(hh, n4):
            cs = slice(n4 * 512, (n4 + 1) * 512)
            Ps = []
            for mt in range(2):
                pS, pSk = bank()
                for dc in range(2):
                    mm(pS, kcT[:, 2 * hh + dc, mt * 128:(mt + 1) * 128], qcT[:, 2 * hh + dc, cs], dc == 0, dc == 1, ["kcT", "qcT%d" % (2 * hh + dc)], [pSk])
                pi = pcc[0] % 4
                pcc[0] += 1
                ACT(lambda e: e.activation(out=Pc[pi], in_=pS, func=AF.Exp, scale=1.0 / 16), [pSk], ["Pc%d" % pi])
                Ps.append((Pc[pi], "Pc%d" % pi))
            yield
            pD, pDk = bank()
            for mt in range(2):
                mm(pD, onesb, Ps[mt][0], mt == 0, mt == 1, ["cbf", Ps[mt][1]], [pDk])
            pOs = []
            for dc in range(2):
                pO, pOk = bank()
                for mt in range(2):
                    mm(pO, vc[:, mt, (2 * hh + dc) * 128:(2 * hh + dc + 1) * 128], Ps[mt][0], mt == 0, mt == 1, ["vc", Ps[mt][1]], [pOk])
                pOs.append((pO, pOk))
            yield
            ri = (hh * 4 + n4) % 2
            ACT(lambda e: e.activation(out=rec[ri], in_=pD, func=AF.Ln), [pDk], ["rcc%d" % ri])
            ACT(lambda e: e.activation(out=rec[ri], in_=rec[ri], func=AF.Exp, scale=-1.0), ["rcc%d" % ri], ["rcc%d" % ri])
            yield
            for dc in range(2):
                pO, pOk = pOs[dc]
                DVE(lambda e: e.tensor_tensor(out=ocT[:, 2 * hh + dc, cs], in0=pO, in1=rec[ri], op=ALU.mult), [pOk, "rcc%d" % ri], ["ocT"])
            yield

        pipeline([cross_gen(hh, n4) for hh in range(4) for n4 in range(4)], 2)
        S.barrier()
        A.release("kcT", "vc", "qcT", "Pc0", "Pc1", "Pc2", "Pc3", "rcc0", "rcc1")
        wd = A.alloc("wd", [128, NFF, D], BF16, top=True)
        wdv = w_dn.rearrange("(c p) n -> p c n", p=128)
        for i in range(0, NFF, 6):
            load_w(wd[:, i:min(i + 6, NFF), :], wdv[:, i:min(i + 6, NFF), :], "wd")
        DMA("sp", gbc, gains["g_cross_post"].partition_broadcast(128), [], ["gbc"], "gbc")
        def coproj_gen(t):
            ts_ = slice(t * 128, (t + 1) * 128)
            p2, p2k = bank2()
            for c in range(8):
                for hf in range(2):
                    mm(p2[:, hf * 512:(hf + 1) * 512], ocT[:, c, ts_], wco[:, c, hf * 512:(hf + 1) * 512], c == 0, c == 7, ["ocT", "wco"], [p2k[hf]])
            yield
            yield from resid_epilogue(p2, p2k, None, x1[:, t, :], ["x1_%d" % t], x1[:, t, :], ["x1_%d" % t], etmps[t % NE], etk[t % NE])

        pipeline([coproj_gen(t) for t in range(NT)], NE)
        S.barrier()
        A.release("ocT", "wco")
        dump("x2", x1[:, 0, :], [])

        A.release("xb0", "xb1", "xb2", "xb3")
        A.release("etmp1", "etmp2", "etmp3")
        etmps = [etmps[0], etmps[0]]
        etk = ["etmp0", "etmp0"]
        h3T = A.alloc("h3T", [128, 8, S_LEN], BF16)
        wg = [A.alloc("wg%d" % i, [128, 8, 2, 128], BF16, top=True) for i in range(2)]
        for i in range(2):
            load_w(wg[i].rearrange("p c a n -> p (c a n)"), w_gu[i], "wg%d" % i)
        norm_T([(x1[:, t, :], ["x1_%d" % t], None) for t in range(NT)], gains["g_ffn_pre"], h3T, "h3T", "d")
        DMA("sp", gbc, gains["g_ffn_post"].partition_broadcast(128), [], ["gbc"], "gbc")
        aT = A.alloc("aT", [128, NFF, 1024], BF16)
        sg = [A.alloc("sg%d" % i, [128, 512], BF16) for i in range(3)]
        wgc = [0]
        sgc = [0]
        for tg in range(2):
            def gu_gen(i, wi):
                banks = {}
                for nn in range(2):
                    banks[("g", nn)] = bank()
                    banks[("u", nn)] = bank()
                for c in range(8):
                    for gu, a_ in (("g", 0), ("u", 1)):
                        for nn in range(2):
                            n4 = tg * 2 + nn
                            cs = slice(n4 * 512, (n4 + 1) * 512)
                            pb_, pbk = banks[(gu, nn)]
                            mm(pb_, wg[wi][:, c, a_, :], h3T[:, c, cs], c == 0, c == 7, ["wg%d" % wi] + hkeys("h3T", n4), [pbk])
                yield
                for nn in range(2):
                    si = (2 * i + nn) % 3
                    ACT(lambda e: e.activation(out=sg[si], in_=banks[("g", nn)][0], func=AF.Silu), [banks[("g", nn)][1]], ["sg%d" % si])
                yield
                for nn in range(2):
                    si = (2 * i + nn) % 3
                    DVE(lambda e: e.tensor_tensor(out=aT[:, i, nn * 512:(nn + 1) * 512], in0=banks[("u", nn)][0], in1=sg[si], op=ALU.mult), [banks[("u", nn)][1], "sg%d" % si], ["aT%d" % nn])
                yield

            gl_ = []
            for i in range(NFF):
                wi = i % 2
                def ldgen(i=i, wi=wi):
                    load_w(wg[wi].rearrange("p c a n -> p (c a n)"), w_gu[i], "wg%d" % wi)
                    yield
                if i >= 2:
                    gl_.append(ldgen())
                gl_.append(gu_gen(i, wi))
            pipeline(gl_, 2)
            if tg == 0:
                for i in range(2):
                    load_w(wg[i].rearrange("p c a n -> p (c a n)"), w_gu[i], "wg%d" % i)
            def down_gen(tt):
                t = tg * 8 + tt
                p2, p2k = bank2()
                for i in range(NFF):
                    for hf in range(2):
                        mm(p2[:, hf * 512:(hf + 1) * 512], aT[:, i, tt * 128:(tt + 1) * 128], wd[:, i, hf * 512:(hf + 1) * 512], i == 0, i == NFF - 1, ["aT%d" % (tt // 4), "wd"], [p2k[hf]])
                yield
                yield from resid_epilogue(p2, p2k, None, x1[:, t, :], ["x1_%d" % t], x1[:, t, :], ["x1_%d" % t], etmps[t % 2], etk[t % 2])
                DMA("sp", out_d[t * 128:(t + 1) * 128, :], x1[:, t, :], ["x1_%d" % t], [], "out")
                yield

            pipeline([down_gen(tt) for tt in range(8)], 1)
        stats = S.finalize(final_waits=["out"] + ["dbg_" + k for k in dbg])
    return nc, stats


def _consts():
    bf = ml_dtypes.bfloat16
    j = np.arange(128)[:, None]
    i = np.arange(128)[None, :]
    cst_bf = np.zeros((128, 4, 128), np.float32)
    cst_bf[:, 0] = np.eye(128)
    cst_bf[:, 1] = 1.0
    cst_bf[:, 2] = (i <= j)
    cst_bf[:, 3] = (j <= i)
    same = (j // 64) == (i // 64)
    NEG = -30000.0
    cst_f = np.zeros((128, 10, 128), np.float32)
    cst_f[:, 0] = np.eye(128)
    cst_f[:, 1] = np.where(same & (i >= j), 0.0, NEG)
    cst_f[:, 2] = np.where(same & (i > j), 0.0, NEG)
    cst_f[:, 3] = np.where(same & (i <= j), 0.0, NEG)
    cst_f[:, 4] = np.where(same & (i < j), 0.0, NEG)
    cst_f[:, 5] = same & (j <= i)
    cst_f[:, 6] = same & (j >= i)
    cst_f[:, 7] = same
    cst_f[:, 8] = (j < 64) & (i >= 0)
    cst_f[:, 9] = (j >= 64) & (i >= 0)
    d = np.arange(128) % 64
    inv = (10000.0 ** (-(d % 32).astype(np.float32) / np.float32(32))).astype(np.float32)
    sign = np.where(d < 32, -1.0, 1.0).astype(np.float32)
    cst_c = np.stack([inv, sign], 1).astype(np.float32)
    return cst_bf.astype(bf), cst_f, cst_c


_CACHE = {}


def _prep_weights(w_in, conv_w):
    w = np.asarray(w_in)[0]
    cols = []
    q0, k0, v0, dq0, dg0, da0, db0 = 0, 512, 640, 768, 2304, 2816, 2824

    def head(base, hd):
        return list(range(base + hd * 64, base + hd * 64 + 64))

    def swp(c):
        return c[32:] + c[:32]
    for j in range(4):
        cols += head(q0, j) + head(q0, 4 + j)
    for j in range(4):
        cols += swp(head(q0, j)) + swp(head(q0, 4 + j))
    cols += head(k0, 0) + head(k0, 1)
    cols += swp(head(k0, 0)) + swp(head(k0, 1))
    cols += list(range(dq0, dq0 + 1536))
    cols += list(range(dg0, dg0 + 512))
    cols += list(range(v0, v0 + 128))
    cols += list(range(da0, da0 + 8)) + list(range(db0, db0 + 8))
    assert len(cols) == 3472
    w_inr = np.ascontiguousarray(w[:, np.array(cols)])
    cw = np.ascontiguousarray(np.asarray(conv_w)[0].T.reshape(12, 128, 5).transpose(1, 0, 2))
    return w_inr, cw


def kernel(x, mem, positions, g_mix_pre, w_in, conv_w, a_log, dt_bias, g_dn_out, attn_sink,
           w_out, g_mix_post, g_cross_pre, g_mem, w_cq, w_ckv, w_co, g_cross_post,
           g_ffn_pre, w_gate_up, w_down, g_ffn_post):
    f = lambda a: np.ascontiguousarray(np.asarray(a, dtype=np.float32))
    if "nc" not in _CACHE:
        _CACHE["nc"] = build_program()
    nc, stats = _CACHE["nc"]
    cst_bf, cst_f, cst_c = _consts()
    w_inr, cw = _prep_weights(w_in, conv_w)
    shared = dict(
        w_inr=w_inr, cw=cw, alog=f(a_log).reshape(1, 8), dtb=f(dt_bias).reshape(1, 8), gdn=f(g_dn_out).reshape(1, 128),
        sink=f(attn_sink).reshape(1, 8), w_out=f(w_out)[0], w_cq=f(w_cq)[0], w_ckv=f(w_ckv)[0], w_co=f(w_co)[0],
        w_gu=np.ascontiguousarray(f(w_gate_up)[0].reshape(8, 128, 2, NFF, 128).transpose(3, 1, 0, 2, 4)).reshape(NFF, 128, 2048), w_dn=f(w_down)[0],
        g_mix_pre=f(g_mix_pre).reshape(1, D), g_mix_post=f(g_mix_post).reshape(1, D), g_cross_pre=f(g_cross_pre).reshape(1, D),
        g_mem=f(g_mem).reshape(1, D), g_cross_post=f(g_cross_post).reshape(1, D), g_ffn_pre=f(g_ffn_pre).reshape(1, D),
        g_ffn_post=f(g_ffn_post).reshape(1, D), cst_bf=cst_bf, cst_f=cst_f, cst_c=cst_c,
    )
    xs = f(x)
    ms = f(mem)
    ps = np.ascontiguousarray(np.asarray(positions).astype(np.int32))
    in_maps = []
    for b in range(8):
        m = dict(shared)
        m["x"] = xs[b]
        m["mem"] = ms[b]
        m["pos"] = ps[b:b + 1]
        in_maps.append(m)
    res = run_bass_kernel_spmd(nc, in_maps, core_ids=list(range(8)))
    _CACHE["res"] = res
    return np.stack([np.asarray(r["out"]) for r in res.results], 0).astype(np.float32)
```
